# Optimizing a Trainium2 kernel written in Bass

```python
import jax, jax.numpy as jnp
from jax import lax
import numpy as np

D_MODEL = 2048
BATCH = 4
SEQ = 2048
DEPTH = 1
DEC_BATCH = 128
DEC_SEQ = 8
PAST_LEN = 16384
PAGE_SIZE = 128

D_MIX = D_MODEL
W_A = D_MIX // 2
W_B = D_MIX - W_A
N_HEADS_A = 8
HB_A = W_A // N_HEADS_A
N_GROUPS_B = 16
CONV_A = 4
CONV_B = 31
LRU_C = 8.0
D_IN = 2 * W_A + 3 * W_B
EPS = 1e-6

kernel_name = "hymba_style_rglru_conformer_conv_step"


def _rmsnorm(x, gain):
    x32 = x.astype(jnp.float32)
    return x32 * lax.rsqrt(jnp.mean(x32 * x32, axis=-1, keepdims=True) + EPS) * gain.astype(jnp.float32)


def _layernorm(x, gain, bias):
    mu = jnp.mean(x, axis=-1, keepdims=True)
    xc = x - mu
    var = jnp.mean(xc * xc, axis=-1, keepdims=True)
    return xc * lax.rsqrt(var + EPS) * gain.astype(jnp.float32) + bias.astype(jnp.float32)


def _dwconv_valid(x_pad, w, b):
    c = x_pad.shape[-1]
    out = lax.conv_general_dilated(
        x_pad, w.astype(jnp.float32)[:, None, :], window_strides=(1,), padding='VALID',
        dimension_numbers=('NWC', 'WIO', 'NWC'), feature_group_count=c)
    return out + b.astype(jnp.float32)


def _lru_combine(left, right):
    a1, b1 = left
    a2, b2 = right
    return a1 * a2, a2 * b1 + b2


def _layer(x, h0, buf_a, buf_b, norm_gain, w_in, conv_a_w, conv_a_b, gate_a_w, gate_a_b,
           gate_x_w, gate_x_b, lru_param, conv_b_w, conv_b_b, ln_b_gain, ln_b_bias, w_out):
    bsz, t = x.shape[0], x.shape[1]
    x32 = x.astype(jnp.float32)
    xn = _rmsnorm(x32, norm_gain)
    proj = jnp.einsum('btd,de->bte', xn, w_in.astype(jnp.float32))
    xa, ga, vb, gglu, gb = jnp.split(
        proj, [W_A, 2 * W_A, 2 * W_A + W_B, 2 * W_A + 2 * W_B], axis=-1)

    xa_pad = jnp.concatenate([buf_a.astype(jnp.float32), xa], axis=1)
    new_buf_a = xa_pad[:, -(CONV_A - 1):]
    xc = _dwconv_valid(xa_pad, conv_a_w, conv_a_b)
    xh = xc.reshape(bsz, t, N_HEADS_A, HB_A)
    r = jax.nn.sigmoid(jnp.einsum('bthi,hij->bthj', xh, gate_a_w.astype(jnp.float32))
                       .reshape(bsz, t, W_A) + gate_a_b.astype(jnp.float32))
    ig = jax.nn.sigmoid(jnp.einsum('bthi,hij->bthj', xh, gate_x_w.astype(jnp.float32))
                        .reshape(bsz, t, W_A) + gate_x_b.astype(jnp.float32))
    log_a = -LRU_C * r * jax.nn.softplus(-lru_param.astype(jnp.float32))
    a = jnp.exp(log_a)
    mult = jnp.sqrt(-jnp.expm1(2.0 * log_a))
    bterm = mult * (ig * xc)
    bterm = bterm.at[:, 0].add(a[:, 0] * h0.astype(jnp.float32))
    _, h = lax.associative_scan(_lru_combine, (a, bterm), axis=1)
    ya = h * jax.nn.silu(ga)

    u = vb * jax.nn.sigmoid(gglu)
    u_pad = jnp.concatenate([buf_b.astype(jnp.float32), u], axis=1)
    new_buf_b = u_pad[:, -(CONV_B - 1):]
    cb = _dwconv_valid(u_pad, conv_b_w, conv_b_b)
    cb = _layernorm(cb, ln_b_gain, ln_b_bias)
    yb = jax.nn.silu(cb) * jax.nn.silu(gb)

    y = jnp.einsum('bte,ed->btd', jnp.concatenate([ya, yb], axis=-1), w_out.astype(jnp.float32))
    return x32 + y, h[:, -1], new_buf_a, new_buf_b


def setup_inputs(seed: int = 0) -> dict:
    key = jax.random.key(seed)
    ks = jax.random.split(key, 24)
    f = jnp.float32
    nrm = lambda k, s, sc: jax.random.normal(k, s, f) * sc
    a_base = jax.random.uniform(ks[12], (DEPTH, W_A), f, 0.9, 0.999)
    return {
        "x_prompt": nrm(ks[0], (BATCH, SEQ, D_MODEL), 1.0),
        "x_sample": nrm(ks[1], (DEC_BATCH, DEC_SEQ, D_MODEL), 1.0),
        "state_lru_h": nrm(ks[2], (DEPTH, DEC_BATCH, W_A), 0.5),
        "state_lru_conv": nrm(ks[3], (DEPTH, DEC_BATCH, CONV_A - 1, W_A), 0.5),
        "state_glu_conv": nrm(ks[4], (DEPTH, DEC_BATCH, CONV_B - 1, W_B), 0.5),
        "norm_gain": 1.0 + nrm(ks[5], (DEPTH, D_MODEL), 0.02),
        "w_in": nrm(ks[6], (DEPTH, D_MODEL, D_IN), D_MODEL ** -0.5),
        "conv_a_w": nrm(ks[7], (DEPTH, CONV_A, W_A), CONV_A ** -0.5),
        "conv_a_b": nrm(ks[8], (DEPTH, W_A), 0.01),
        "gate_a_w": nrm(ks[9], (DEPTH, N_HEADS_A, HB_A, HB_A), HB_A ** -0.5),
        "gate_a_b": nrm(ks[10], (DEPTH, W_A), 0.01),
        "gate_x_w": nrm(ks[11], (DEPTH, N_HEADS_A, HB_A, HB_A), HB_A ** -0.5),
        "gate_x_b": nrm(ks[13], (DEPTH, W_A), 0.01),
        "lru_param": jnp.log(a_base) - jnp.log1p(-a_base),
        "conv_b_w": nrm(ks[14], (DEPTH, CONV_B, W_B), CONV_B ** -0.5),
        "conv_b_b": nrm(ks[15], (DEPTH, W_B), 0.01),
        "ln_b_gain": 1.0 + nrm(ks[16], (DEPTH, W_B), 0.02),
        "ln_b_bias": nrm(ks[17], (DEPTH, W_B), 0.01),
        "w_out": nrm(ks[18], (DEPTH, D_MIX, D_MODEL), D_MIX ** -0.5),
        "final_gain": 1.0 + nrm(ks[19], (D_MODEL,), 0.02),
    }


def reference(x_prompt, x_sample, state_lru_h, state_lru_conv, state_glu_conv,
              norm_gain, w_in, conv_a_w, conv_a_b, gate_a_w, gate_a_b, gate_x_w, gate_x_b,
              lru_param, conv_b_w, conv_b_b, ln_b_gain, ln_b_bias, w_out, final_gain):
    bp = x_prompt.shape[0]
    sdt = state_lru_h.dtype
    hp = x_prompt.astype(jnp.float32)
    hs = x_sample.astype(jnp.float32)
    ph, pca, pcb, sh, sca, scb = [], [], [], [], [], []
    for l in range(DEPTH):
        w = (norm_gain[l], w_in[l], conv_a_w[l], conv_a_b[l], gate_a_w[l], gate_a_b[l],
             gate_x_w[l], gate_x_b[l], lru_param[l], conv_b_w[l], conv_b_b[l],
             ln_b_gain[l], ln_b_bias[l], w_out[l])
        hp, h_p, ca_p, cb_p = _layer(
            hp, jnp.zeros((bp, W_A), jnp.float32), jnp.zeros((bp, CONV_A - 1, W_A), jnp.float32),
            jnp.zeros((bp, CONV_B - 1, W_B), jnp.float32), *w)
        hs, h_s, ca_s, cb_s = _layer(hs, state_lru_h[l], state_lru_conv[l], state_glu_conv[l], *w)
        ph.append(h_p); pca.append(ca_p); pcb.append(cb_p)
        sh.append(h_s); sca.append(ca_s); scb.append(cb_s)
    y_prompt = _rmsnorm(hp, final_gain).astype(x_prompt.dtype)
    y_sample = _rmsnorm(hs, final_gain).astype(x_sample.dtype)
    new_lru_h_prompt = jnp.stack(ph).astype(sdt)
    new_lru_conv_prompt = jnp.stack(pca).astype(sdt)
    new_glu_conv_prompt = jnp.stack(pcb).astype(sdt)
    new_lru_h_sample = jnp.stack(sh).astype(sdt)
    new_lru_conv_sample = jnp.stack(sca).astype(sdt)
    new_glu_conv_sample = jnp.stack(scb).astype(sdt)
    return (y_prompt, y_sample, new_lru_h_prompt, new_lru_conv_prompt, new_glu_conv_prompt,
            new_lru_h_sample, new_lru_conv_sample, new_glu_conv_sample)
```

```python
import contextlib
import numpy as np
import concourse.bass as bass
import concourse.mybir as mybir
from concourse.bass_utils import run_bass_kernel_spmd

F32 = mybir.dt.float32
BF16 = mybir.dt.bfloat16
AF = mybir.ActivationFunctionType
ALU = mybir.AluOpType

NCORES = 8
D = 2048
WA = 1024
NH = 8
CONV_A = 4
CONV_B = 31
HALO = 32
NP = 1024
NS = 128
NSEQ = 16
TS = 8
NT = HALO + NP + NS
NTOK = NP + NS
EPS = 1e-6
NWS = 4
NLANES = 30
NL_SP = 20

HCH = [(0, 512), (512, 1024), (1024, NT)]
NCHK = [(HALO, HALO + 512), (HALO + 512, HALO + 1024), (HALO + 1024, NT)]


class Sched:
    def __init__(self):
        self.ops = []
        self.res = {}
        self.lane_last = [None] * NLANES
        self.lane_count = [0] * NLANES
        self.next_lane = {"sp": 0, "pool": NL_SP}
        self.cur_fence = []
        self.phase_keys = set()
        self.nofence_default = False

    def _deps(self, reads, writes, nofence=False):
        deps = set()
        for k in list(reads) + list(writes):
            if not nofence:
                self.phase_keys.add(k)
            if k not in self.res and not nofence:
                deps.update(self.cur_fence)
        for k in reads:
            st = self.res.get(k)
            if st is not None and st[0] is not None:
                deps.add(st[0])
        for k in writes:
            st = self.res.get(k)
            if st is not None:
                if st[0] is not None:
                    deps.add(st[0])
                deps.update(st[1])
        return deps

    def op(self, eng, fn, reads=(), writes=(), ndma=0, cc=False, after=(), nofence=None):
        reads = list(reads)
        writes = list(writes)
        if nofence is None:
            nofence = self.nofence_default
        deps = self._deps(reads, writes, nofence)
        deps.update(after)
        idx = len(self.ops)
        o = dict(eng=eng, fn=fn, deps=deps, ndma=ndma, cc=cc, lane=None, sig=False, val=None)
        if ndma:
            lane = self.next_lane[eng]
            lo, hi = (0, NL_SP) if eng == "sp" else (NL_SP, NLANES)
            self.next_lane[eng] = lo + (lane + 1 - lo) % (hi - lo)
            if self.lane_last[lane] is not None:
                deps.add(self.lane_last[lane])
            self.lane_last[lane] = idx
            self.lane_count[lane] += ndma
            o["lane"] = lane
            o["val"] = 16 * self.lane_count[lane]
        self.ops.append(o)
        for k in reads:
            st = self.res.setdefault(k, [None, []])
            st[1].append(idx)
        for k in writes:
            self.res[k] = [idx, []]
        return idx

    def fence(self):
        deps = set(self.cur_fence)
        for k in self.phase_keys:
            st = self.res.get(k)
            if st is not None:
                if st[0] is not None:
                    deps.add(st[0])
                deps.update(st[1])
        self.cur_fence = sorted(deps)
        self.phase_keys = set()

    def late_keys(self, keys):
        deps = set(self.cur_fence)
        for k in self.phase_keys:
            st = self.res.get(k)
            if st is not None:
                if st[0] is not None:
                    deps.add(st[0])
                deps.update(st[1])
        deps = sorted(deps)
        for k in keys:
            assert k not in self.res, k
            self.res[k] = [None, list(deps)]

    def finalize(self):
        for o in self.ops:
            for d in o["deps"]:
                self.ops[d]["sig"] = True
        cnt = {}
        for o in self.ops:
            if o["ndma"] or o["cc"]:
                continue
            if o["sig"]:
                cnt[o["eng"]] = cnt.get(o["eng"], 0) + 1
                o["val"] = cnt[o["eng"]]

    def emit(self, eng_name, eng, sems, lane_sems, cc_sem, final_wait=False):
        seen = {}
        for o in self.ops:
            if o["eng"] != eng_name:
                continue
            need = {}
            for d in o["deps"]:
                p = self.ops[d]
                if p["cc"]:
                    key, sem, val = "cc", cc_sem, 1
                elif p["ndma"]:
                    key, sem, val = ("l", p["lane"]), lane_sems[p["lane"]], p["val"]
                else:
                    if p["eng"] == "pe" and eng_name == "pe":
                        continue
                    key, sem, val = p["eng"], sems[p["eng"]], p["val"]
                if val > need.get(key, (None, 0))[1]:
                    need[key] = (sem, val)
            for key, (sem, val) in need.items():
                if val > seen.get(key, 0):
                    eng.wait_ge(sem, val)
                    seen[key] = val
            r = o["fn"](eng)
            if o["cc"]:
                r.then_inc(cc_sem, 1)
            elif o["ndma"]:
                rl = r if isinstance(r, (list, tuple)) else [r]
                assert len(rl) == o["ndma"], (len(rl), o["ndma"])
                for ins in rl:
                    ins.then_inc(lane_sems[o["lane"]], 16)
            elif o["sig"]:
                r.then_inc(sems[o["eng"]], 1)
        if final_wait:
            for lane in range(NLANES):
                if self.lane_count[lane]:
                    eng.wait_ge(lane_sems[lane], 16 * self.lane_count[lane])


def build_nc():
    nc = bass.Bass("TRN2", target_bir_lowering=False)
    S = Sched()

    def din(name, shape):
        return nc.dram_tensor(name, list(shape), F32, kind="ExternalInput").ap()

    def dout(name, shape):
        return nc.dram_tensor(name, list(shape), F32, kind="ExternalOutput").ap()

    x_d = din("x", [NT, D])
    sth_d = din("sth", [NSEQ, WA])
    stca_d = din("stca", [NSEQ * 3, WA])
    stcb_d = din("stcb", [NSEQ * 30, WA])
    win_d = din("win", [40, 128, D])
    wout_d = din("wout", [4, 128, 16 * 512])
    gw_d = din("gw", [128, 16 * 128])
    pa_d = din("pa", [128, 64])
    pb_d = din("pb", [128, 8 * 35])
    ng_d = din("ng", [D])
    fg_d = din("fg", [D])
    ident_d = din("ident", [128, 128])
    mask_d = din("mask", [128, 1])

    y_d = dout("y", [NTOK, D])
    ohp_d = dout("ohp", [8, 128])
    ocap_d = dout("ocap", [3, WA])
    ocbp_d = dout("ocbp", [30, WA])
    ohs_d = dout("ohs", [NSEQ, WA])
    ocas_d = dout("ocas", [NSEQ * 3, WA])
    ocbh_d = dout("ocbh", [NSEQ, 22, WA])
    ocbn_d = dout("ocbn", [NS, WA])

    a_sc = nc.dram_tensor("a_scr", [8, 128, NP], F32)
    b_sc = nc.dram_tensor("b_scr", [8, 128, NP], F32)
    wo_bf = nc.dram_tensor("wo_bf", [4, 128, 16 * 512], BF16)
    cc_in = nc.dram_tensor("cc_in", [128, 8], F32)
    cc_out = nc.dram_tensor("cc_out", [256, 8], F32)

    es = contextlib.ExitStack()
    with es:
        def sb(name, shape, dt):
            return es.enter_context(nc.sbuf_tensor(name, list(shape), dt))

        PA = sb("PA", [128, 8, 8], F32)
        PB = sb("PB", [128, 8, 35], F32)
        GW = sb("GW", [128, 16, 128], BF16)
        IDF = sb("IDF", [128, 128], F32)
        IDB = sb("IDB", [128, 128], BF16)
        ONES = sb("ONES", [128, 128], BF16)
        MASK = sb("MASK", [128, 1], F32)
        EPSC = sb("EPSC", [128, 1], F32)
        ONE1 = sb("ONE1", [1, 128], F32)
        CL = sb("CL", [128, 8], F32)
        CLT = sb("CLT", [128, 8], F32)
        SS = sb("SS", [128, 16], F32)
        LNT = sb("LNT", [128, 16], F32)
        RS = sb("RS", [128, 16], F32)
        H0S = sb("H0S", [128, 8, NSEQ], F32)
        HSL = sb("HSL", [128, 8, NSEQ], F32)
        X3 = sb("X3", [128, 8, NSEQ * 3], F32)
        HFIN = sb("HFIN", [128, 8], F32)
        HINR = sb("HINR", [128, 8], F32)
        HIN = sb("HIN", [128, 8], F32)
        HFO = sb("HFO", [128, 8], F32)
        XAP3 = sb("XAP3", [128, 8, 3], F32)
        TMP16 = sb("TMP16", [128, NSEQ], F32)
        MIX = sb("MIX", [128, 16, NTOK], BF16)
        STG = sb("STG", [128, 1024], F32)
        STG2 = sb("STG2", [128, 1024], F32)
        SSQ = sb("SSQ", [128, 9, 4], F32)
        SSQ1 = sb("SSQ1", [128, 9], F32)
        LN2 = sb("LN2", [128, 9], F32)
        RS2 = sb("RS2", [128, 9], F32)

        main_bytes = (nc.sbuf_bytes_remaining - 1024) // 64 * 64
        MAINF = sb("MAIN", [128, main_bytes // 4], F32)

        class Carver:
            def __init__(self):
                self.off = 0

            def reset(self, base=0):
                self.off = base

            def f32(self, n):
                a = MAINF[:, self.off // 4: self.off // 4 + n]
                self.off += 4 * n
                assert self.off <= main_bytes, (self.off, main_bytes)
                return a

            def bf16(self, n):
                n2 = (n + 1) // 2
                a = MAINF[:, self.off // 4: self.off // 4 + n2].bitcast(BF16)
                self.off += 4 * n2
                assert self.off <= main_bytes, (self.off, main_bytes)
                return a[:, 0:n]

        CV = Carver()
        XNT = CV.bf16(16 * NT).rearrange("p (k t) -> p k t", k=16)
        WS = CV.bf16(NWS * 16 * 128).rearrange("p (s k j) -> p s k j", s=NWS, k=16)
        UBS = CV.bf16(8 * NSEQ * 38).rearrange("p (j s r) -> p j s r", j=8, s=NSEQ)
        EARLY_B = CV.off
        XS = CV.f32(8 * NSEQ * 11).rearrange("p (h s r) -> p h s r", h=8, s=NSEQ)
        HS = CV.f32(8 * NS).rearrange("p (h t) -> p h t", h=8)
        EARLY = CV.off
        WO_END = main_bytes - 16 * 512 * 2

        PS = [es.enter_context(nc.psum_tensor(f"ps{b}", [128, 512], F32)) for b in range(8)]
        sems = {e: es.enter_context(nc.semaphore(f"s_{e}")) for e in ("pe", "act", "dve", "pool")}
        lane_sems = [es.enter_context(nc.semaphore(f"lane{i}")) for i in range(NLANES)]
        cc_sem = es.enter_context(nc.semaphore("cc_sem"))

        bank_ctr = [0]

        def next_bank():
            b = bank_ctr[0] % 8
            bank_ctr[0] += 1
            return b

        def xkeys(c0, c1):
            ks = []
            for t in range(c0 // 128, (c1 - 1) // 128 + 1):
                ks += [("xnt", t, 0), ("xnt", t, 1)]
            return ks

        def dma(eng, out, in_, reads, writes, after=(), nofence=None, **kw):
            return S.op(eng, lambda e: e.dma_start(out=out, in_=in_, **kw), reads, writes, ndma=1, after=after,
                        nofence=nofence)

        def act(out, in_, func, reads, writes, bias=None, scale=None, accum_out=None):
            kw = {}
            if bias is not None:
                kw["bias"] = bias
            if scale is not None:
                kw["scale"] = scale
            if accum_out is not None:
                kw["accum_out"] = accum_out
            return S.op("act", lambda e: e.activation(out=out, in_=in_, func=func, **kw), reads, writes)

        def tt(eng, out, in0, in1, op, reads, writes):
            return S.op(eng, lambda e: e.tensor_tensor(out=out, in0=in0, in1=in1, op=op), reads, writes)

        def stt(out, in0, scalar, in1, op0, op1, reads, writes):
            return S.op("dve", lambda e: e.scalar_tensor_tensor(out=out, in0=in0, scalar=scalar, in1=in1,
                                                               op0=op0, op1=op1), reads, writes)

        def ts(eng, out, in0, s1, s2, op0, op1, reads, writes):
            if op1 is None:
                return S.op(eng, lambda e: e.tensor_scalar(out=out, in0=in0, scalar1=s1, scalar2=None, op0=op0),
                            reads, writes)
            return S.op(eng, lambda e: e.tensor_scalar(out=out, in0=in0, scalar1=s1, scalar2=s2, op0=op0, op1=op1),
                        reads, writes)

        def copy(eng, out, in_, reads, writes):
            if eng == "act":
                return S.op("act", lambda e: e.copy(out=out, in_=in_), reads, writes)
            return S.op(eng, lambda e: e.tensor_copy(out=out, in_=in_), reads, writes)

        def scan(out, a, b, initial, reads, writes):
            return S.op("dve", lambda e: e.tensor_tensor_scan(out=out, data0=a, data1=b, initial=initial,
                                                              op0=ALU.mult, op1=ALU.add), reads, writes)

        def mm_group(out, pairs, reads, writes):
            def fn(e):
                n = len(pairs)
                ins = None
                for i, (l, r) in enumerate(pairs):
                    ins = e.matmul(out, l, r, start=(i == 0), stop=(i == n - 1))
                return ins
            return S.op("pe", fn, reads, writes)

        def transposes(items, reads, writes):
            def fn(e):
                ins = None
                for (o, i, idn) in items:
                    ins = e.transpose(out=o, in_=i, identity=idn)
                return ins
            return S.op("pe", fn, reads, writes)

        wseq = list(range(0, 16)) + [24, 16, 25, 17, 32, 33]
        for j in range(2, 8):
            wseq += [24 + j, 16 + j]
        wseq += list(range(34, 40))
        wpos = {m: i for i, m in enumerate(wseq)}
        wloaded = [0]

        w_after = {}

        def w_prefetch(upto):
            while wloaded[0] <= min(upto, len(wseq) - 1):
                i = wloaded[0]
                m = wseq[i]
                s = i % NWS
                dma("pool", WS[:, s, :, :], win_d[m].rearrange("p (k j) -> p k j", k=16), [], [("ws", s)],
                    after=w_after.get(i, ()))
                wloaded[0] += 1

        inproj_done = {}

        def inproj(m, chunks, consume, cis=None):
            i = wpos[m]
            s = i % NWS
            w_prefetch(i)
            ndone = inproj_done.get(m, 0) + (len(chunks) if cis is None else len(cis))
            inproj_done[m] = ndone
            for ci, (c0, c1) in enumerate(chunks):
                if cis is not None and ci not in cis:
                    continue
                b = next_bank()
                pairs = [(WS[:, s, kc, :], XNT[:, kc, c0:c1]) for kc in range(16)]
                mm_group(PS[b][:, 0:c1 - c0], pairs, [("ws", s)] + xkeys(c0, c1), [("ps", b)])
                consume(ci, b, c0, c1)
            if ndone >= len(chunks):
                w_prefetch(i + NWS)

        CV.reset(EARLY)
        NXT = 5
        XT = [CV.f32(D) for _ in range(NXT)]
        XB = [CV.bf16(D) for _ in range(2)]
        STCB = [XT[0][:, 0:1024], XT[1][:, 0:1024]]
        STA = XT[2][:, 0:1024]
        STH = XT[3][:, 0:1024]
        GBC = CV.f32(D)

        JUNK = CV.bf16(D)
        NG1 = CV.f32(D)
        XA_OFF = CV.off
        XA = [CV.f32(HALO + NP) for _ in range(2)]
        ntile = (NT + 127) // 128
        xload = {}

        def p1_load(t):
            r0 = t * 128
            nr = min(128, NT - r0)
            xload[t] = dma("sp", XT[t % NXT][:nr, :], x_d[r0:r0 + nr, :], [], [("xt", t % NXT)])

        S.op("dve", lambda e: e.memset(EPSC[:, :], EPS), [], ["EPSC"])
        S.op("dve", lambda e: e.memset(ONES[:, :], 1.0), [], ["ONES"])
        dma("sp", NG1[0:1, :], ng_d.unsqueeze(0), [], ["NG1"])
        p1_load(0)
        dma("sp", IDF[:, :], ident_d, [], ["IDF"])
        S.op("dve", lambda e: e.memset(ONE1[:, :], 1.0), [], ["ONE1"])
        for q in range(4):
            b = next_bank()
            S.op("pe", (lambda e, q=q, b=b: e.matmul(PS[b][:, :], ONE1[0:1, :], NG1[0:1, q * 512:(q + 1) * 512],
                                                    start=True, stop=True)), ["ONE1", "NG1"], [("ps", b)])
            copy("dve", GBC[:, q * 512:(q + 1) * 512], PS[b][:, :], [("ps", b)], ["GBC"])
        copy("dve", IDB[:, :], IDF[:, :], ["IDF"], ["IDB"])
        p1_load(1)
        p1_load(2)
        p1_load(3)
        p1_load(4)
        dma("sp", PA[:, :, :], pa_d.rearrange("p (h k) -> p h k", h=8), [], ["PA"])
        dma("sp", PB[:, :, :], pb_d.rearrange("p (h k) -> p h k", h=8), [], ["PB"])
        dma("sp", MASK[:, :], mask_d, [], ["MASK"])

        def p1_sq(t):
            r0 = t * 128
            nr = min(128, NT - r0)
            xs_ = t % NXT
            act(JUNK[:nr, :], XT[xs_][:nr, :], AF.Square, [("xt", xs_)], ["JUNK", ("ss", t)],
                accum_out=SS[:nr, t:t + 1])
            act(LNT[:nr, t:t + 1], SS[:nr, t:t + 1], AF.Ln, [("ss", t), "EPSC"], [("lnt", t)],
                bias=EPSC[:nr, :], scale=1.0 / D)
            act(RS[:nr, t:t + 1], LNT[:nr, t:t + 1], AF.Exp, [("lnt", t)], [("rs", t)], scale=-0.5)

        def p1_stt(t):
            r0 = t * 128
            nr = min(128, NT - r0)
            xs_, bs_ = t % NXT, t % 2
            stt(XB[bs_][:nr, :], XT[xs_][:nr, :], RS[:nr, t:t + 1], GBC[:nr, :], ALU.mult, ALU.mult,
                [("xt", xs_), ("rs", t), "GBC"], [("xb", bs_)])
            if t + NXT < ntile:
                p1_load(t + NXT)
            if t == 4:
                dma("pool", GW[:, :, :], gw_d.rearrange("p (g j) -> p g j", g=16), [], ["GW"], after=[xload[ntile - 1]])
            if t == 1:
                w_after[0] = [xload[5]]
                w_prefetch(0)
            if t in (4, 5, 6):
                w_after[t - 3] = [xload[8 if t == 4 else ntile - 1]]
                w_prefetch(t - 3)

        def p1_tr(t):
            r0 = t * 128
            nr = min(128, NT - r0)
            bs_ = t % 2
            for half in range(2):
                b = next_bank()
                psb = PS[b][:, :].bitcast(BF16)
                items = [(psb[:, s * 128: s * 128 + nr], XB[bs_][:nr, (half * 8 + s) * 128:(half * 8 + s + 1) * 128],
                          IDB[:nr, :nr]) for s in range(8)]
                transposes(items, [("xb", bs_), "IDB"], [("ps", b)])
                src = psb.rearrange("p (s c) -> p s c", s=8)[:, :, 0:nr]
                dst = XNT[:, half * 8:(half + 1) * 8, r0:r0 + nr]
                copy("act" if half == 0 else "dve", dst, src, [("ps", b)], [("xnt", t, half)])

        def a1_s0(h, cis=None):
            sl = h % 2

            def consume(ci, b, c0, c1):
                if ci < 2:
                    copy("act", XA[sl][:, c0:c1], PS[b][:, 0:c1 - c0], [("ps", b)], [("xa", sl, ci)])
                else:
                    copy("act", XA[sl][:, 1024:HALO + NP], PS[b][:, 0:HALO], [("ps", b)], [("xa", sl, 2)])
                    copy("act", XS[:, h, :, 3:11], PS[b][:, HALO:HALO + NS].rearrange("p (s t) -> p s t", s=NSEQ),
                         [("ps", b), "XShist"], [("xs", h)])
            inproj(h, HCH, consume, cis)

        early_xa = {7: (0, 0), 8: (0, 1)}
        xa_done = {h: set() for h in range(8)}
        p1_sq(0)
        p1_sq(1)
        p1_stt(0)
        for t in range(ntile):
            if t + 2 < ntile:
                p1_sq(t + 2)
            if t + 1 < ntile:
                p1_stt(t + 1)
            p1_tr(t)
            if t in early_xa:
                h_, c_ = early_xa[t]
                a1_s0(h_, cis=[c_])
                xa_done[h_].add(c_)
        act(CLT[:, :], PA[:, :, 7], AF.Exp, ["PA"], ["CLT"], scale=-1.0)
        act(CL[:, :], CLT[:, :], AF.Ln, ["CLT"], ["CL0"], bias=1.0)
        ts("dve", CL[:, :], CL[:, :], -8.0, None, ALU.mult, None, ["CL0"], ["CL"])

        dma("sp", STH[:NSEQ, :], sth_d, [], [("xt", 3)])
        dma("sp", STA[:NSEQ * 3, :], stca_d, [], [("xt", 2)])
        b = next_bank()
        transposes([(PS[b][:, h * NSEQ:(h + 1) * NSEQ], STH[:NSEQ, h * 128:(h + 1) * 128], IDF[:NSEQ, :NSEQ])
                    for h in range(8)], [("xt", 3), "IDF"], [("ps", b)])
        copy("dve", H0S[:, :, :], PS[b][:, 0:8 * NSEQ].rearrange("p (h s) -> p h s", h=8), [("ps", b)], ["H0S"])
        b = next_bank()
        transposes([(PS[b][:, h * 48:(h + 1) * 48], STA[:48, h * 128:(h + 1) * 128], IDF[:48, :48])
                    for h in range(8)], [("xt", 2), "IDF"], [("ps", b)])
        copy("dve", XS[:, :, :, 0:3], PS[b][:, 0:8 * 48].rearrange("p (h s r) -> p h s r", h=8, s=NSEQ),
             [("ps", b)], ["XShist"])
        for rt in range(4):
            sl = rt % 2
            dma("sp", STCB[sl][:120, :], stcb_d[rt * 120:(rt + 1) * 120, :], [], [("xt", sl)])
            for half in range(2):
                b = next_bank()
                transposes([(PS[b][:, jj * 120:(jj + 1) * 120],
                             STCB[sl][:120, (half * 4 + jj) * 128:(half * 4 + jj + 1) * 128],
                             IDF[:120, :120]) for jj in range(4)], [("xt", sl), "IDF"], [("ps", b)])
                copy("act", UBS[:, half * 4:(half + 1) * 4, rt * 4:(rt + 1) * 4, 0:30],
                     PS[b][:, 0:480].rearrange("p (j s r) -> p j s r", j=4, s=4), [("ps", b)], [("ubsh", rt, half)])

        S.fence()
        CV.reset(EARLY)
        XC = [CV.f32(NTOK) for _ in range(2)]
        XCB = [CV.bf16(NTOK) for _ in range(2)]
        RR = CV.f32(NTOK)
        II = CV.f32(NTOK)
        T1 = CV.f32(NTOK)
        AT = [CV.f32(NTOK) for _ in range(2)]
        BT = [CV.f32(NTOK) for _ in range(2)]
        HSCR = CV.f32(NP)
        assert CV.off <= XA_OFF, (CV.off, XA_OFF)

        def a1_s1(h):
            sl = h % 2
            xa_r = [("xa", sl, 0), ("xa", sl, 1), ("xa", sl, 2), "PA"]
            ts("dve", XC[sl][:, 0:NP], XA[sl][:, HALO:HALO + NP], PA[:, h, 3:4], PA[:, h, 4:5], ALU.mult, ALU.add,
               xa_r, [("xc", sl)])
            for k in range(3):
                stt(XC[sl][:, 0:NP], XA[sl][:, HALO - 3 + k:HALO - 3 + k + NP], PA[:, h, k:k + 1], XC[sl][:, 0:NP],
                    ALU.mult, ALU.add, xa_r + [("xc", sl)], [("xc", sl)])
            xcs = XC[sl][:, NP:NTOK].rearrange("p (s t) -> p s t", s=NSEQ)
            ts("dve", xcs, XS[:, h, :, 3:11], PA[:, h, 3:4], PA[:, h, 4:5], ALU.mult, ALU.add,
               [("xs", h), "XShist", "PA"], [("xcs", sl)])
            for k in range(3):
                stt(xcs, XS[:, h, :, k:k + 8], PA[:, h, k:k + 1], xcs, ALU.mult, ALU.add,
                    [("xs", h), "XShist", "PA", ("xcs", sl)], [("xcs", sl)])
            copy("dve", XCB[sl][:, :], XC[sl][:, :], [("xc", sl), ("xcs", sl)], [("xcb", sl)])
            copy("act", XAP3[:, h, :], XA[sl][:, HALO + NP - 3:HALO + NP], xa_r, [("xap3", h)])

        GCH = [(0, 512), (512, 1024), (1024, NTOK)]

        def a1_s2(h):
            sl = h % 2
            for g, dst, key, bcol in ((0, RR, "RR", 5), (1, II, "II", 6)):
                for ci, (c0, c1) in enumerate(GCH):
                    b = next_bank()
                    mm_group(PS[b][:, 0:c1 - c0], [(GW[:, g * 8 + h, :], XCB[sl][:, c0:c1])], ["GW", ("xcb", sl)],
                             [("ps", b)])
                    act(dst[:, c0:c1], PS[b][:, 0:c1 - c0], AF.Sigmoid, [("ps", b), "PA"], [(key, ci)],
                        bias=PA[:, h, bcol:bcol + 1])

        def a1_s3(h):
            sl = h % 2
            rk = [("RR", i) for i in range(3)]
            ik = [("II", i) for i in range(3)]
            atk = [("at", sl), ("ats", sl)]
            btk = [("bt", sl), ("bts", sl)]
            act(AT[sl][:, :], RR[:, :], AF.Exp, rk + ["CL"], atk, scale=CL[:, h:h + 1])
            act(T1[:, :], AT[sl][:, :], AF.Square, atk, ["T1"])
            act(T1[:, :], T1[:, :], AF.Ln, ["T1"], ["T1"], scale=-1.0, bias=1.0)
            act(T1[:, :], T1[:, :], AF.Exp, ["T1"], ["T1"], scale=0.5)
            tt("dve", BT[sl][:, :], II[:, :], XC[sl][:, :], ALU.mult, ik + [("xc", sl), ("xcs", sl)], btk)

        def a1_s4(h):
            sl = h % 2
            btk = [("bt", sl), ("bts", sl)]
            tt("dve", BT[sl][:, :], BT[sl][:, :], T1[:, :], ALU.mult, btk + ["T1"], btk)
            dma("sp", a_sc[h, :, :], AT[sl][:, 0:NP], [("at", sl)], [("asc", h)])
            dma("sp", b_sc[h, :, :], BT[sl][:, 0:NP], [("bt", sl)], [("bsc", h)])
            scan(HSCR[:, :], AT[sl][:, 0:NP], BT[sl][:, 0:NP], 0.0, [("at", sl), ("bt", sl)], ["HSCR"])
            copy("dve", HFIN[:, h:h + 1], HSCR[:, NP - 1:NP], ["HSCR"], [("hfin", h)])
            a3 = AT[sl][:, NP:NTOK].rearrange("p (s t) -> p s t", s=NSEQ)
            b3 = BT[sl][:, NP:NTOK].rearrange("p (s t) -> p s t", s=NSEQ)
            tt("dve", TMP16[:, :], a3[:, :, 0], H0S[:, h, :], ALU.mult, [("ats", sl), "H0S"], ["TMP16"])
            tt("dve", b3[:, :, 0], b3[:, :, 0], TMP16[:, :], ALU.add, [("bts", sl), "TMP16"], [("bts", sl)])
            S.op("dve", lambda e: e.memset(a3[:, :, 0], 0.0), [("ats", sl)], [("ats", sl)])
            scan(HS[:, h, :], AT[sl][:, NP:NTOK], BT[sl][:, NP:NTOK], 0.0, [("ats", sl), ("bts", sl)], [("hs", h)])

        def a2_t0(h):
            def consume(ci, b, c0, c1):
                act(MIX[:, h, c0 - HALO:c1 - HALO], PS[b][:, 0:c1 - c0], AF.Silu, [("ps", b)], [("mix", h, ci)])
            inproj(8 + h, NCHK, consume)

        for i in range(-3, 8):
            if 0 <= i < 8:
                a1_s3(i)
            if 0 <= i + 2 < 8:
                a1_s1(i + 2)
            if 0 <= i < 8:
                a1_s4(i)
            if 0 <= i + 1 < 8:
                a1_s2(i + 1)
            if 0 <= i + 3 < 8:
                a1_s0(i + 3, cis=[c for c in range(3) if c not in xa_done[i + 3]])
            elif i + 3 >= 8:
                a2_t0(i + 3 - 8)

        S.fence()
        CV.reset(EARLY)
        NAB = 3
        ABL = [CV.f32(2 * NP) for _ in range(NAB)]
        HP = [CV.f32(NP) for _ in range(2)]

        def a_out_gather():
            hsk = [("hs", h) for h in range(8)]
            copy("act", HSL[:, :, :], HS[:, :, :].rearrange("p h (s t) -> p h s t", s=NSEQ)[:, :, :, TS - 1], hsk, ["HSL"])
            copy("act", X3[:, :, :].rearrange("p h (s r) -> p h s r", s=NSEQ), XS[:, :, :, 8:11],
                 [("xs", h) for h in range(8)], ["X3"])

        def a_outputs():
            S.nofence_default = True
            for half in range(2):
                b = next_bank()
                transposes([(PS[b][:NSEQ, q * 128:(q + 1) * 128], HSL[:, half * 4 + q, :], IDF[:, :]) for q in range(4)],
                           ["HSL", "IDF"], [("ps", b)])
                copy("act", STG[:NSEQ, half * 512:(half + 1) * 512], PS[b][:NSEQ, :], [("ps", b)], ["STG"])
            dma("sp", ohs_d, STG[:NSEQ, :], ["STG"], ["ohs"])
            for half in range(2):
                b = next_bank()
                transposes([(PS[b][:48, q * 128:(q + 1) * 128], X3[:, half * 4 + q, :], IDF[:, :]) for q in range(4)],
                           ["X3", "IDF"], [("ps", b)])
                copy("act", STG2[:48, half * 512:(half + 1) * 512], PS[b][:48, :], [("ps", b)], ["STG2"])
            dma("sp", ocas_d, STG2[:48, :], ["STG2"], ["ocas"])
            for half in range(2):
                b = next_bank()
                transposes([(PS[b][:3, q * 128:(q + 1) * 128], XAP3[:, half * 4 + q, :], IDF[:, :]) for q in range(4)],
                           [("xap3", h) for h in range(8)] + ["IDF"], [("ps", b)])
                copy("act", STG[:3, half * 512:(half + 1) * 512], PS[b][:3, :], [("ps", b)], ["STG"])
            dma("sp", ocap_d, STG[:3, :], ["STG"], ["ocap"])
            b = next_bank()
            transposes([(PS[b][:8, 0:128], HFO[:, :], IDF[:, :])], [("hfo", h) for h in range(8)] + ["IDF"], [("ps", b)])
            copy("act", STG2[:8, 0:128], PS[b][:8, 0:128], [("ps", b)], ["STG2"])
            dma("sp", ohp_d, STG2[:8, 0:128], ["STG2"], ["ohp"])
            S.nofence_default = False

        def a2_load(h):
            sl = h % NAB
            dma("sp", ABL[sl][:, 0:NP], a_sc[h, :, :], [("asc", h)], [("abla", sl)])
            dma("sp", ABL[sl][:, NP:2 * NP], b_sc[h, :, :], [("bsc", h)], [("ablb", sl)])

        def a2_t1(h):
            sl = h % 2
            al = h % NAB
            scan(HP[sl][:, :], ABL[al][:, 0:NP], ABL[al][:, NP:2 * NP], HIN[:, h:h + 1],
                 [("abla", al), ("ablb", al), "HIN"], [("hp", sl)])
            mk = [("mix", h, i) for i in range(3)]
            tt("dve", MIX[:, h, 0:NP], HP[sl][:, :], MIX[:, h, 0:NP], ALU.mult, [("hp", sl)] + mk, [("mixp", h)])
            tt("dve", MIX[:, h, NP:NTOK], HS[:, h, :], MIX[:, h, NP:NTOK], ALU.mult, [("hs", h)] + mk, [("mixs", h)])
            copy("dve", HFO[:, h:h + 1], HP[sl][:, NP - 1:NP], [("hp", sl)], [("hfo", h)])

        a2_load(0)
        a2_load(1)
        a2_load(2)
        a_out_gather()
        hk = [("hfin", h) for h in range(8)]
        dma("pool", cc_in[:, :], HFIN[:, :], hk, ["cc_in"])
        S.op("pool", lambda e: e.collective_compute("AllGather", ALU.bypass,
                                                    replica_groups=[[0, 1], [2, 3], [4, 5], [6, 7]],
                                                    ins=[cc_in.ap().opt()], outs=[cc_out.ap().opt()]),
             ["cc_in"], ["cc_out"], cc=True)
        dma("sp", HINR[:, :], cc_out[0:128, :], ["cc_out"], ["HINR"])
        ts("dve", HIN[:, :], HINR[:, :], MASK[:, 0:1], None, ALU.mult, None, ["HINR", "MASK"], ["HIN"])

        for h in range(8):
            if h + 3 < 8:
                a2_t0(h + 3)
            a2_t1(h)
            if h + 3 < 8:
                a2_load(h + 3)

        KD = 14
        NPT = 31 - KD
        CV.reset(EARLY_B)
        CB = CV.bf16(8 * NTOK).rearrange("p (j t) -> p j t", j=8)
        ZZ = [CV.f32(NTOK) for _ in range(2)]
        MEAN = CV.f32(NTOK)
        RSTD = CV.f32(NTOK)
        ACC = CV.f32(NTOK)
        SQ = [CV.bf16(512) for _ in range(2)]
        assert CV.off >= EARLY + 3 * 2 * NP * 4 + 2 * NP * 4, CV.off
        S.late_keys([("cb", j, ci) for j in range(8) for ci in range(3)] + [("zz", 0), ("zz", 1)] +
                    [("m2", i) for i in range(3)] + [("mean", i) for i in range(3)] +
                    [("rstd", i) for i in range(3)] + [("sq", i) for i in range(2)] + ["ACC", "ACCs"])
        SG = [CV.f32(NT) for _ in range(2)]
        UU = [CV.f32(NT) for _ in range(2)]
        UBP = [CV.bf16(HALO + NP) for _ in range(2)]
        DG = [CV.bf16(NPT * 128).rearrange("p (k c) -> p k c", k=NPT) for _ in range(2)]
        assert CV.off <= WO_END, (CV.off, WO_END)

        def b1_u0(j):
            sl = j % 2
            def dg_fn(e):
                ins = None
                for k in range(KD, 31):
                    ins = e.activation(out=DG[sl][:, k - KD, :], in_=IDB[:, :], func=AF.Identity, scale=PB[:, j, k:k + 1])
                return ins
            S.op("act", dg_fn, ["IDB", "PB"], [("dg", sl)])

            def cons_g(ci, b, c0, c1):
                act(SG[sl][:, c0:c1], PS[b][:, 0:c1 - c0], AF.Sigmoid, [("ps", b)], [("sg", sl, ci)])
            inproj(24 + j, HCH, cons_g)

            def cons_v(ci, b, c0, c1):
                copy("act", UU[sl][:, c0:c1], PS[b][:, 0:c1 - c0], [("ps", b)], [("uu", sl, ci)])
            inproj(16 + j, HCH, cons_v)
            for ci, (c0, c1) in enumerate(HCH):
                tt("dve", UU[sl][:, c0:c1], UU[sl][:, c0:c1], SG[sl][:, c0:c1], ALU.mult,
                   [("uu", sl, ci), ("sg", sl, ci)], [("uu", sl, ci)])
            uk = [("uu", sl, 0), ("uu", sl, 1), ("uu", sl, 2)]
            ceng = "dve"
            copy(ceng, UBP[sl][:, :], UU[sl][:, 0:HALO + NP], uk, [("ubp", sl)])
            copy(ceng, UBS[:, j, :, 30:38], UU[sl][:, HALO + NP:NT].rearrange("p (s t) -> p s t", s=NSEQ), uk,
                 [("ubsn", j)])

        def b2_proj(j):
            def cons(ci, b, c0, c1):
                act(MIX[:, 8 + j, c0 - HALO:c1 - HALO], PS[b][:, 0:c1 - c0], AF.Silu, [("ps", b)], [("mixb", j, ci)])
            inproj(32 + j, NCHK, cons)

        def b1_taps(j):
            sl = j % 2
            uk = [("uu", sl, 0), ("uu", sl, 1), ("uu", sl, 2)]
            ubsk = [("ubsn", j)] + [("ubsh", rt, j // 4) for rt in range(4)]
            accs = ACC[:, NP:NTOK].rearrange("p (s t) -> p s t", s=NSEQ)
            if j == 0:
                ts("dve", ACC[:, 0:NP], UU[sl][:, 2:2 + NP], PB[:, j, 0:1], None, ALU.mult, None, uk + ["PB"], ["ACC"])
                ts("dve", accs, UBS[:, j, :, 0:8], PB[:, j, 0:1], None, ALU.mult, None, ubsk + ["PB"], ["ACCs"])
            else:
                act(ACC[:, 0:NP], UU[sl][:, 2:2 + NP], AF.Identity, uk + ["PB"], ["ACC"], scale=PB[:, j, 0:1])
                act(accs, UBS[:, j, :, 0:8], AF.Identity, ubsk + ["PB"], ["ACCs"], scale=PB[:, j, 0:1])
            for k in range(1, KD):
                stt(ACC[:, 0:NP], UU[sl][:, 2 + k:2 + k + NP], PB[:, j, k:k + 1], ACC[:, 0:NP], ALU.mult, ALU.add,
                    uk + ["PB", "ACC"], ["ACC"])
                stt(accs, UBS[:, j, :, k:k + 8], PB[:, j, k:k + 1], accs, ALU.mult, ALU.add,
                    ubsk + ["PB", "ACCs"], ["ACCs"])

        def b1_u1(j):
            sl = j % 2
            uk = [("uu", sl, 0), ("uu", sl, 1), ("uu", sl, 2)]
            b = next_bank()
            transposes([(PS[b][:30, 0:128], UU[sl][:, HALO + NP - 30:HALO + NP], IDF[:, :]),
                        (PS[b][:, 128:256], UU[sl][:, HALO + NP:NT], IDF[:, :])], uk + ["IDF"], [("ps", b)])
            copy("act", STG[:30, j * 128:(j + 1) * 128], PS[b][:30, 0:128], [("ps", b)], ["STG"])
            copy("act", STG2[:, j * 128:(j + 1) * 128], PS[b][:, 128:256], [("ps", b)], ["STG2"])
            for ci, (c0, c1) in enumerate([(0, 512), (512, 1024)]):
                b = next_bank()
                pairs = [(DG[sl][:, k - KD, :], UBP[sl][:, c0 + k + 2:c0 + k + 2 + 512]) for k in range(KD, 31)]
                mm_group(PS[b][:, :], pairs, [("dg", sl), ("ubp", sl)], [("ps", b)])
                stt(CB[:, j, c0:c1], PS[b][:, :], PB[:, j, 31:32], ACC[:, c0:c1], ALU.add, ALU.add,
                    [("ps", b), "PB", "ACC"], [("cb", j, ci)])
            b = next_bank()
            pairs = [(DG[sl][:, k - KD, :], UBS[:, j, :, k:k + 8]) for k in range(KD, 31)]
            mm_group(PS[b][:, 0:NS].rearrange("p (s t) -> p s t", s=NSEQ), pairs,
                     [("dg", sl), ("ubsn", j)] + [("ubsh", rt, j // 4) for rt in range(4)], [("ps", b)])
            stt(CB[:, j, NP:NTOK], PS[b][:, 0:NS], PB[:, j, 31:32], ACC[:, NP:NTOK], ALU.add, ALU.add,
                [("ps", b), "PB", "ACCs"], [("cb", j, 2)])

        b1_u0(0)
        for j in range(8):
            b1_taps(j)
            if j + 1 < 8:
                b1_u0(j + 1)
            if j == 0:
                b2_proj(0)
                b2_proj(1)
                a_outputs()
            if j == 2:
                dma("sp", ocbh_d, stcb_d.rearrange("(s r) c -> s r c", r=30)[:, 8:30, :], [], ["ocbh"], nofence=True)
            if 2 <= j <= 5:
                n_ = j - 2
                dma("pool", wo_bf[n_, :, :].rearrange("p (k j) -> p k j", k=16),
                    wout_d[n_].rearrange("p (k j) -> p k j", k=16), [], [("wobf", n_)], nofence=True)
            if j == 7:
                b2_proj(2)
                b2_proj(3)
            b1_u1(j)
        S.nofence_default = True
        dma("sp", ocbp_d, STG[:30, :], ["STG"], ["ocbp"])
        dma("sp", ocbn_d, STG2[:, :], ["STG2"], ["ocbn"])
        S.nofence_default = False

        SCH = [(0, 512), (512, 1024), (1024, NTOK)]
        sbanks = {}
        for ci, (c0, c1) in enumerate(SCH):
            n = c1 - c0
            b1_ = next_bank()
            b2_ = next_bank()
            sbanks[ci] = (b1_, b2_)
            for j in range(8):
                sq = j % 2
                if j % 2 == 0:
                    act(SQ[sq][:, 0:n], CB[:, j, c0:c1], AF.Square, [("cb", j, ci)], [("sq", sq)])
                else:
                    tt("dve", SQ[sq][:, 0:n], CB[:, j, c0:c1], CB[:, j, c0:c1], ALU.mult, [("cb", j, ci)], [("sq", sq)])
                S.op("pe", (lambda e, j=j, b=b1_, c0=c0, c1=c1, n=n:
                            e.matmul(PS[b][:, 0:n], ONES[:, :], CB[:, j, c0:c1], start=(j == 0), stop=(j == 7))),
                     ["ONES", ("cb", j, ci)], [("ps", b1_)] if j == 0 else [("psacc", b1_, j)])
                S.op("pe", (lambda e, j=j, b=b2_, sq=sq, n=n:
                            e.matmul(PS[b][:, 0:n], ONES[:, :], SQ[sq][:, 0:n], start=(j == 0), stop=(j == 7))),
                     ["ONES", ("sq", sq)], [("ps", b2_)] if j == 0 else [("psacc", b2_, j)])
        mk = [("mean", ci) for ci in range(3)]
        rk_ = [("rstd", ci) for ci in range(3)]
        m2k = [("m2", ci) for ci in range(3)]
        for ci, (c0, c1) in enumerate(SCH):
            b1_, _b2 = sbanks[ci]
            k1 = [("ps", b1_)] + [("psacc", b1_, j) for j in range(1, 8)]
            act(MEAN[:, c0:c1], PS[b1_][:, 0:c1 - c0], AF.Identity, k1, [("mean", ci)], scale=1.0 / WA)
        tt("dve", ZZ[0][:, :], MEAN[:, :], MEAN[:, :], ALU.mult, mk, m2k)
        for ci, (c0, c1) in enumerate(SCH):
            _b1, b2_ = sbanks[ci]
            k2 = [("ps", b2_)] + [("psacc", b2_, j) for j in range(1, 8)]
            stt(RSTD[:, c0:c1], PS[b2_][:, 0:c1 - c0], 1.0 / WA, ZZ[0][:, c0:c1], ALU.mult, ALU.subtract,
                k2 + m2k, [("rstd", ci)])
        act(RSTD[:, :], RSTD[:, :], AF.Ln, rk_ + ["EPSC"], rk_, bias=EPSC[:, :])
        act(RSTD[:, :], RSTD[:, :], AF.Exp, rk_, rk_, scale=-0.5)
        stt(MEAN[:, :], MEAN[:, :], -1.0, RSTD[:, :], ALU.mult, ALU.mult, mk + rk_, mk)
        stat_k = [("rstd", i) for i in range(3)] + [("mean", i) for i in range(3)]

        def b2_rest_a(j):
            sl = j % 2
            cbk = [("cb", j, i) for i in range(3)]
            zk = [("zz", sl)] + ([("m2", i) for i in range(3)] if sl == 0 else [])
            tt("dve", ZZ[sl][:, :], CB[:, j, :], RSTD[:, :], ALU.mult, cbk + stat_k, zk)
            tt("dve", ZZ[sl][:, :], ZZ[sl][:, :], MEAN[:, :], ALU.add, zk + stat_k, zk)
            act(ZZ[sl][:, :], ZZ[sl][:, :], AF.Silu, zk + ["PB"], zk, bias=PB[:, j, 33:34], scale=PB[:, j, 32:33])

        def b2_rest_b(j):
            sl = j % 2
            zk = [("zz", sl)] + ([("m2", i) for i in range(3)] if sl == 0 else [])
            tt("dve", MIX[:, 8 + j, :], ZZ[sl][:, :], MIX[:, 8 + j, :], ALU.mult,
               zk + [("mixb", j, i) for i in range(3)], [("mixB", j)])

        def wo_view(off):
            return MAINF[:, off // 4: off // 4 + 16 * 512 // 2].bitcast(BF16).rearrange("p (k n) -> p k n", k=16)

        WO = [wo_view(WO_END), None, None]
        WSL = {0: 0, 1: 1, 2: 2, 3: 0}

        def wo_load(n):
            src = wo_bf[n, :, :].rearrange("p (k j) -> p k j", k=16)
            for hf in range(2):
                dma("pool", WO[WSL[n]][:, hf * 8:(hf + 1) * 8, :], src[:, hf * 8:(hf + 1) * 8, :], [("wobf", n)],
                    [("wo", WSL[n], hf)])

        wo_load(0)
        b2_rest_a(0)
        for j in range(8):
            if j + 1 < 8:
                b2_rest_a(j + 1)
            b2_rest_b(j)
            if j + 4 < 8:
                b2_proj(j + 4)

        S.fence()
        CV.reset(0)
        HRES = CV.f32(9 * D).rearrange("p (i d) -> p i d", i=9)
        WO[1] = CV.bf16(16 * 512).rearrange("p (k n) -> p k n", k=16)
        WO[2] = CV.bf16(16 * 512).rearrange("p (k n) -> p k n", k=16)
        NXRE = 6
        XRE = [CV.f32(512) for _ in range(NXRE)]
        SQJ = CV.bf16(512)
        FGB = CV.f32(D)
        assert CV.off <= WO_END, (CV.off, WO_END)
        wo_load(1)
        allmix = []
        for h in range(8):
            allmix += [("mixp", h), ("mixs", h)]
        for j in range(8):
            allmix += [("mixB", j)]
        xi = 0
        amix = []
        for h in range(8):
            amix += [("mixp", h), ("mixs", h)]
        bmix = [("mixB", j) for j in range(8)]

        def mm_part(out, pairs, first, last, reads, writes):
            def fn(e):
                ins = None
                n_ = len(pairs)
                for q, (l, r) in enumerate(pairs):
                    ins = e.matmul(out, l, r, start=(first and q == 0), stop=(last and q == n_ - 1))
                return ins
            return S.op("pe", fn, reads, writes)

        pend = [None]

        def fin_a(i):
            S.op("dve", lambda e: e.tensor_reduce(out=SSQ1[:, i:i + 1], in_=SSQ[:, i, :],
                                                  axis=mybir.AxisListType.X, op=ALU.add),
                 [("ssq", i, q) for q in range(4)], [("ssq1", i)])
            act(LN2[:, i:i + 1], SSQ1[:, i:i + 1], AF.Ln, [("ssq1", i), "EPSC"], [("ln2", i)],
                bias=EPSC[:, :], scale=1.0 / D)
            act(RS2[:, i:i + 1], LN2[:, i:i + 1], AF.Exp, [("ln2", i)], [("rs2", i)], scale=-0.5)

        def fin_b(i):
            hk = [("hres", i, q) for q in range(4)]
            if i < 8:
                stt(HRES[:, i, :], HRES[:, i, :], RS2[:, i:i + 1], FGB[:, :], ALU.mult, ALU.mult,
                    hk + [("rs2", i), "FGB"], hk)
                dma("pool", y_d[i * 128:(i + 1) * 128, :], HRES[:, i, :], hk, [("y", i)])
            else:
                for hf in range(2):
                    cs = slice(hf * 1024, (hf + 1) * 1024)
                    hk2 = [("hres", i, 2 * hf), ("hres", i, 2 * hf + 1)]
                    stt(HRES[:, i, cs], HRES[:, i, cs], RS2[:, i:i + 1], FGB[:, cs], ALU.mult, ALU.mult,
                        hk2 + [("rs2", i), "FGB"], hk2)
                    dma("sp", y_d[i * 128:(i + 1) * 128, cs], HRES[:, i, cs], hk2, [("y", i, hf)])

        G = 4
        ph1 = [(n, i) for n in range(2) for i in range(9)]
        ph2 = [(n, i) for i in range(9) for n in (2, 3)]
        G0 = 7
        batches = [ph1[0:G0]] + [ph1[g0:g0 + G] for g0 in range(G0, len(ph1), G)] + \
                  [ph2[g0:g0 + G] for g0 in range(0, len(ph2), G)]
        for batch in batches:
            banks = []
            for (n, i) in batch:
                b = next_bank()
                banks.append(b)
                pairs = [(MIX[:, kc, i * 128:(i + 1) * 128], WO[WSL[n]][:, kc, :]) for kc in range(8)]
                mm_part(PS[b][:, :], pairs, True, False, amix + [("wo", WSL[n], 0)], [("ps", b)])
            for (n, i), b in zip(batch, banks):
                pairs = [(MIX[:, kc, i * 128:(i + 1) * 128], WO[WSL[n]][:, kc, :]) for kc in range(8, 16)]
                mm_part(PS[b][:, :], pairs, False, True, bmix + [("wo", WSL[n], 1)], [("ps2", b)])
                xs_ = xi % NXRE
                xi += 1
                r0 = HALO + i * 128
                dma("sp", XRE[xs_][:, :], x_d[r0:r0 + 128, n * 512:(n + 1) * 512], [], [("xre", xs_)])
                tt("dve", HRES[:, i, n * 512:(n + 1) * 512], PS[b][:, :], XRE[xs_][:, :], ALU.add,
                   [("ps", b), ("ps2", b), ("xre", xs_)], [("hres", i, n)])
                act(SQJ[:, :], HRES[:, i, n * 512:(n + 1) * 512], AF.Square, [("hres", i, n)], ["SQJ", ("ssq", i, n)],
                    accum_out=SSQ[:, i, n:n + 1])
                if n == 3:
                    if pend[0] is not None:
                        fin_b(pend[0])
                    fin_a(i)
                    pend[0] = i
                if i == 4 and n == 0:
                    wo_load(2)
                if i == 0 and n == 1:
                    dma("sp", FGB[:, :], fg_d.partition_broadcast(128), [], ["FGB"])
                if i == 8 and n == 0:
                    wo_load(3)

        fin_b(pend[0])

        S.finalize()
        with nc.Block(no_gpsimd_drain=True) as block:
            @block.tensor
            def _(e):
                S.emit("pe", e, sems, lane_sems, cc_sem)

            @block.scalar
            def _(e):
                S.emit("act", e, sems, lane_sems, cc_sem)

            @block.vector
            def _(e):
                S.emit("dve", e, sems, lane_sems, cc_sem)

            @block.gpsimd
            def _(e):
                S.emit("pool", e, sems, lane_sems, cc_sem)

            @block.sync
            def _(e):
                S.emit("sp", e, sems, lane_sems, cc_sem, final_wait=True)
    return nc


_NC_CACHE = {}


def kernel(x_prompt, x_sample, state_lru_h, state_lru_conv, state_glu_conv,
           norm_gain, w_in, conv_a_w, conv_a_b, gate_a_w, gate_a_b, gate_x_w, gate_x_b,
           lru_param, conv_b_w, conv_b_b, ln_b_gain, ln_b_bias, w_out, final_gain):
    f = np.float32
    x_prompt = np.asarray(x_prompt, f)
    x_sample = np.asarray(x_sample, f)
    sth = np.asarray(state_lru_h, f)[0]
    stca = np.asarray(state_lru_conv, f)[0]
    stcb = np.asarray(state_glu_conv, f)[0]
    w_in = np.asarray(w_in, f)[0]
    w_out = np.asarray(w_out, f)[0]

    win_r = np.ascontiguousarray(w_in.reshape(16, 128, 40, 128).transpose(2, 1, 0, 3).reshape(40, 128, 2048))
    wout_r = np.ascontiguousarray(w_out.reshape(16, 128, 4, 512).transpose(2, 1, 0, 3).reshape(4, 128, 16 * 512))
    gws = np.stack([np.asarray(gate_a_w, f)[0], np.asarray(gate_x_w, f)[0]])
    gw_r = np.ascontiguousarray(gws.transpose(2, 0, 1, 3).reshape(128, 16 * 128))

    def chan(v):
        return np.asarray(v, f).reshape(8, 128).T

    pa = np.zeros((128, 8, 8), f)
    caw = np.asarray(conv_a_w, f)[0]
    for k in range(4):
        pa[:, :, k] = chan(caw[k])
    pa[:, :, 4] = chan(np.asarray(conv_a_b, f)[0])
    pa[:, :, 5] = chan(np.asarray(gate_a_b, f)[0])
    pa[:, :, 6] = chan(np.asarray(gate_x_b, f)[0])
    pa[:, :, 7] = chan(np.asarray(lru_param, f)[0])
    pb = np.zeros((128, 8, 35), f)
    cbw = np.asarray(conv_b_w, f)[0]
    for k in range(31):
        pb[:, :, k] = chan(cbw[k])
    pb[:, :, 31] = chan(np.asarray(conv_b_b, f)[0])
    pb[:, :, 32] = chan(np.asarray(ln_b_gain, f)[0])
    pb[:, :, 33] = chan(np.asarray(ln_b_bias, f)[0])
    ng = np.ascontiguousarray(np.asarray(norm_gain, f)[0])
    fg = np.ascontiguousarray(np.asarray(final_gain, f))
    ident = np.eye(128, dtype=f)

    in_maps = []
    for c in range(NCORES):
        q, half = c // 2, c % 2
        xl = np.zeros((NT, D), f)
        if half == 1:
            xl[0:HALO] = x_prompt[q, NP - HALO:NP]
        xl[HALO:HALO + NP] = x_prompt[q, half * NP:(half + 1) * NP]
        xl[HALO + NP:] = x_sample[c * NSEQ:(c + 1) * NSEQ].reshape(NS, D)
        in_maps.append({
            "x": xl,
            "sth": np.ascontiguousarray(sth[c * NSEQ:(c + 1) * NSEQ]),
            "stca": np.ascontiguousarray(stca[c * NSEQ:(c + 1) * NSEQ].reshape(NSEQ * 3, WA)),
            "stcb": np.ascontiguousarray(stcb[c * NSEQ:(c + 1) * NSEQ].reshape(NSEQ * 30, WA)),
            "win": win_r, "wout": wout_r, "gw": gw_r,
            "pa": pa.reshape(128, 64), "pb": pb.reshape(128, 280),
            "ng": ng, "fg": fg, "ident": ident,
            "mask": np.full((128, 1), float(half), f),
        })

    if "nc" not in _NC_CACHE:
        _NC_CACHE["nc"] = build_nc()
    nc = _NC_CACHE["nc"]
    res = run_bass_kernel_spmd(nc, in_maps, core_ids=list(range(NCORES)))
    R = res.results

    y_prompt = np.zeros((4, 2048, D), f)
    y_sample = np.zeros((128, TS, D), f)
    o_hp = np.zeros((1, 4, WA), f)
    o_cap = np.zeros((1, 4, 3, WA), f)
    o_cbp = np.zeros((1, 4, 30, WA), f)
    o_hs = np.zeros((1, 128, WA), f)
    o_cas = np.zeros((1, 128, 3, WA), f)
    o_cbs = np.zeros((1, 128, 30, WA), f)
    for c in range(NCORES):
        q, half = c // 2, c % 2
        r = R[c]
        y_prompt[q, half * NP:(half + 1) * NP] = r["y"][0:NP]
        y_sample[c * NSEQ:(c + 1) * NSEQ] = r["y"][NP:].reshape(NSEQ, TS, D)
        if half == 1:
            o_hp[0, q] = r["ohp"].reshape(WA)
            o_cap[0, q] = r["ocap"]
            o_cbp[0, q] = r["ocbp"]
        o_hs[0, c * NSEQ:(c + 1) * NSEQ] = r["ohs"]
        o_cas[0, c * NSEQ:(c + 1) * NSEQ] = r["ocas"].reshape(NSEQ, 3, WA)
        o_cbs[0, c * NSEQ:(c + 1) * NSEQ, 0:22] = r["ocbh"]
        o_cbs[0, c * NSEQ:(c + 1) * NSEQ, 22:30] = r["ocbn"].reshape(NSEQ, TS, WA)
    return (y_prompt, y_sample, o_hp, o_cap, o_cbp, o_hs, o_cas, o_cbs)
```

```python
import contextlib
import numpy as np
import concourse.bass as bass
import concourse.mybir as mybir
from concourse.bass_utils import run_bass_kernel_spmd

F32 = mybir.dt.float32
BF16 = mybir.dt.bfloat16
AF = mybir.ActivationFunctionType
ALU = mybir.AluOpType

NCORES = 8
D = 2048
WA = 1024
NH = 8
CONV_A = 4
CONV_B = 31
HALO = 32
NP = 1024
NS = 128
NSEQ = 16
TS = 8
NT = HALO + NP + NS
NTOK = NP + NS
EPS = 1e-6
NWS = 4
NLANES = 30
NL_SP = 20

HCH = [(0, 512), (512, 1024), (1024, NT)]
NCHK = [(HALO, HALO + 512), (HALO + 512, HALO + 1024), (HALO + 1024, NT)]


class Sched:
    def __init__(self):
        self.ops = []
        self.res = {}
        self.lane_last = [None] * NLANES
        self.lane_count = [0] * NLANES
        self.next_lane = {"sp": 0, "pool": NL_SP}
        self.cur_fence = []
        self.phase_keys = set()
        self.nofence_default = False

    def _deps(self, reads, writes, nofence=False):
        deps = set()
        for k in list(reads) + list(writes):
            if not nofence:
                self.phase_keys.add(k)
            if k not in self.res and not nofence:
                deps.update(self.cur_fence)
        for k in reads:
            st = self.res.get(k)
            if st is not None and st[0] is not None:
                deps.add(st[0])
        for k in writes:
            st = self.res.get(k)
            if st is not None:
                if st[0] is not None:
                    deps.add(st[0])
                deps.update(st[1])
        return deps

    def op(self, eng, fn, reads=(), writes=(), ndma=0, cc=False, after=(), nofence=None):
        reads = list(reads)
        writes = list(writes)
        if nofence is None:
            nofence = self.nofence_default
        deps = self._deps(reads, writes, nofence)
        deps.update(after)
        idx = len(self.ops)
        o = dict(eng=eng, fn=fn, deps=deps, ndma=ndma, cc=cc, lane=None, sig=False, val=None)
        if ndma:
            lane = self.next_lane[eng]
            lo, hi = (0, NL_SP) if eng == "sp" else (NL_SP, NLANES)
            self.next_lane[eng] = lo + (lane + 1 - lo) % (hi - lo)
            if self.lane_last[lane] is not None:
                deps.add(self.lane_last[lane])
            self.lane_last[lane] = idx
            self.lane_count[lane] += ndma
            o["lane"] = lane
            o["val"] = 16 * self.lane_count[lane]
        self.ops.append(o)
        for k in reads:
            st = self.res.setdefault(k, [None, []])
            st[1].append(idx)
        for k in writes:
            self.res[k] = [idx, []]
        return idx

    def fence(self):
        deps = set(self.cur_fence)
        for k in self.phase_keys:
            st = self.res.get(k)
            if st is not None:
                if st[0] is not None:
                    deps.add(st[0])
                deps.update(st[1])
        self.cur_fence = sorted(deps)
        self.phase_keys = set()

    def late_keys(self, keys):
        deps = set(self.cur_fence)
        for k in self.phase_keys:
            st = self.res.get(k)
            if st is not None:
                if st[0] is not None:
                    deps.add(st[0])
                deps.update(st[1])
        deps = sorted(deps)
        for k in keys:
            assert k not in self.res, k
            self.res[k] = [None, list(deps)]

    def finalize(self):
        for o in self.ops:
            for d in o["deps"]:
                self.ops[d]["sig"] = True
        cnt = {}
        for o in self.ops:
            if o["ndma"] or o["cc"]:
                continue
            if o["sig"]:
                cnt[o["eng"]] = cnt.get(o["eng"], 0) + 1
                o["val"] = cnt[o["eng"]]

    def emit(self, eng_name, eng, sems, lane_sems, cc_sem, final_wait=False):
        seen = {}
        for o in self.ops:
            if o["eng"] != eng_name:
                continue
            need = {}
            for d in o["deps"]:
                p = self.ops[d]
                if p["cc"]:
                    key, sem, val = "cc", cc_sem, 1
                elif p["ndma"]:
                    key, sem, val = ("l", p["lane"]), lane_sems[p["lane"]], p["val"]
                else:
                    if p["eng"] == "pe" and eng_name == "pe":
                        continue
                    key, sem, val = p["eng"], sems[p["eng"]], p["val"]
                if val > need.get(key, (None, 0))[1]:
                    need[key] = (sem, val)
            for key, (sem, val) in need.items():
                if val > seen.get(key, 0):
                    eng.wait_ge(sem, val)
                    seen[key] = val
            r = o["fn"](eng)
            if o["cc"]:
                r.then_inc(cc_sem, 1)
            elif o["ndma"]:
                rl = r if isinstance(r, (list, tuple)) else [r]
                assert len(rl) == o["ndma"], (len(rl), o["ndma"])
                for ins in rl:
                    ins.then_inc(lane_sems[o["lane"]], 16)
            elif o["sig"]:
                r.then_inc(sems[o["eng"]], 1)
        if final_wait:
            for lane in range(NLANES):
                if self.lane_count[lane]:
                    eng.wait_ge(lane_sems[lane], 16 * self.lane_count[lane])


def build_nc():
    nc = bass.Bass("TRN2", target_bir_lowering=False)
    S = Sched()

    def din(name, shape):
        return nc.dram_tensor(name, list(shape), F32, kind="ExternalInput").ap()

    def dout(name, shape):
        return nc.dram_tensor(name, list(shape), F32, kind="ExternalOutput").ap()

    x_d = din("x", [NT, D])
    sth_d = din("sth", [NSEQ, WA])
    stca_d = din("stca", [NSEQ * 3, WA])
    stcb_d = din("stcb", [NSEQ * 30, WA])
    win_d = din("win", [40, 128, D])
    wout_d = din("wout", [4, 128, 16 * 512])
    gw_d = din("gw", [128, 16 * 128])
    pa_d = din("pa", [128, 64])
    pb_d = din("pb", [128, 8 * 35])
    ng_d = din("ng", [D])
    fg_d = din("fg", [D])
    ident_d = din("ident", [128, 128])
    mask_d = din("mask", [128, 1])

    y_d = dout("y", [NTOK, D])
    ohp_d = dout("ohp", [8, 128])
    ocap_d = dout("ocap", [3, WA])
    ocbp_d = dout("ocbp", [30, WA])
    ohs_d = dout("ohs", [NSEQ, WA])
    ocas_d = dout("ocas", [NSEQ * 3, WA])
    ocbh_d = dout("ocbh", [NSEQ, 22, WA])
    ocbn_d = dout("ocbn", [NS, WA])

    a_sc = nc.dram_tensor("a_scr", [8, 128, NP], F32)
    b_sc = nc.dram_tensor("b_scr", [8, 128, NP], F32)
    wo_bf = nc.dram_tensor("wo_bf", [4, 128, 16 * 512], BF16)
    cc_in = nc.dram_tensor("cc_in", [128, 8], F32)
    cc_out = nc.dram_tensor("cc_out", [256, 8], F32)

    es = contextlib.ExitStack()
    with es:
        def sb(name, shape, dt):
            return es.enter_context(nc.sbuf_tensor(name, list(shape), dt))

        PA = sb("PA", [128, 8, 8], F32)
        PB = sb("PB", [128, 8, 35], F32)
        GW = sb("GW", [128, 16, 128], BF16)
        IDF = sb("IDF", [128, 128], F32)
        IDB = sb("IDB", [128, 128], BF16)
        ONES = sb("ONES", [128, 128], BF16)
        MASK = sb("MASK", [128, 1], F32)
        EPSC = sb("EPSC", [128, 1], F32)
        ONE1 = sb("ONE1", [1, 128], F32)
        CL = sb("CL", [128, 8], F32)
        CLT = sb("CLT", [128, 8], F32)
        SS = sb("SS", [128, 16], F32)
        LNT = sb("LNT", [128, 16], F32)
        RS = sb("RS", [128, 16], F32)
        H0S = sb("H0S", [128, 8, NSEQ], F32)
        HSL = sb("HSL", [128, 8, NSEQ], F32)
        X3 = sb("X3", [128, 8, NSEQ * 3], F32)
        HFIN = sb("HFIN", [128, 8], F32)
        HINR = sb("HINR", [128, 8], F32)
        HIN = sb("HIN", [128, 8], F32)
        HFO = sb("HFO", [128, 8], F32)
        XAP3 = sb("XAP3", [128, 8, 3], F32)
        TMP16 = sb("TMP16", [128, NSEQ], F32)
        MIX = sb("MIX", [128, 16, NTOK], BF16)
        STG = sb("STG", [128, 1024], F32)
        STG2 = sb("STG2", [128, 1024], F32)
        SSQ = sb("SSQ", [128, 9, 4], F32)
        SSQ1 = sb("SSQ1", [128, 9], F32)
        LN2 = sb("LN2", [128, 9], F32)
        RS2 = sb("RS2", [128, 9], F32)

        main_bytes = (nc.sbuf_bytes_remaining - 1024) // 64 * 64
        MAINF = sb("MAIN", [128, main_bytes // 4], F32)

        class Carver:
            def __init__(self):
                self.off = 0

            def reset(self, base=0):
                self.off = base

            def f32(self, n):
                a = MAINF[:, self.off // 4: self.off // 4 + n]
                self.off += 4 * n
                assert self.off <= main_bytes, (self.off, main_bytes)
                return a

            def bf16(self, n):
                n2 = (n + 1) // 2
                a = MAINF[:, self.off // 4: self.off // 4 + n2].bitcast(BF16)
                self.off += 4 * n2
                assert self.off <= main_bytes, (self.off, main_bytes)
                return a[:, 0:n]

        CV = Carver()
        XNT = CV.bf16(16 * NT).rearrange("p (k t) -> p k t", k=16)
        WS = CV.bf16(NWS * 16 * 128).rearrange("p (s k j) -> p s k j", s=NWS, k=16)
        UBS = CV.bf16(8 * NSEQ * 38).rearrange("p (j s r) -> p j s r", j=8, s=NSEQ)
        EARLY_B = CV.off
        XS = CV.f32(8 * NSEQ * 11).rearrange("p (h s r) -> p h s r", h=8, s=NSEQ)
        HS = CV.f32(8 * NS).rearrange("p (h t) -> p h t", h=8)
        EARLY = CV.off
        WO_END = main_bytes - 16 * 512 * 2

        PS = [es.enter_context(nc.psum_tensor(f"ps{b}", [128, 512], F32)) for b in range(8)]
        sems = {e: es.enter_context(nc.semaphore(f"s_{e}")) for e in ("pe", "act", "dve", "pool")}
        lane_sems = [es.enter_context(nc.semaphore(f"lane{i}")) for i in range(NLANES)]
        cc_sem = es.enter_context(nc.semaphore("cc_sem"))

        bank_ctr = [0]

        def next_bank():
            b = bank_ctr[0] % 8
            bank_ctr[0] += 1
            return b

        def xkeys(c0, c1):
            ks = []
            for t in range(c0 // 128, (c1 - 1) // 128 + 1):
                ks += [("xnt", t, 0), ("xnt", t, 1)]
            return ks

        def dma(eng, out, in_, reads, writes, after=(), nofence=None, **kw):
            return S.op(eng, lambda e: e.dma_start(out=out, in_=in_, **kw), reads, writes, ndma=1, after=after,
                        nofence=nofence)

        def act(out, in_, func, reads, writes, bias=None, scale=None, accum_out=None):
            kw = {}
            if bias is not None:
                kw["bias"] = bias
            if scale is not None:
                kw["scale"] = scale
            if accum_out is not None:
                kw["accum_out"] = accum_out
            return S.op("act", lambda e: e.activation(out=out, in_=in_, func=func, **kw), reads, writes)

        def tt(eng, out, in0, in1, op, reads, writes):
            return S.op(eng, lambda e: e.tensor_tensor(out=out, in0=in0, in1=in1, op=op), reads, writes)

        def stt(out, in0, scalar, in1, op0, op1, reads, writes):
            return S.op("dve", lambda e: e.scalar_tensor_tensor(out=out, in0=in0, scalar=scalar, in1=in1,
                                                               op0=op0, op1=op1), reads, writes)

        def ts(eng, out, in0, s1, s2, op0, op1, reads, writes):
            if op1 is None:
                return S.op(eng, lambda e: e.tensor_scalar(out=out, in0=in0, scalar1=s1, scalar2=None, op0=op0),
                            reads, writes)
            return S.op(eng, lambda e: e.tensor_scalar(out=out, in0=in0, scalar1=s1, scalar2=s2, op0=op0, op1=op1),
                        reads, writes)

        def copy(eng, out, in_, reads, writes):
            if eng == "act":
                return S.op("act", lambda e: e.copy(out=out, in_=in_), reads, writes)
            return S.op(eng, lambda e: e.tensor_copy(out=out, in_=in_), reads, writes)

        def scan(out, a, b, initial, reads, writes):
            return S.op("dve", lambda e: e.tensor_tensor_scan(out=out, data0=a, data1=b, initial=initial,
                                                              op0=ALU.mult, op1=ALU.add), reads, writes)

        def mm_group(out, pairs, reads, writes):
            def fn(e):
                n = len(pairs)
                ins = None
                for i, (l, r) in enumerate(pairs):
                    ins = e.matmul(out, l, r, start=(i == 0), stop=(i == n - 1))
                return ins
            return S.op("pe", fn, reads, writes)

        def transposes(items, reads, writes):
            def fn(e):
                ins = None
                for (o, i, idn) in items:
                    ins = e.transpose(out=o, in_=i, identity=idn)
                return ins
            return S.op("pe", fn, reads, writes)

        wseq = list(range(0, 16)) + [24, 16, 25, 17, 32, 33]
        for j in range(2, 8):
            wseq += [24 + j, 16 + j]
        wseq += list(range(34, 40))
        wpos = {m: i for i, m in enumerate(wseq)}
        wloaded = [0]

        w_after = {}

        def w_prefetch(upto):
            while wloaded[0] <= min(upto, len(wseq) - 1):
                i = wloaded[0]
                m = wseq[i]
                s = i % NWS
                dma("pool", WS[:, s, :, :], win_d[m].rearrange("p (k j) -> p k j", k=16), [], [("ws", s)],
                    after=w_after.get(i, ()))
                wloaded[0] += 1

        inproj_done = {}

        def inproj(m, chunks, consume, cis=None):
            i = wpos[m]
            s = i % NWS
            w_prefetch(i)
            ndone = inproj_done.get(m, 0) + (len(chunks) if cis is None else len(cis))
            inproj_done[m] = ndone
            for ci, (c0, c1) in enumerate(chunks):
                if cis is not None and ci not in cis:
                    continue
                b = next_bank()
                pairs = [(WS[:, s, kc, :], XNT[:, kc, c0:c1]) for kc in range(16)]
                mm_group(PS[b][:, 0:c1 - c0], pairs, [("ws", s)] + xkeys(c0, c1), [("ps", b)])
                consume(ci, b, c0, c1)
            if ndone >= len(chunks):
                w_prefetch(i + NWS)

        CV.reset(EARLY)
        NXT = 5
        XT = [CV.f32(D) for _ in range(NXT)]
        XB = [CV.bf16(D) for _ in range(2)]
        STCB = [XT[0][:, 0:1024], XT[1][:, 0:1024]]
        STA = XT[2][:, 0:1024]
        STH = XT[3][:, 0:1024]
        GBC = CV.f32(D)

        JUNK = CV.bf16(D)
        NG1 = CV.f32(D)
        XA_OFF = CV.off
        XA = [CV.f32(HALO + NP) for _ in range(2)]
        ntile = (NT + 127) // 128
        xload = {}

        def p1_load(t):
            r0 = t * 128
            nr = min(128, NT - r0)
            xload[t] = dma("sp", XT[t % NXT][:nr, :], x_d[r0:r0 + nr, :], [], [("xt", t % NXT)])

        S.op("dve", lambda e: e.memset(EPSC[:, :], EPS), [], ["EPSC"])
        S.op("dve", lambda e: e.memset(ONES[:, :], 1.0), [], ["ONES"])
        dma("sp", NG1[0:1, :], ng_d.unsqueeze(0), [], ["NG1"])
        p1_load(0)
        dma("sp", IDF[:, :], ident_d, [], ["IDF"])
        S.op("dve", lambda e: e.memset(ONE1[:, :], 1.0), [], ["ONE1"])
        for q in range(4):
            b = next_bank()
            S.op("pe", (lambda e, q=q, b=b: e.matmul(PS[b][:, :], ONE1[0:1, :], NG1[0:1, q * 512:(q + 1) * 512],
                                                    start=True, stop=True)), ["ONE1", "NG1"], [("ps", b)])
            copy("dve", GBC[:, q * 512:(q + 1) * 512], PS[b][:, :], [("ps", b)], ["GBC"])
        copy("dve", IDB[:, :], IDF[:, :], ["IDF"], ["IDB"])
        p1_load(1)
        p1_load(2)
        p1_load(3)
        p1_load(4)
        dma("sp", PA[:, :, :], pa_d.rearrange("p (h k) -> p h k", h=8), [], ["PA"])
        dma("sp", PB[:, :, :], pb_d.rearrange("p (h k) -> p h k", h=8), [], ["PB"])
        dma("sp", MASK[:, :], mask_d, [], ["MASK"])

        def p1_sq(t):
            r0 = t * 128
            nr = min(128, NT - r0)
            xs_ = t % NXT
            act(JUNK[:nr, :], XT[xs_][:nr, :], AF.Square, [("xt", xs_)], ["JUNK", ("ss", t)],
                accum_out=SS[:nr, t:t + 1])
            act(LNT[:nr, t:t + 1], SS[:nr, t:t + 1], AF.Ln, [("ss", t), "EPSC"], [("lnt", t)],
                bias=EPSC[:nr, :], scale=1.0 / D)
            act(RS[:nr, t:t + 1], LNT[:nr, t:t + 1], AF.Exp, [("lnt", t)], [("rs", t)], scale=-0.5)

        def p1_stt(t):
            r0 = t * 128
            nr = min(128, NT - r0)
            xs_, bs_ = t % NXT, t % 2
            stt(XB[bs_][:nr, :], XT[xs_][:nr, :], RS[:nr, t:t + 1], GBC[:nr, :], ALU.mult, ALU.mult,
                [("xt", xs_), ("rs", t), "GBC"], [("xb", bs_)])
            if t + NXT < ntile:
                p1_load(t + NXT)
            if t == 4:
                dma("pool", GW[:, :, :], gw_d.rearrange("p (g j) -> p g j", g=16), [], ["GW"], after=[xload[ntile - 1]])
            if t == 1:
                w_after[0] = [xload[5]]
                w_prefetch(0)
            if t in (4, 5, 6):
                w_after[t - 3] = [xload[6 if t == 4 else ntile - 1]]
                w_prefetch(t - 3)

        def p1_tr(t):
            r0 = t * 128
            nr = min(128, NT - r0)
            bs_ = t % 2
            for half in range(2):
                b = next_bank()
                psb = PS[b][:, :].bitcast(BF16)
                items = [(psb[:, s * 128: s * 128 + nr], XB[bs_][:nr, (half * 8 + s) * 128:(half * 8 + s + 1) * 128],
                          IDB[:nr, :nr]) for s in range(8)]
                transposes(items, [("xb", bs_), "IDB"], [("ps", b)])
                src = psb.rearrange("p (s c) -> p s c", s=8)[:, :, 0:nr]
                dst = XNT[:, half * 8:(half + 1) * 8, r0:r0 + nr]
                copy("act" if half == 0 else "dve", dst, src, [("ps", b)], [("xnt", t, half)])

        def a1_s0(h, cis=None):
            sl = h % 2

            def consume(ci, b, c0, c1):
                if ci < 2:
                    copy("act", XA[sl][:, c0:c1], PS[b][:, 0:c1 - c0], [("ps", b)], [("xa", sl, ci)])
                else:
                    copy("act", XA[sl][:, 1024:HALO + NP], PS[b][:, 0:HALO], [("ps", b)], [("xa", sl, 2)])
                    copy("act", XS[:, h, :, 3:11], PS[b][:, HALO:HALO + NS].rearrange("p (s t) -> p s t", s=NSEQ),
                         [("ps", b), "XShist"], [("xs", h)])
            inproj(h, HCH, consume, cis)

        early_xa = {7: (0, 0), 8: (0, 1), 9: (1, 0)}
        xa_done = {h: set() for h in range(8)}
        p1_sq(0)
        p1_sq(1)
        p1_stt(0)
        for t in range(ntile):
            if t + 2 < ntile:
                p1_sq(t + 2)
            if t + 1 < ntile:
                p1_stt(t + 1)
            p1_tr(t)
            if t in early_xa:
                h_, c_ = early_xa[t]
                a1_s0(h_, cis=[c_])
                xa_done[h_].add(c_)
        act(CLT[:, :], PA[:, :, 7], AF.Exp, ["PA"], ["CLT"], scale=-1.0)
        act(CL[:, :], CLT[:, :], AF.Ln, ["CLT"], ["CL0"], bias=1.0)
        ts("dve", CL[:, :], CL[:, :], -8.0, None, ALU.mult, None, ["CL0"], ["CL"])

        dma("sp", STH[:NSEQ, :], sth_d, [], [("xt", 3)])
        dma("sp", STA[:NSEQ * 3, :], stca_d, [], [("xt", 2)])
        b = next_bank()
        transposes([(PS[b][:, h * NSEQ:(h + 1) * NSEQ], STH[:NSEQ, h * 128:(h + 1) * 128], IDF[:NSEQ, :NSEQ])
                    for h in range(8)], [("xt", 3), "IDF"], [("ps", b)])
        copy("dve", H0S[:, :, :], PS[b][:, 0:8 * NSEQ].rearrange("p (h s) -> p h s", h=8), [("ps", b)], ["H0S"])
        b = next_bank()
        transposes([(PS[b][:, h * 48:(h + 1) * 48], STA[:48, h * 128:(h + 1) * 128], IDF[:48, :48])
                    for h in range(8)], [("xt", 2), "IDF"], [("ps", b)])
        copy("dve", XS[:, :, :, 0:3], PS[b][:, 0:8 * 48].rearrange("p (h s r) -> p h s r", h=8, s=NSEQ),
             [("ps", b)], ["XShist"])
        for rt in range(4):
            sl = rt % 2
            dma("sp", STCB[sl][:120, :], stcb_d[rt * 120:(rt + 1) * 120, :], [], [("xt", sl)])
            for half in range(2):
                b = next_bank()
                transposes([(PS[b][:, jj * 120:(jj + 1) * 120],
                             STCB[sl][:120, (half * 4 + jj) * 128:(half * 4 + jj + 1) * 128],
                             IDF[:120, :120]) for jj in range(4)], [("xt", sl), "IDF"], [("ps", b)])
                copy("act", UBS[:, half * 4:(half + 1) * 4, rt * 4:(rt + 1) * 4, 0:30],
                     PS[b][:, 0:480].rearrange("p (j s r) -> p j s r", j=4, s=4), [("ps", b)], [("ubsh", rt, half)])

        S.fence()
        CV.reset(EARLY)
        XC = [CV.f32(NTOK) for _ in range(2)]
        XCB = [CV.bf16(NTOK) for _ in range(2)]
        RR = CV.f32(NTOK)
        II = CV.f32(NTOK)
        T1 = CV.f32(NTOK)
        AT = [CV.f32(NTOK) for _ in range(2)]
        BT = [CV.f32(NTOK) for _ in range(2)]
        HSCR = CV.f32(NP)
        assert CV.off <= XA_OFF, (CV.off, XA_OFF)

        def a1_s1(h):
            sl = h % 2
            xa_r = [("xa", sl, 0), ("xa", sl, 1), ("xa", sl, 2), "PA"]
            ts("dve", XC[sl][:, 0:NP], XA[sl][:, HALO:HALO + NP], PA[:, h, 3:4], PA[:, h, 4:5], ALU.mult, ALU.add,
               xa_r, [("xc", sl)])
            for k in range(3):
                stt(XC[sl][:, 0:NP], XA[sl][:, HALO - 3 + k:HALO - 3 + k + NP], PA[:, h, k:k + 1], XC[sl][:, 0:NP],
                    ALU.mult, ALU.add, xa_r + [("xc", sl)], [("xc", sl)])
            xcs = XC[sl][:, NP:NTOK].rearrange("p (s t) -> p s t", s=NSEQ)
            ts("dve", xcs, XS[:, h, :, 3:11], PA[:, h, 3:4], PA[:, h, 4:5], ALU.mult, ALU.add,
               [("xs", h), "XShist", "PA"], [("xcs", sl)])
            for k in range(3):
                stt(xcs, XS[:, h, :, k:k + 8], PA[:, h, k:k + 1], xcs, ALU.mult, ALU.add,
                    [("xs", h), "XShist", "PA", ("xcs", sl)], [("xcs", sl)])
            copy("dve", XCB[sl][:, :], XC[sl][:, :], [("xc", sl), ("xcs", sl)], [("xcb", sl)])
            copy("act", XAP3[:, h, :], XA[sl][:, HALO + NP - 3:HALO + NP], xa_r, [("xap3", h)])

        GCH = [(0, 512), (512, 1024), (1024, NTOK)]

        def a1_s2(h):
            sl = h % 2
            for g, dst, key, bcol in ((0, RR, "RR", 5), (1, II, "II", 6)):
                for ci, (c0, c1) in enumerate(GCH):
                    b = next_bank()
                    mm_group(PS[b][:, 0:c1 - c0], [(GW[:, g * 8 + h, :], XCB[sl][:, c0:c1])], ["GW", ("xcb", sl)],
                             [("ps", b)])
                    act(dst[:, c0:c1], PS[b][:, 0:c1 - c0], AF.Sigmoid, [("ps", b), "PA"], [(key, ci)],
                        bias=PA[:, h, bcol:bcol + 1])

        def a1_s3(h):
            sl = h % 2
            rk = [("RR", i) for i in range(3)]
            ik = [("II", i) for i in range(3)]
            atk = [("at", sl), ("ats", sl)]
            btk = [("bt", sl), ("bts", sl)]
            act(AT[sl][:, :], RR[:, :], AF.Exp, rk + ["CL"], atk, scale=CL[:, h:h + 1])
            act(T1[:, :], AT[sl][:, :], AF.Square, atk, ["T1"])
            act(T1[:, :], T1[:, :], AF.Ln, ["T1"], ["T1"], scale=-1.0, bias=1.0)
            act(T1[:, :], T1[:, :], AF.Exp, ["T1"], ["T1"], scale=0.5)
            tt("dve", BT[sl][:, :], II[:, :], XC[sl][:, :], ALU.mult, ik + [("xc", sl), ("xcs", sl)], btk)

        def a1_s4(h):
            sl = h % 2
            btk = [("bt", sl), ("bts", sl)]
            tt("dve", BT[sl][:, :], BT[sl][:, :], T1[:, :], ALU.mult, btk + ["T1"], btk)
            dma("sp", a_sc[h, :, :], AT[sl][:, 0:NP], [("at", sl)], [("asc", h)])
            dma("sp", b_sc[h, :, :], BT[sl][:, 0:NP], [("bt", sl)], [("bsc", h)])
            scan(HSCR[:, :], AT[sl][:, 0:NP], BT[sl][:, 0:NP], 0.0, [("at", sl), ("bt", sl)], ["HSCR"])
            copy("dve", HFIN[:, h:h + 1], HSCR[:, NP - 1:NP], ["HSCR"], [("hfin", h)])
            a3 = AT[sl][:, NP:NTOK].rearrange("p (s t) -> p s t", s=NSEQ)
            b3 = BT[sl][:, NP:NTOK].rearrange("p (s t) -> p s t", s=NSEQ)
            tt("dve", TMP16[:, :], a3[:, :, 0], H0S[:, h, :], ALU.mult, [("ats", sl), "H0S"], ["TMP16"])
            tt("dve", b3[:, :, 0], b3[:, :, 0], TMP16[:, :], ALU.add, [("bts", sl), "TMP16"], [("bts", sl)])
            S.op("dve", lambda e: e.memset(a3[:, :, 0], 0.0), [("ats", sl)], [("ats", sl)])
            scan(HS[:, h, :], AT[sl][:, NP:NTOK], BT[sl][:, NP:NTOK], 0.0, [("ats", sl), ("bts", sl)], [("hs", h)])

        def a2_t0(h):
            def consume(ci, b, c0, c1):
                act(MIX[:, h, c0 - HALO:c1 - HALO], PS[b][:, 0:c1 - c0], AF.Silu, [("ps", b)], [("mix", h, ci)])
            inproj(8 + h, NCHK, consume)

        for i in range(-3, 8):
            if 0 <= i < 8:
                a1_s3(i)
            if 0 <= i + 2 < 8:
                a1_s1(i + 2)
            if 0 <= i < 8:
                a1_s4(i)
            if 0 <= i + 1 < 8:
                a1_s2(i + 1)
            if 0 <= i + 3 < 8:
                a1_s0(i + 3, cis=[c for c in range(3) if c not in xa_done[i + 3]])
            elif i + 3 >= 8:
                a2_t0(i + 3 - 8)

        S.fence()
        CV.reset(EARLY)
        NAB = 3
        ABL = [CV.f32(2 * NP) for _ in range(NAB)]
        HP = [CV.f32(NP) for _ in range(2)]

        def a_out_gather():
            hsk = [("hs", h) for h in range(8)]
            copy("act", HSL[:, :, :], HS[:, :, :].rearrange("p h (s t) -> p h s t", s=NSEQ)[:, :, :, TS - 1], hsk, ["HSL"])
            copy("act", X3[:, :, :].rearrange("p h (s r) -> p h s r", s=NSEQ), XS[:, :, :, 8:11],
                 [("xs", h) for h in range(8)], ["X3"])

        def a_outputs():
            S.nofence_default = True
            for half in range(2):
                b = next_bank()
                transposes([(PS[b][:NSEQ, q * 128:(q + 1) * 128], HSL[:, half * 4 + q, :], IDF[:, :]) for q in range(4)],
                           ["HSL", "IDF"], [("ps", b)])
                copy("act", STG[:NSEQ, half * 512:(half + 1) * 512], PS[b][:NSEQ, :], [("ps", b)], ["STG"])
            dma("sp", ohs_d, STG[:NSEQ, :], ["STG"], ["ohs"])
            for half in range(2):
                b = next_bank()
                transposes([(PS[b][:48, q * 128:(q + 1) * 128], X3[:, half * 4 + q, :], IDF[:, :]) for q in range(4)],
                           ["X3", "IDF"], [("ps", b)])
                copy("act", STG2[:48, half * 512:(half + 1) * 512], PS[b][:48, :], [("ps", b)], ["STG2"])
            dma("sp", ocas_d, STG2[:48, :], ["STG2"], ["ocas"])
            for half in range(2):
                b = next_bank()
                transposes([(PS[b][:3, q * 128:(q + 1) * 128], XAP3[:, half * 4 + q, :], IDF[:, :]) for q in range(4)],
                           [("xap3", h) for h in range(8)] + ["IDF"], [("ps", b)])
                copy("act", STG[:3, half * 512:(half + 1) * 512], PS[b][:3, :], [("ps", b)], ["STG"])
            dma("sp", ocap_d, STG[:3, :], ["STG"], ["ocap"])
            b = next_bank()
            transposes([(PS[b][:8, 0:128], HFO[:, :], IDF[:, :])], [("hfo", h) for h in range(8)] + ["IDF"], [("ps", b)])
            copy("act", STG2[:8, 0:128], PS[b][:8, 0:128], [("ps", b)], ["STG2"])
            dma("sp", ohp_d, STG2[:8, 0:128], ["STG2"], ["ohp"])
            S.nofence_default = False

        def a2_load(h):
            sl = h % NAB
            dma("sp", ABL[sl][:, 0:NP], a_sc[h, :, :], [("asc", h)], [("abla", sl)])
            dma("sp", ABL[sl][:, NP:2 * NP], b_sc[h, :, :], [("bsc", h)], [("ablb", sl)])

        def a2_t1(h):
            sl = h % 2
            al = h % NAB
            scan(HP[sl][:, :], ABL[al][:, 0:NP], ABL[al][:, NP:2 * NP], HIN[:, h:h + 1],
                 [("abla", al), ("ablb", al), "HIN"], [("hp", sl)])
            mk = [("mix", h, i) for i in range(3)]
            tt("dve", MIX[:, h, 0:NP], HP[sl][:, :], MIX[:, h, 0:NP], ALU.mult, [("hp", sl)] + mk, [("mixp", h)])
            tt("dve", MIX[:, h, NP:NTOK], HS[:, h, :], MIX[:, h, NP:NTOK], ALU.mult, [("hs", h)] + mk, [("mixs", h)])
            copy("dve", HFO[:, h:h + 1], HP[sl][:, NP - 1:NP], [("hp", sl)], [("hfo", h)])

        a2_load(0)
        a2_load(1)
        a2_load(2)
        a_out_gather()
        hk = [("hfin", h) for h in range(8)]
        dma("pool", cc_in[:, :], HFIN[:, :], hk, ["cc_in"])
        S.op("pool", lambda e: e.collective_compute("AllGather", ALU.bypass,
                                                    replica_groups=[[0, 1], [2, 3], [4, 5], [6, 7]],
                                                    ins=[cc_in.ap().opt()], outs=[cc_out.ap().opt()]),
             ["cc_in"], ["cc_out"], cc=True)
        dma("sp", HINR[:, :], cc_out[0:128, :], ["cc_out"], ["HINR"])
        ts("dve", HIN[:, :], HINR[:, :], MASK[:, 0:1], None, ALU.mult, None, ["HINR", "MASK"], ["HIN"])

        for h in range(8):
            if h + 3 < 8:
                a2_t0(h + 3)
            a2_t1(h)
            if h + 3 < 8:
                a2_load(h + 3)

        KD = 14
        NPT = 31 - KD
        CV.reset(EARLY_B)
        CB = CV.bf16(8 * NTOK).rearrange("p (j t) -> p j t", j=8)
        ZZ = [CV.f32(NTOK) for _ in range(2)]
        MEAN = CV.f32(NTOK)
        RSTD = CV.f32(NTOK)
        ACC = CV.f32(NTOK)
        SQ = [CV.bf16(512) for _ in range(2)]
        assert CV.off >= EARLY + 3 * 2 * NP * 4 + 2 * NP * 4, CV.off
        S.late_keys([("cb", j, ci) for j in range(8) for ci in range(3)] + [("zz", 0), ("zz", 1)] +
                    [("m2", i) for i in range(3)] + [("mean", i) for i in range(3)] +
                    [("rstd", i) for i in range(3)] + [("sq", i) for i in range(2)] + ["ACC", "ACCs"])
        SG = [CV.f32(NT) for _ in range(2)]
        UU = [CV.f32(NT) for _ in range(2)]
        UBP = [CV.bf16(HALO + NP) for _ in range(2)]
        DG = [CV.bf16(NPT * 128).rearrange("p (k c) -> p k c", k=NPT) for _ in range(2)]
        assert CV.off <= WO_END, (CV.off, WO_END)

        def b1_u0(j):
            sl = j % 2
            def dg_fn(e):
                ins = None
                for k in range(KD, 31):
                    ins = e.activation(out=DG[sl][:, k - KD, :], in_=IDB[:, :], func=AF.Identity, scale=PB[:, j, k:k + 1])
                return ins
            S.op("act", dg_fn, ["IDB", "PB"], [("dg", sl)])

            def cons_g(ci, b, c0, c1):
                act(SG[sl][:, c0:c1], PS[b][:, 0:c1 - c0], AF.Sigmoid, [("ps", b)], [("sg", sl, ci)])
            inproj(24 + j, HCH, cons_g)

            def cons_v(ci, b, c0, c1):
                copy("act", UU[sl][:, c0:c1], PS[b][:, 0:c1 - c0], [("ps", b)], [("uu", sl, ci)])
            inproj(16 + j, HCH, cons_v)
            for ci, (c0, c1) in enumerate(HCH):
                tt("dve", UU[sl][:, c0:c1], UU[sl][:, c0:c1], SG[sl][:, c0:c1], ALU.mult,
                   [("uu", sl, ci), ("sg", sl, ci)], [("uu", sl, ci)])
            uk = [("uu", sl, 0), ("uu", sl, 1), ("uu", sl, 2)]
            ceng = "dve"
            copy(ceng, UBP[sl][:, :], UU[sl][:, 0:HALO + NP], uk, [("ubp", sl)])
            copy(ceng, UBS[:, j, :, 30:38], UU[sl][:, HALO + NP:NT].rearrange("p (s t) -> p s t", s=NSEQ), uk,
                 [("ubsn", j)])

        def b2_proj(j):
            def cons(ci, b, c0, c1):
                act(MIX[:, 8 + j, c0 - HALO:c1 - HALO], PS[b][:, 0:c1 - c0], AF.Silu, [("ps", b)], [("mixb", j, ci)])
            inproj(32 + j, NCHK, cons)

        def b1_taps(j):
            sl = j % 2
            uk = [("uu", sl, 0), ("uu", sl, 1), ("uu", sl, 2)]
            ubsk = [("ubsn", j)] + [("ubsh", rt, j // 4) for rt in range(4)]
            accs = ACC[:, NP:NTOK].rearrange("p (s t) -> p s t", s=NSEQ)
            if j == 0:
                ts("dve", ACC[:, 0:NP], UU[sl][:, 2:2 + NP], PB[:, j, 0:1], None, ALU.mult, None, uk + ["PB"], ["ACC"])
                ts("dve", accs, UBS[:, j, :, 0:8], PB[:, j, 0:1], None, ALU.mult, None, ubsk + ["PB"], ["ACCs"])
            else:
                act(ACC[:, 0:NP], UU[sl][:, 2:2 + NP], AF.Identity, uk + ["PB"], ["ACC"], scale=PB[:, j, 0:1])
                act(accs, UBS[:, j, :, 0:8], AF.Identity, ubsk + ["PB"], ["ACCs"], scale=PB[:, j, 0:1])
            for k in range(1, KD):
                stt(ACC[:, 0:NP], UU[sl][:, 2 + k:2 + k + NP], PB[:, j, k:k + 1], ACC[:, 0:NP], ALU.mult, ALU.add,
                    uk + ["PB", "ACC"], ["ACC"])
                stt(accs, UBS[:, j, :, k:k + 8], PB[:, j, k:k + 1], accs, ALU.mult, ALU.add,
                    ubsk + ["PB", "ACCs"], ["ACCs"])

        def b1_u1(j):
            sl = j % 2
            uk = [("uu", sl, 0), ("uu", sl, 1), ("uu", sl, 2)]
            b = next_bank()
            transposes([(PS[b][:30, 0:128], UU[sl][:, HALO + NP - 30:HALO + NP], IDF[:, :]),
                        (PS[b][:, 128:256], UU[sl][:, HALO + NP:NT], IDF[:, :])], uk + ["IDF"], [("ps", b)])
            copy("act", STG[:30, j * 128:(j + 1) * 128], PS[b][:30, 0:128], [("ps", b)], ["STG"])
            copy("act", STG2[:, j * 128:(j + 1) * 128], PS[b][:, 128:256], [("ps", b)], ["STG2"])
            for ci, (c0, c1) in enumerate([(0, 512), (512, 1024)]):
                b = next_bank()
                pairs = [(DG[sl][:, k - KD, :], UBP[sl][:, c0 + k + 2:c0 + k + 2 + 512]) for k in range(KD, 31)]
                mm_group(PS[b][:, :], pairs, [("dg", sl), ("ubp", sl)], [("ps", b)])
                stt(CB[:, j, c0:c1], PS[b][:, :], PB[:, j, 31:32], ACC[:, c0:c1], ALU.add, ALU.add,
                    [("ps", b), "PB", "ACC"], [("cb", j, ci)])
            b = next_bank()
            pairs = [(DG[sl][:, k - KD, :], UBS[:, j, :, k:k + 8]) for k in range(KD, 31)]
            mm_group(PS[b][:, 0:NS].rearrange("p (s t) -> p s t", s=NSEQ), pairs,
                     [("dg", sl), ("ubsn", j)] + [("ubsh", rt, j // 4) for rt in range(4)], [("ps", b)])
            stt(CB[:, j, NP:NTOK], PS[b][:, 0:NS], PB[:, j, 31:32], ACC[:, NP:NTOK], ALU.add, ALU.add,
                [("ps", b), "PB", "ACCs"], [("cb", j, 2)])

        b1_u0(0)
        for j in range(8):
            b1_taps(j)
            if j + 1 < 8:
                b1_u0(j + 1)
            if j == 0:
                b2_proj(0)
                b2_proj(1)
                a_outputs()
            if j == 2:
                dma("sp", ocbh_d, stcb_d.rearrange("(s r) c -> s r c", r=30)[:, 8:30, :], [], ["ocbh"], nofence=True)
            if 2 <= j <= 5:
                n_ = j - 2
                dma("pool", wo_bf[n_, :, :].rearrange("p (k j) -> p k j", k=16),
                    wout_d[n_].rearrange("p (k j) -> p k j", k=16), [], [("wobf", n_)], nofence=True)
            if j == 7:
                b2_proj(2)
                b2_proj(3)
            b1_u1(j)
        S.nofence_default = True
        dma("sp", ocbp_d, STG[:30, :], ["STG"], ["ocbp"])
        dma("sp", ocbn_d, STG2[:, :], ["STG2"], ["ocbn"])
        S.nofence_default = False

        SCH = [(0, 512), (512, 1024), (1024, NTOK)]
        sbanks = {}
        for ci, (c0, c1) in enumerate(SCH):
            n = c1 - c0
            b1_ = next_bank()
            b2_ = next_bank()
            sbanks[ci] = (b1_, b2_)
            for j in range(8):
                sq = j % 2
                if j % 2 == 0:
                    act(SQ[sq][:, 0:n], CB[:, j, c0:c1], AF.Square, [("cb", j, ci)], [("sq", sq)])
                else:
                    tt("dve", SQ[sq][:, 0:n], CB[:, j, c0:c1], CB[:, j, c0:c1], ALU.mult, [("cb", j, ci)], [("sq", sq)])
                S.op("pe", (lambda e, j=j, b=b1_, c0=c0, c1=c1, n=n:
                            e.matmul(PS[b][:, 0:n], ONES[:, :], CB[:, j, c0:c1], start=(j == 0), stop=(j == 7))),
                     ["ONES", ("cb", j, ci)], [("ps", b1_)] if j == 0 else [("psacc", b1_, j)])
                S.op("pe", (lambda e, j=j, b=b2_, sq=sq, n=n:
                            e.matmul(PS[b][:, 0:n], ONES[:, :], SQ[sq][:, 0:n], start=(j == 0), stop=(j == 7))),
                     ["ONES", ("sq", sq)], [("ps", b2_)] if j == 0 else [("psacc", b2_, j)])
        mk = [("mean", ci) for ci in range(3)]
        rk_ = [("rstd", ci) for ci in range(3)]
        m2k = [("m2", ci) for ci in range(3)]
        for ci, (c0, c1) in enumerate(SCH):
            b1_, _b2 = sbanks[ci]
            k1 = [("ps", b1_)] + [("psacc", b1_, j) for j in range(1, 8)]
            act(MEAN[:, c0:c1], PS[b1_][:, 0:c1 - c0], AF.Identity, k1, [("mean", ci)], scale=1.0 / WA)
        tt("dve", ZZ[0][:, :], MEAN[:, :], MEAN[:, :], ALU.mult, mk, m2k)
        for ci, (c0, c1) in enumerate(SCH):
            _b1, b2_ = sbanks[ci]
            k2 = [("ps", b2_)] + [("psacc", b2_, j) for j in range(1, 8)]
            stt(RSTD[:, c0:c1], PS[b2_][:, 0:c1 - c0], 1.0 / WA, ZZ[0][:, c0:c1], ALU.mult, ALU.subtract,
                k2 + m2k, [("rstd", ci)])
        act(RSTD[:, :], RSTD[:, :], AF.Ln, rk_ + ["EPSC"], rk_, bias=EPSC[:, :])
        act(RSTD[:, :], RSTD[:, :], AF.Exp, rk_, rk_, scale=-0.5)
        stt(MEAN[:, :], MEAN[:, :], -1.0, RSTD[:, :], ALU.mult, ALU.mult, mk + rk_, mk)
        stat_k = [("rstd", i) for i in range(3)] + [("mean", i) for i in range(3)]

        def b2_rest_a(j):
            sl = j % 2
            cbk = [("cb", j, i) for i in range(3)]
            zk = [("zz", sl)] + ([("m2", i) for i in range(3)] if sl == 0 else [])
            tt("dve", ZZ[sl][:, :], CB[:, j, :], RSTD[:, :], ALU.mult, cbk + stat_k, zk)
            tt("dve", ZZ[sl][:, :], ZZ[sl][:, :], MEAN[:, :], ALU.add, zk + stat_k, zk)
            act(ZZ[sl][:, :], ZZ[sl][:, :], AF.Silu, zk + ["PB"], zk, bias=PB[:, j, 33:34], scale=PB[:, j, 32:33])

        def b2_rest_b(j):
            sl = j % 2
            zk = [("zz", sl)] + ([("m2", i) for i in range(3)] if sl == 0 else [])
            tt("dve", MIX[:, 8 + j, :], ZZ[sl][:, :], MIX[:, 8 + j, :], ALU.mult,
               zk + [("mixb", j, i) for i in range(3)], [("mixB", j)])

        def wo_view(off):
            return MAINF[:, off // 4: off // 4 + 16 * 512 // 2].bitcast(BF16).rearrange("p (k n) -> p k n", k=16)

        WO = [wo_view(WO_END), None, None]
        WSL = {0: 0, 1: 1, 2: 2, 3: 0}

        def wo_load(n):
            src = wo_bf[n, :, :].rearrange("p (k j) -> p k j", k=16)
            for hf in range(2):
                dma("pool", WO[WSL[n]][:, hf * 8:(hf + 1) * 8, :], src[:, hf * 8:(hf + 1) * 8, :], [("wobf", n)],
                    [("wo", WSL[n], hf)])

        wo_load(0)
        b2_rest_a(0)
        for j in range(8):
            if j + 1 < 8:
                b2_rest_a(j + 1)
            b2_rest_b(j)
            if j + 4 < 8:
                b2_proj(j + 4)

        S.fence()
        CV.reset(0)
        HRES = CV.f32(9 * D).rearrange("p (i d) -> p i d", i=9)
        WO[1] = CV.bf16(16 * 512).rearrange("p (k n) -> p k n", k=16)
        WO[2] = CV.bf16(16 * 512).rearrange("p (k n) -> p k n", k=16)
        NXRE = 6
        XRE = [CV.f32(512) for _ in range(NXRE)]
        SQJ = CV.bf16(512)
        FGB = CV.f32(D)
        assert CV.off <= WO_END, (CV.off, WO_END)
        wo_load(1)
        allmix = []
        for h in range(8):
            allmix += [("mixp", h), ("mixs", h)]
        for j in range(8):
            allmix += [("mixB", j)]
        xi = 0
        amix = []
        for h in range(8):
            amix += [("mixp", h), ("mixs", h)]
        bmix = [("mixB", j) for j in range(8)]

        def mm_part(out, pairs, first, last, reads, writes):
            def fn(e):
                ins = None
                n_ = len(pairs)
                for q, (l, r) in enumerate(pairs):
                    ins = e.matmul(out, l, r, start=(first and q == 0), stop=(last and q == n_ - 1))
                return ins
            return S.op("pe", fn, reads, writes)

        pend = [None]

        def fin_a(i):
            S.op("dve", lambda e: e.tensor_reduce(out=SSQ1[:, i:i + 1], in_=SSQ[:, i, :],
                                                  axis=mybir.AxisListType.X, op=ALU.add),
                 [("ssq", i, q) for q in range(4)], [("ssq1", i)])
            act(LN2[:, i:i + 1], SSQ1[:, i:i + 1], AF.Ln, [("ssq1", i), "EPSC"], [("ln2", i)],
                bias=EPSC[:, :], scale=1.0 / D)
            act(RS2[:, i:i + 1], LN2[:, i:i + 1], AF.Exp, [("ln2", i)], [("rs2", i)], scale=-0.5)

        def fin_b(i):
            hk = [("hres", i, q) for q in range(4)]
            if i < 8:
                stt(HRES[:, i, :], HRES[:, i, :], RS2[:, i:i + 1], FGB[:, :], ALU.mult, ALU.mult,
                    hk + [("rs2", i), "FGB"], hk)
                dma("pool", y_d[i * 128:(i + 1) * 128, :], HRES[:, i, :], hk, [("y", i)])
            else:
                for hf in range(2):
                    cs = slice(hf * 1024, (hf + 1) * 1024)
                    hk2 = [("hres", i, 2 * hf), ("hres", i, 2 * hf + 1)]
                    stt(HRES[:, i, cs], HRES[:, i, cs], RS2[:, i:i + 1], FGB[:, cs], ALU.mult, ALU.mult,
                        hk2 + [("rs2", i), "FGB"], hk2)
                    dma("pool", y_d[i * 128:(i + 1) * 128, cs], HRES[:, i, cs], hk2, [("y", i, hf)])

        G = 4
        ph1 = [(n, i) for n in range(2) for i in range(9)]
        ph2 = [(n, i) for i in range(9) for n in (2, 3)]
        G0 = 7
        batches = [ph1[0:G0]] + [ph1[g0:g0 + G] for g0 in range(G0, len(ph1), G)] + \
                  [ph2[g0:g0 + G] for g0 in range(0, len(ph2), G)]
        for batch in batches:
            banks = []
            for (n, i) in batch:
                b = next_bank()
                banks.append(b)
                pairs = [(MIX[:, kc, i * 128:(i + 1) * 128], WO[WSL[n]][:, kc, :]) for kc in range(8)]
                mm_part(PS[b][:, :], pairs, True, False, amix + [("wo", WSL[n], 0)], [("ps", b)])
            for (n, i), b in zip(batch, banks):
                pairs = [(MIX[:, kc, i * 128:(i + 1) * 128], WO[WSL[n]][:, kc, :]) for kc in range(8, 16)]
                mm_part(PS[b][:, :], pairs, False, True, bmix + [("wo", WSL[n], 1)], [("ps2", b)])
                xs_ = xi % NXRE
                xi += 1
                r0 = HALO + i * 128
                dma("sp", XRE[xs_][:, :], x_d[r0:r0 + 128, n * 512:(n + 1) * 512], [], [("xre", xs_)])
                tt("dve", HRES[:, i, n * 512:(n + 1) * 512], PS[b][:, :], XRE[xs_][:, :], ALU.add,
                   [("ps", b), ("ps2", b), ("xre", xs_)], [("hres", i, n)])
                act(SQJ[:, :], HRES[:, i, n * 512:(n + 1) * 512], AF.Square, [("hres", i, n)], ["SQJ", ("ssq", i, n)],
                    accum_out=SSQ[:, i, n:n + 1])
                if n == 3:
                    if pend[0] is not None:
                        fin_b(pend[0])
                    fin_a(i)
                    pend[0] = i
                if i == 4 and n == 0:
                    wo_load(2)
                if i == 0 and n == 1:
                    dma("sp", FGB[:, :], fg_d.partition_broadcast(128), [], ["FGB"])
                if i == 8 and n == 0:
                    wo_load(3)

        fin_b(pend[0])

        S.finalize()
        with nc.Block(no_gpsimd_drain=True) as block:
            @block.tensor
            def _(e):
                S.emit("pe", e, sems, lane_sems, cc_sem)

            @block.scalar
            def _(e):
                S.emit("act", e, sems, lane_sems, cc_sem)

            @block.vector
            def _(e):
                S.emit("dve", e, sems, lane_sems, cc_sem)

            @block.gpsimd
            def _(e):
                S.emit("pool", e, sems, lane_sems, cc_sem)

            @block.sync
            def _(e):
                S.emit("sp", e, sems, lane_sems, cc_sem, final_wait=True)
    return nc


_NC_CACHE = {}


def kernel(x_prompt, x_sample, state_lru_h, state_lru_conv, state_glu_conv,
           norm_gain, w_in, conv_a_w, conv_a_b, gate_a_w, gate_a_b, gate_x_w, gate_x_b,
           lru_param, conv_b_w, conv_b_b, ln_b_gain, ln_b_bias, w_out, final_gain):
    f = np.float32
    x_prompt = np.asarray(x_prompt, f)
    x_sample = np.asarray(x_sample, f)
    sth = np.asarray(state_lru_h, f)[0]
    stca = np.asarray(state_lru_conv, f)[0]
    stcb = np.asarray(state_glu_conv, f)[0]
    w_in = np.asarray(w_in, f)[0]
    w_out = np.asarray(w_out, f)[0]

    win_r = np.ascontiguousarray(w_in.reshape(16, 128, 40, 128).transpose(2, 1, 0, 3).reshape(40, 128, 2048))
    wout_r = np.ascontiguousarray(w_out.reshape(16, 128, 4, 512).transpose(2, 1, 0, 3).reshape(4, 128, 16 * 512))
    gws = np.stack([np.asarray(gate_a_w, f)[0], np.asarray(gate_x_w, f)[0]])
    gw_r = np.ascontiguousarray(gws.transpose(2, 0, 1, 3).reshape(128, 16 * 128))

    def chan(v):
        return np.asarray(v, f).reshape(8, 128).T

    pa = np.zeros((128, 8, 8), f)
    caw = np.asarray(conv_a_w, f)[0]
    for k in range(4):
        pa[:, :, k] = chan(caw[k])
    pa[:, :, 4] = chan(np.asarray(conv_a_b, f)[0])
    pa[:, :, 5] = chan(np.asarray(gate_a_b, f)[0])
    pa[:, :, 6] = chan(np.asarray(gate_x_b, f)[0])
    pa[:, :, 7] = chan(np.asarray(lru_param, f)[0])
    pb = np.zeros((128, 8, 35), f)
    cbw = np.asarray(conv_b_w, f)[0]
    for k in range(31):
        pb[:, :, k] = chan(cbw[k])
    pb[:, :, 31] = chan(np.asarray(conv_b_b, f)[0])
    pb[:, :, 32] = chan(np.asarray(ln_b_gain, f)[0])
    pb[:, :, 33] = chan(np.asarray(ln_b_bias, f)[0])
    ng = np.ascontiguousarray(np.asarray(norm_gain, f)[0])
    fg = np.ascontiguousarray(np.asarray(final_gain, f))
    ident = np.eye(128, dtype=f)

    in_maps = []
    for c in range(NCORES):
        q, half = c // 2, c % 2
        xl = np.zeros((NT, D), f)
        if half == 1:
            xl[0:HALO] = x_prompt[q, NP - HALO:NP]
        xl[HALO:HALO + NP] = x_prompt[q, half * NP:(half + 1) * NP]
        xl[HALO + NP:] = x_sample[c * NSEQ:(c + 1) * NSEQ].reshape(NS, D)
        in_maps.append({
            "x": xl,
            "sth": np.ascontiguousarray(sth[c * NSEQ:(c + 1) * NSEQ]),
            "stca": np.ascontiguousarray(stca[c * NSEQ:(c + 1) * NSEQ].reshape(NSEQ * 3, WA)),
            "stcb": np.ascontiguousarray(stcb[c * NSEQ:(c + 1) * NSEQ].reshape(NSEQ * 30, WA)),
            "win": win_r, "wout": wout_r, "gw": gw_r,
            "pa": pa.reshape(128, 64), "pb": pb.reshape(128, 280),
            "ng": ng, "fg": fg, "ident": ident,
            "mask": np.full((128, 1), float(half), f),
        })

    if "nc" not in _NC_CACHE:
        _NC_CACHE["nc"] = build_nc()
    nc = _NC_CACHE["nc"]
    res = run_bass_kernel_spmd(nc, in_maps, core_ids=list(range(NCORES)))
    R = res.results

    y_prompt = np.zeros((4, 2048, D), f)
    y_sample = np.zeros((128, TS, D), f)
    o_hp = np.zeros((1, 4, WA), f)
    o_cap = np.zeros((1, 4, 3, WA), f)
    o_cbp = np.zeros((1, 4, 30, WA), f)
    o_hs = np.zeros((1, 128, WA), f)
    o_cas = np.zeros((1, 128, 3, WA), f)
    o_cbs = np.zeros((1, 128, 30, WA), f)
    for c in range(NCORES):
        q, half = c // 2, c % 2
        r = R[c]
        y_prompt[q, half * NP:(half + 1) * NP] = r["y"][0:NP]
        y_sample[c * NSEQ:(c + 1) * NSEQ] = r["y"][NP:].reshape(NSEQ, TS, D)
        if half == 1:
            o_hp[0, q] = r["ohp"].reshape(WA)
            o_cap[0, q] = r["ocap"]
            o_cbp[0, q] = r["ocbp"]
        o_hs[0, c * NSEQ:(c + 1) * NSEQ] = r["ohs"]
        o_cas[0, c * NSEQ:(c + 1) * NSEQ] = r["ocas"].reshape(NSEQ, 3, WA)
        o_cbs[0, c * NSEQ:(c + 1) * NSEQ, 0:22] = r["ocbh"]
        o_cbs[0, c * NSEQ:(c + 1) * NSEQ, 22:30] = r["ocbn"].reshape(NSEQ, TS, WA)
    return (y_prompt, y_sample, o_hp, o_cap, o_cbp, o_hs, o_cas, o_cbs)
```

```python
import contextlib
import numpy as np
import concourse.bass as bass
import concourse.mybir as mybir
from concourse.bass_utils import run_bass_kernel_spmd

F32 = mybir.dt.float32
BF16 = mybir.dt.bfloat16
AF = mybir.ActivationFunctionType
ALU = mybir.AluOpType

NCORES = 8
D = 2048
WA = 1024
NH = 8
CONV_A = 4
CONV_B = 31
HALO = 32
NP = 1024
NS = 128
NSEQ = 16
TS = 8
NT = HALO + NP + NS
NTOK = NP + NS
EPS = 1e-6
NWS = 4
NLANES = 30
NL_SP = 20

HCH = [(0, 512), (512, 1024), (1024, NT)]
NCHK = [(HALO, HALO + 512), (HALO + 512, HALO + 1024), (HALO + 1024, NT)]


class Sched:
    def __init__(self):
        self.ops = []
        self.res = {}
        self.lane_last = [None] * NLANES
        self.lane_count = [0] * NLANES
        self.next_lane = {"sp": 0, "pool": NL_SP}
        self.cur_fence = []
        self.phase_keys = set()
        self.nofence_default = False

    def _deps(self, reads, writes, nofence=False):
        deps = set()
        for k in list(reads) + list(writes):
            if not nofence:
                self.phase_keys.add(k)
            if k not in self.res and not nofence:
                deps.update(self.cur_fence)
        for k in reads:
            st = self.res.get(k)
            if st is not None and st[0] is not None:
                deps.add(st[0])
        for k in writes:
            st = self.res.get(k)
            if st is not None:
                if st[0] is not None:
                    deps.add(st[0])
                deps.update(st[1])
        return deps

    def op(self, eng, fn, reads=(), writes=(), ndma=0, cc=False, after=(), nofence=None):
        reads = list(reads)
        writes = list(writes)
        if nofence is None:
            nofence = self.nofence_default
        deps = self._deps(reads, writes, nofence)
        deps.update(after)
        idx = len(self.ops)
        o = dict(eng=eng, fn=fn, deps=deps, ndma=ndma, cc=cc, lane=None, sig=False, val=None)
        if ndma:
            lane = self.next_lane[eng]
            lo, hi = (0, NL_SP) if eng == "sp" else (NL_SP, NLANES)
            self.next_lane[eng] = lo + (lane + 1 - lo) % (hi - lo)
            if self.lane_last[lane] is not None:
                deps.add(self.lane_last[lane])
            self.lane_last[lane] = idx
            self.lane_count[lane] += ndma
            o["lane"] = lane
            o["val"] = 16 * self.lane_count[lane]
        self.ops.append(o)
        for k in reads:
            st = self.res.setdefault(k, [None, []])
            st[1].append(idx)
        for k in writes:
            self.res[k] = [idx, []]
        return idx

    def fence(self):
        deps = set(self.cur_fence)
        for k in self.phase_keys:
            st = self.res.get(k)
            if st is not None:
                if st[0] is not None:
                    deps.add(st[0])
                deps.update(st[1])
        self.cur_fence = sorted(deps)
        self.phase_keys = set()

    def late_keys(self, keys):
        deps = set(self.cur_fence)
        for k in self.phase_keys:
            st = self.res.get(k)
            if st is not None:
                if st[0] is not None:
                    deps.add(st[0])
                deps.update(st[1])
        deps = sorted(deps)
        for k in keys:
            assert k not in self.res, k
            self.res[k] = [None, list(deps)]

    def finalize(self):
        for o in self.ops:
            for d in o["deps"]:
                self.ops[d]["sig"] = True
        cnt = {}
        for o in self.ops:
            if o["ndma"] or o["cc"]:
                continue
            if o["sig"]:
                cnt[o["eng"]] = cnt.get(o["eng"], 0) + 1
                o["val"] = cnt[o["eng"]]

    def emit(self, eng_name, eng, sems, lane_sems, cc_sem, final_wait=False):
        seen = {}
        for o in self.ops:
            if o["eng"] != eng_name:
                continue
            need = {}
            for d in o["deps"]:
                p = self.ops[d]
                if p["cc"]:
                    key, sem, val = "cc", cc_sem, 1
                elif p["ndma"]:
                    key, sem, val = ("l", p["lane"]), lane_sems[p["lane"]], p["val"]
                else:
                    if p["eng"] == "pe" and eng_name == "pe":
                        continue
                    key, sem, val = p["eng"], sems[p["eng"]], p["val"]
                if val > need.get(key, (None, 0))[1]:
                    need[key] = (sem, val)
            for key, (sem, val) in need.items():
                if val > seen.get(key, 0):
                    eng.wait_ge(sem, val)
                    seen[key] = val
            r = o["fn"](eng)
            if o["cc"]:
                r.then_inc(cc_sem, 1)
            elif o["ndma"]:
                rl = r if isinstance(r, (list, tuple)) else [r]
                assert len(rl) == o["ndma"], (len(rl), o["ndma"])
                for ins in rl:
                    ins.then_inc(lane_sems[o["lane"]], 16)
            elif o["sig"]:
                r.then_inc(sems[o["eng"]], 1)
        if final_wait:
            for lane in range(NLANES):
                if self.lane_count[lane]:
                    eng.wait_ge(lane_sems[lane], 16 * self.lane_count[lane])


def build_nc():
    nc = bass.Bass("TRN2", target_bir_lowering=False)
    S = Sched()

    def din(name, shape):
        return nc.dram_tensor(name, list(shape), F32, kind="ExternalInput").ap()

    def dout(name, shape):
        return nc.dram_tensor(name, list(shape), F32, kind="ExternalOutput").ap()

    x_d = din("x", [NT, D])
    sth_d = din("sth", [NSEQ, WA])
    stca_d = din("stca", [NSEQ * 3, WA])
    stcb_d = din("stcb", [NSEQ * 30, WA])
    win_d = din("win", [40, 128, D])
    wout_d = din("wout", [4, 128, 16 * 512])
    gw_d = din("gw", [128, 16 * 128])
    pa_d = din("pa", [128, 64])
    pb_d = din("pb", [128, 8 * 35])
    ng_d = din("ng", [D])
    fg_d = din("fg", [D])
    ident_d = din("ident", [128, 128])
    mask_d = din("mask", [128, 1])

    y_d = dout("y", [NTOK, D])
    ohp_d = dout("ohp", [8, 128])
    ocap_d = dout("ocap", [3, WA])
    ocbp_d = dout("ocbp", [30, WA])
    ohs_d = dout("ohs", [NSEQ, WA])
    ocas_d = dout("ocas", [NSEQ * 3, WA])
    ocbh_d = dout("ocbh", [NSEQ, 22, WA])
    ocbn_d = dout("ocbn", [NS, WA])

    a_sc = nc.dram_tensor("a_scr", [8, 128, NP], F32)
    b_sc = nc.dram_tensor("b_scr", [8, 128, NP], F32)
    wo_bf = nc.dram_tensor("wo_bf", [4, 128, 16 * 512], BF16)
    cc_in = nc.dram_tensor("cc_in", [128, 8], F32)
    cc_out = nc.dram_tensor("cc_out", [256, 8], F32)

    es = contextlib.ExitStack()
    with es:
        def sb(name, shape, dt):
            return es.enter_context(nc.sbuf_tensor(name, list(shape), dt))

        PA = sb("PA", [128, 8, 8], F32)
        PB = sb("PB", [128, 8, 35], F32)
        GW = sb("GW", [128, 16, 128], BF16)
        IDF = sb("IDF", [128, 128], F32)
        IDB = sb("IDB", [128, 128], BF16)
        ONES = sb("ONES", [128, 128], BF16)
        MASK = sb("MASK", [128, 1], F32)
        EPSC = sb("EPSC", [128, 1], F32)
        ONE1 = sb("ONE1", [1, 128], F32)
        CL = sb("CL", [128, 8], F32)
        CLT = sb("CLT", [128, 8], F32)
        SS = sb("SS", [128, 16], F32)
        LNT = sb("LNT", [128, 16], F32)
        RS = sb("RS", [128, 16], F32)
        H0S = sb("H0S", [128, 8, NSEQ], F32)
        HSL = sb("HSL", [128, 8, NSEQ], F32)
        X3 = sb("X3", [128, 8, NSEQ * 3], F32)
        HFIN = sb("HFIN", [128, 8], F32)
        HINR = sb("HINR", [128, 8], F32)
        HIN = sb("HIN", [128, 8], F32)
        HFO = sb("HFO", [128, 8], F32)
        XAP3 = sb("XAP3", [128, 8, 3], F32)
        TMP16 = sb("TMP16", [128, NSEQ], F32)
        MIX = sb("MIX", [128, 16, NTOK], BF16)
        STG = sb("STG", [128, 1024], F32)
        STG2 = sb("STG2", [128, 1024], F32)
        SSQ = sb("SSQ", [128, 9, 4], F32)
        SSQ1 = sb("SSQ1", [128, 9], F32)
        LN2 = sb("LN2", [128, 9], F32)
        RS2 = sb("RS2", [128, 9], F32)

        main_bytes = (nc.sbuf_bytes_remaining - 1024) // 64 * 64
        MAINF = sb("MAIN", [128, main_bytes // 4], F32)

        class Carver:
            def __init__(self):
                self.off = 0

            def reset(self, base=0):
                self.off = base

            def f32(self, n):
                a = MAINF[:, self.off // 4: self.off // 4 + n]
                self.off += 4 * n
                assert self.off <= main_bytes, (self.off, main_bytes)
                return a

            def bf16(self, n):
                n2 = (n + 1) // 2
                a = MAINF[:, self.off // 4: self.off // 4 + n2].bitcast(BF16)
                self.off += 4 * n2
                assert self.off <= main_bytes, (self.off, main_bytes)
                return a[:, 0:n]

        CV = Carver()
        XNT = CV.bf16(16 * NT).rearrange("p (k t) -> p k t", k=16)
        WS = CV.bf16(NWS * 16 * 128).rearrange("p (s k j) -> p s k j", s=NWS, k=16)
        UBS = CV.bf16(8 * NSEQ * 38).rearrange("p (j s r) -> p j s r", j=8, s=NSEQ)
        EARLY_B = CV.off
        XS = CV.f32(8 * NSEQ * 11).rearrange("p (h s r) -> p h s r", h=8, s=NSEQ)
        HS = CV.f32(8 * NS).rearrange("p (h t) -> p h t", h=8)
        EARLY = CV.off
        WO_END = main_bytes - 16 * 512 * 2

        PS = [es.enter_context(nc.psum_tensor(f"ps{b}", [128, 512], F32)) for b in range(8)]
        sems = {e: es.enter_context(nc.semaphore(f"s_{e}")) for e in ("pe", "act", "dve", "pool")}
        lane_sems = [es.enter_context(nc.semaphore(f"lane{i}")) for i in range(NLANES)]
        cc_sem = es.enter_context(nc.semaphore("cc_sem"))

        bank_ctr = [0]

        def next_bank():
            b = bank_ctr[0] % 8
            bank_ctr[0] += 1
            return b

        def xkeys(c0, c1):
            ks = []
            for t in range(c0 // 128, (c1 - 1) // 128 + 1):
                ks += [("xnt", t, 0), ("xnt", t, 1)]
            return ks

        def dma(eng, out, in_, reads, writes, after=(), nofence=None, **kw):
            return S.op(eng, lambda e: e.dma_start(out=out, in_=in_, **kw), reads, writes, ndma=1, after=after,
                        nofence=nofence)

        def act(out, in_, func, reads, writes, bias=None, scale=None, accum_out=None):
            kw = {}
            if bias is not None:
                kw["bias"] = bias
            if scale is not None:
                kw["scale"] = scale
            if accum_out is not None:
                kw["accum_out"] = accum_out
            return S.op("act", lambda e: e.activation(out=out, in_=in_, func=func, **kw), reads, writes)

        def tt(eng, out, in0, in1, op, reads, writes):
            return S.op(eng, lambda e: e.tensor_tensor(out=out, in0=in0, in1=in1, op=op), reads, writes)

        def stt(out, in0, scalar, in1, op0, op1, reads, writes):
            return S.op("dve", lambda e: e.scalar_tensor_tensor(out=out, in0=in0, scalar=scalar, in1=in1,
                                                               op0=op0, op1=op1), reads, writes)

        def ts(eng, out, in0, s1, s2, op0, op1, reads, writes):
            if op1 is None:
                return S.op(eng, lambda e: e.tensor_scalar(out=out, in0=in0, scalar1=s1, scalar2=None, op0=op0),
                            reads, writes)
            return S.op(eng, lambda e: e.tensor_scalar(out=out, in0=in0, scalar1=s1, scalar2=s2, op0=op0, op1=op1),
                        reads, writes)

        def copy(eng, out, in_, reads, writes):
            if eng == "act":
                return S.op("act", lambda e: e.copy(out=out, in_=in_), reads, writes)
            return S.op(eng, lambda e: e.tensor_copy(out=out, in_=in_), reads, writes)

        def scan(out, a, b, initial, reads, writes):
            return S.op("dve", lambda e: e.tensor_tensor_scan(out=out, data0=a, data1=b, initial=initial,
                                                              op0=ALU.mult, op1=ALU.add), reads, writes)

        def mm_group(out, pairs, reads, writes):
            def fn(e):
                n = len(pairs)
                ins = None
                for i, (l, r) in enumerate(pairs):
                    ins = e.matmul(out, l, r, start=(i == 0), stop=(i == n - 1))
                return ins
            return S.op("pe", fn, reads, writes)

        def transposes(items, reads, writes):
            def fn(e):
                ins = None
                for (o, i, idn) in items:
                    ins = e.transpose(out=o, in_=i, identity=idn)
                return ins
            return S.op("pe", fn, reads, writes)

        wseq = list(range(0, 16)) + [24, 16, 25, 17, 32, 33]
        for j in range(2, 8):
            wseq += [24 + j, 16 + j]
        wseq += list(range(34, 40))
        wpos = {m: i for i, m in enumerate(wseq)}
        wloaded = [0]

        w_after = {}

        def w_prefetch(upto):
            while wloaded[0] <= min(upto, len(wseq) - 1):
                i = wloaded[0]
                m = wseq[i]
                s = i % NWS
                dma("pool", WS[:, s, :, :], win_d[m].rearrange("p (k j) -> p k j", k=16), [], [("ws", s)],
                    after=w_after.get(i, ()))
                wloaded[0] += 1

        inproj_done = {}

        def inproj(m, chunks, consume, cis=None):
            i = wpos[m]
            s = i % NWS
            w_prefetch(i)
            ndone = inproj_done.get(m, 0) + (len(chunks) if cis is None else len(cis))
            inproj_done[m] = ndone
            for ci, (c0, c1) in enumerate(chunks):
                if cis is not None and ci not in cis:
                    continue
                b = next_bank()
                pairs = [(WS[:, s, kc, :], XNT[:, kc, c0:c1]) for kc in range(16)]
                mm_group(PS[b][:, 0:c1 - c0], pairs, [("ws", s)] + xkeys(c0, c1), [("ps", b)])
                consume(ci, b, c0, c1)
            if ndone >= len(chunks):
                w_prefetch(i + NWS)

        CV.reset(EARLY)
        NXT = 5
        XT = [CV.f32(D) for _ in range(NXT)]
        XB = [CV.bf16(D) for _ in range(2)]
        STA = XT[2][:, 0:1024]
        STH = XT[3][:, 0:1024]
        GBC = CV.f32(D)

        JUNK = CV.bf16(D)
        NG1 = CV.f32(D)
        XA_OFF = CV.off
        XA = [CV.f32(HALO + NP) for _ in range(2)]
        ntile = (NT + 127) // 128
        xload = {}

        def p1_load(t):
            r0 = t * 128
            nr = min(128, NT - r0)
            xload[t] = dma("sp", XT[t % NXT][:nr, :], x_d[r0:r0 + nr, :], [], [("xt", t % NXT)])

        S.op("dve", lambda e: e.memset(EPSC[:, :], EPS), [], ["EPSC"])
        S.op("dve", lambda e: e.memset(ONES[:, :], 1.0), [], ["ONES"])
        dma("sp", NG1[0:1, :], ng_d.unsqueeze(0), [], ["NG1"])
        p1_load(0)
        dma("sp", IDF[:, :], ident_d, [], ["IDF"])
        S.op("dve", lambda e: e.memset(ONE1[:, :], 1.0), [], ["ONE1"])
        for q in range(4):
            b = next_bank()
            S.op("pe", (lambda e, q=q, b=b: e.matmul(PS[b][:, :], ONE1[0:1, :], NG1[0:1, q * 512:(q + 1) * 512],
                                                    start=True, stop=True)), ["ONE1", "NG1"], [("ps", b)])
            copy("dve", GBC[:, q * 512:(q + 1) * 512], PS[b][:, :], [("ps", b)], ["GBC"])
        copy("dve", IDB[:, :], IDF[:, :], ["IDF"], ["IDB"])
        p1_load(1)
        p1_load(2)
        p1_load(3)
        p1_load(4)
        dma("sp", PA[:, :, :], pa_d.rearrange("p (h k) -> p h k", h=8), [], ["PA"])
        dma("sp", PB[:, :, :], pb_d.rearrange("p (h k) -> p h k", h=8), [], ["PB"])
        dma("sp", MASK[:, :], mask_d, [], ["MASK"])

        def p1_sq(t):
            r0 = t * 128
            nr = min(128, NT - r0)
            xs_ = t % NXT
            act(JUNK[:nr, :], XT[xs_][:nr, :], AF.Square, [("xt", xs_)], ["JUNK", ("ss", t)],
                accum_out=SS[:nr, t:t + 1])
            act(LNT[:nr, t:t + 1], SS[:nr, t:t + 1], AF.Ln, [("ss", t), "EPSC"], [("lnt", t)],
                bias=EPSC[:nr, :], scale=1.0 / D)
            act(RS[:nr, t:t + 1], LNT[:nr, t:t + 1], AF.Exp, [("lnt", t)], [("rs", t)], scale=-0.5)

        def p1_stt(t):
            r0 = t * 128
            nr = min(128, NT - r0)
            xs_, bs_ = t % NXT, t % 2
            stt(XB[bs_][:nr, :], XT[xs_][:nr, :], RS[:nr, t:t + 1], GBC[:nr, :], ALU.mult, ALU.mult,
                [("xt", xs_), ("rs", t), "GBC"], [("xb", bs_)])
            if t + NXT < ntile:
                p1_load(t + NXT)
            if t == 4:
                dma("pool", GW[:, :, :], gw_d.rearrange("p (g j) -> p g j", g=16), [], ["GW"], after=[xload[ntile - 1]])
            if t == 1:
                w_after[0] = [xload[5]]
                w_prefetch(0)
            if t in (4, 5, 6):
                w_after[t - 3] = [xload[8 if t == 4 else ntile - 1]]
                w_prefetch(t - 3)

        def p1_tr(t):
            r0 = t * 128
            nr = min(128, NT - r0)
            bs_ = t % 2
            for half in range(2):
                b = next_bank()
                psb = PS[b][:, :].bitcast(BF16)
                items = [(psb[:, s * 128: s * 128 + nr], XB[bs_][:nr, (half * 8 + s) * 128:(half * 8 + s + 1) * 128],
                          IDB[:nr, :nr]) for s in range(8)]
                transposes(items, [("xb", bs_), "IDB"], [("ps", b)])
                src = psb.rearrange("p (s c) -> p s c", s=8)[:, :, 0:nr]
                dst = XNT[:, half * 8:(half + 1) * 8, r0:r0 + nr]
                copy("act" if half == 0 else "dve", dst, src, [("ps", b)], [("xnt", t, half)])

        def a1_s0(h, cis=None):
            sl = h % 2

            def consume(ci, b, c0, c1):
                if ci < 2:
                    copy("act", XA[sl][:, c0:c1], PS[b][:, 0:c1 - c0], [("ps", b)], [("xa", sl, ci)])
                else:
                    copy("act", XA[sl][:, 1024:HALO + NP], PS[b][:, 0:HALO], [("ps", b)], [("xa", sl, 2)])
                    copy("act", XS[:, h, :, 3:11], PS[b][:, HALO:HALO + NS].rearrange("p (s t) -> p s t", s=NSEQ),
                         [("ps", b), "XShist"], [("xs", h)])
            inproj(h, HCH, consume, cis)

        early_xa = {7: (0, 0), 8: (0, 1)}
        xa_done = {h: set() for h in range(8)}
        p1_sq(0)
        p1_sq(1)
        p1_stt(0)
        for t in range(ntile):
            if t + 2 < ntile:
                p1_sq(t + 2)
            if t + 1 < ntile:
                p1_stt(t + 1)
            p1_tr(t)
            if t in early_xa:
                h_, c_ = early_xa[t]
                a1_s0(h_, cis=[c_])
                xa_done[h_].add(c_)
        act(CLT[:, :], PA[:, :, 7], AF.Exp, ["PA"], ["CLT"], scale=-1.0)
        act(CL[:, :], CLT[:, :], AF.Ln, ["CLT"], ["CL0"], bias=1.0)
        ts("dve", CL[:, :], CL[:, :], -8.0, None, ALU.mult, None, ["CL0"], ["CL"])

        dma("sp", STH[:NSEQ, :], sth_d, [], [("xt", 3)])
        dma("sp", STA[:NSEQ * 3, :], stca_d, [], [("xt", 2)])
        b = next_bank()
        transposes([(PS[b][:, h * NSEQ:(h + 1) * NSEQ], STH[:NSEQ, h * 128:(h + 1) * 128], IDF[:NSEQ, :NSEQ])
                    for h in range(8)], [("xt", 3), "IDF"], [("ps", b)])
        copy("dve", H0S[:, :, :], PS[b][:, 0:8 * NSEQ].rearrange("p (h s) -> p h s", h=8), [("ps", b)], ["H0S"])
        b = next_bank()
        transposes([(PS[b][:, h * 48:(h + 1) * 48], STA[:48, h * 128:(h + 1) * 128], IDF[:48, :48])
                    for h in range(8)], [("xt", 2), "IDF"], [("ps", b)])
        copy("dve", XS[:, :, :, 0:3], PS[b][:, 0:8 * 48].rearrange("p (h s r) -> p h s r", h=8, s=NSEQ),
             [("ps", b)], ["XShist"])
        STCB = [NG1[:, 0:1024], NG1[:, 1024:2048]]

        def stcb_tile(rt):
            sl = rt % 2
            dma("sp", STCB[sl][:120, :], stcb_d[rt * 120:(rt + 1) * 120, :], [], [("stcb", sl), "NG1"])
            for half in range(2):
                b = next_bank()
                transposes([(PS[b][:, jj * 120:(jj + 1) * 120],
                             STCB[sl][:120, (half * 4 + jj) * 128:(half * 4 + jj + 1) * 128],
                             IDF[:120, :120]) for jj in range(4)], [("stcb", sl), "IDF"], [("ps", b)])
                copy("act", UBS[:, half * 4:(half + 1) * 4, rt * 4:(rt + 1) * 4, 0:30],
                     PS[b][:, 0:480].rearrange("p (j s r) -> p j s r", j=4, s=4), [("ps", b)], [("ubsh", rt, half)])

        S.fence()
        CV.reset(EARLY)
        XC = [CV.f32(NTOK) for _ in range(2)]
        XCB = [CV.bf16(NTOK) for _ in range(2)]
        RR = CV.f32(NTOK)
        II = CV.f32(NTOK)
        T1 = CV.f32(NTOK)
        AT = [CV.f32(NTOK) for _ in range(2)]
        BT = [CV.f32(NTOK) for _ in range(2)]
        HSCR = CV.f32(NP)
        assert CV.off <= XA_OFF, (CV.off, XA_OFF)

        def a1_s1(h):
            sl = h % 2
            xa_r = [("xa", sl, 0), ("xa", sl, 1), ("xa", sl, 2), "PA"]
            ts("dve", XC[sl][:, 0:NP], XA[sl][:, HALO:HALO + NP], PA[:, h, 3:4], PA[:, h, 4:5], ALU.mult, ALU.add,
               xa_r, [("xc", sl)])
            for k in range(3):
                stt(XC[sl][:, 0:NP], XA[sl][:, HALO - 3 + k:HALO - 3 + k + NP], PA[:, h, k:k + 1], XC[sl][:, 0:NP],
                    ALU.mult, ALU.add, xa_r + [("xc", sl)], [("xc", sl)])
            xcs = XC[sl][:, NP:NTOK].rearrange("p (s t) -> p s t", s=NSEQ)
            ts("dve", xcs, XS[:, h, :, 3:11], PA[:, h, 3:4], PA[:, h, 4:5], ALU.mult, ALU.add,
               [("xs", h), "XShist", "PA"], [("xcs", sl)])
            for k in range(3):
                stt(xcs, XS[:, h, :, k:k + 8], PA[:, h, k:k + 1], xcs, ALU.mult, ALU.add,
                    [("xs", h), "XShist", "PA", ("xcs", sl)], [("xcs", sl)])
            copy("dve", XCB[sl][:, :], XC[sl][:, :], [("xc", sl), ("xcs", sl)], [("xcb", sl)])
            copy("act", XAP3[:, h, :], XA[sl][:, HALO + NP - 3:HALO + NP], xa_r, [("xap3", h)])

        GCH = [(0, 512), (512, 1024), (1024, NTOK)]

        def a1_s2(h):
            sl = h % 2
            for g, dst, key, bcol in ((0, RR, "RR", 5), (1, II, "II", 6)):
                for ci, (c0, c1) in enumerate(GCH):
                    b = next_bank()
                    mm_group(PS[b][:, 0:c1 - c0], [(GW[:, g * 8 + h, :], XCB[sl][:, c0:c1])], ["GW", ("xcb", sl)],
                             [("ps", b)])
                    act(dst[:, c0:c1], PS[b][:, 0:c1 - c0], AF.Sigmoid, [("ps", b), "PA"], [(key, ci)],
                        bias=PA[:, h, bcol:bcol + 1])

        def a1_s3(h):
            sl = h % 2
            rk = [("RR", i) for i in range(3)]
            ik = [("II", i) for i in range(3)]
            atk = [("at", sl), ("ats", sl)]
            btk = [("bt", sl), ("bts", sl)]
            act(AT[sl][:, :], RR[:, :], AF.Exp, rk + ["CL"], atk, scale=CL[:, h:h + 1])
            act(T1[:, :], AT[sl][:, :], AF.Square, atk, ["T1"])
            act(T1[:, :], T1[:, :], AF.Ln, ["T1"], ["T1"], scale=-1.0, bias=1.0)
            act(T1[:, :], T1[:, :], AF.Exp, ["T1"], ["T1"], scale=0.5)
            tt("dve", BT[sl][:, :], II[:, :], XC[sl][:, :], ALU.mult, ik + [("xc", sl), ("xcs", sl)], btk)

        def a1_s4(h):
            sl = h % 2
            btk = [("bt", sl), ("bts", sl)]
            tt("dve", BT[sl][:, :], BT[sl][:, :], T1[:, :], ALU.mult, btk + ["T1"], btk)
            dma("sp", a_sc[h, :, :], AT[sl][:, 0:NP], [("at", sl)], [("asc", h)])
            dma("sp", b_sc[h, :, :], BT[sl][:, 0:NP], [("bt", sl)], [("bsc", h)])
            scan(HSCR[:, :], AT[sl][:, 0:NP], BT[sl][:, 0:NP], 0.0, [("at", sl), ("bt", sl)], ["HSCR"])
            copy("dve", HFIN[:, h:h + 1], HSCR[:, NP - 1:NP], ["HSCR"], [("hfin", h)])
            a3 = AT[sl][:, NP:NTOK].rearrange("p (s t) -> p s t", s=NSEQ)
            b3 = BT[sl][:, NP:NTOK].rearrange("p (s t) -> p s t", s=NSEQ)
            tt("dve", TMP16[:, :], a3[:, :, 0], H0S[:, h, :], ALU.mult, [("ats", sl), "H0S"], ["TMP16"])
            tt("dve", b3[:, :, 0], b3[:, :, 0], TMP16[:, :], ALU.add, [("bts", sl), "TMP16"], [("bts", sl)])
            S.op("dve", lambda e: e.memset(a3[:, :, 0], 0.0), [("ats", sl)], [("ats", sl)])
            scan(HS[:, h, :], AT[sl][:, NP:NTOK], BT[sl][:, NP:NTOK], 0.0, [("ats", sl), ("bts", sl)], [("hs", h)])

        def a2_t0(h):
            def consume(ci, b, c0, c1):
                act(MIX[:, h, c0 - HALO:c1 - HALO], PS[b][:, 0:c1 - c0], AF.Silu, [("ps", b)], [("mix", h, ci)])
            inproj(8 + h, NCHK, consume)

        for i in range(-3, 8):
            if 0 <= i < 8:
                a1_s3(i)
            if 0 <= i + 2 < 8:
                a1_s1(i + 2)
            if 0 <= i < 8:
                a1_s4(i)
            if 0 <= i + 1 < 8:
                a1_s2(i + 1)
            if 0 <= i + 3 < 8:
                a1_s0(i + 3, cis=[c for c in range(3) if c not in xa_done[i + 3]])
            elif i + 3 >= 8:
                a2_t0(i + 3 - 8)
            if 1 <= i <= 4:
                stcb_tile(i - 1)

        S.fence()
        CV.reset(EARLY)
        NAB = 3
        ABL = [CV.f32(2 * NP) for _ in range(NAB)]
        HP = [CV.f32(NP) for _ in range(2)]

        def a_out_gather():
            hsk = [("hs", h) for h in range(8)]
            copy("act", HSL[:, :, :], HS[:, :, :].rearrange("p h (s t) -> p h s t", s=NSEQ)[:, :, :, TS - 1], hsk, ["HSL"])
            copy("act", X3[:, :, :].rearrange("p h (s r) -> p h s r", s=NSEQ), XS[:, :, :, 8:11],
                 [("xs", h) for h in range(8)], ["X3"])

        def a_outputs():
            S.nofence_default = True
            for half in range(2):
                b = next_bank()
                transposes([(PS[b][:NSEQ, q * 128:(q + 1) * 128], HSL[:, half * 4 + q, :], IDF[:, :]) for q in range(4)],
                           ["HSL", "IDF"], [("ps", b)])
                copy("act", STG[:NSEQ, half * 512:(half + 1) * 512], PS[b][:NSEQ, :], [("ps", b)], ["STG"])
            dma("sp", ohs_d, STG[:NSEQ, :], ["STG"], ["ohs"])
            for half in range(2):
                b = next_bank()
                transposes([(PS[b][:48, q * 128:(q + 1) * 128], X3[:, half * 4 + q, :], IDF[:, :]) for q in range(4)],
                           ["X3", "IDF"], [("ps", b)])
                copy("act", STG2[:48, half * 512:(half + 1) * 512], PS[b][:48, :], [("ps", b)], ["STG2"])
            dma("sp", ocas_d, STG2[:48, :], ["STG2"], ["ocas"])
            for half in range(2):
                b = next_bank()
                transposes([(PS[b][:3, q * 128:(q + 1) * 128], XAP3[:, half * 4 + q, :], IDF[:, :]) for q in range(4)],
                           [("xap3", h) for h in range(8)] + ["IDF"], [("ps", b)])
                copy("act", STG[:3, half * 512:(half + 1) * 512], PS[b][:3, :], [("ps", b)], ["STG"])
            dma("sp", ocap_d, STG[:3, :], ["STG"], ["ocap"])
            b = next_bank()
            transposes([(PS[b][:8, 0:128], HFO[:, :], IDF[:, :])], [("hfo", h) for h in range(8)] + ["IDF"], [("ps", b)])
            copy("act", STG2[:8, 0:128], PS[b][:8, 0:128], [("ps", b)], ["STG2"])
            dma("sp", ohp_d, STG2[:8, 0:128], ["STG2"], ["ohp"])
            S.nofence_default = False

        def a2_load(h):
            sl = h % NAB
            dma("sp", ABL[sl][:, 0:NP], a_sc[h, :, :], [("asc", h)], [("abla", sl)])
            dma("sp", ABL[sl][:, NP:2 * NP], b_sc[h, :, :], [("bsc", h)], [("ablb", sl)])

        def a2_t1(h):
            sl = h % 2
            al = h % NAB
            scan(HP[sl][:, :], ABL[al][:, 0:NP], ABL[al][:, NP:2 * NP], HIN[:, h:h + 1],
                 [("abla", al), ("ablb", al), "HIN"], [("hp", sl)])
            mk = [("mix", h, i) for i in range(3)]
            tt("dve", MIX[:, h, 0:NP], HP[sl][:, :], MIX[:, h, 0:NP], ALU.mult, [("hp", sl)] + mk, [("mixp", h)])
            tt("dve", MIX[:, h, NP:NTOK], HS[:, h, :], MIX[:, h, NP:NTOK], ALU.mult, [("hs", h)] + mk, [("mixs", h)])
            copy("dve", HFO[:, h:h + 1], HP[sl][:, NP - 1:NP], [("hp", sl)], [("hfo", h)])

        a2_load(0)
        a2_load(1)
        a2_load(2)
        a_out_gather()
        hk = [("hfin", h) for h in range(8)]
        dma("pool", cc_in[:, :], HFIN[:, :], hk, ["cc_in"])
        S.op("pool", lambda e: e.collective_compute("AllGather", ALU.bypass,
                                                    replica_groups=[[0, 1], [2, 3], [4, 5], [6, 7]],
                                                    ins=[cc_in.ap().opt()], outs=[cc_out.ap().opt()]),
             ["cc_in"], ["cc_out"], cc=True)
        dma("sp", HINR[:, :], cc_out[0:128, :], ["cc_out"], ["HINR"])
        ts("dve", HIN[:, :], HINR[:, :], MASK[:, 0:1], None, ALU.mult, None, ["HINR", "MASK"], ["HIN"])

        for h in range(8):
            if h + 3 < 8:
                a2_t0(h + 3)
            a2_t1(h)
            if h + 3 < 8:
                a2_load(h + 3)

        KD = 14
        NPT = 31 - KD
        CV.reset(EARLY_B)
        CB = CV.bf16(8 * NTOK).rearrange("p (j t) -> p j t", j=8)
        ZZ = [CV.f32(NTOK) for _ in range(2)]
        MEAN = CV.f32(NTOK)
        RSTD = CV.f32(NTOK)
        ACC = CV.f32(NTOK)
        SQ = [CV.bf16(512) for _ in range(2)]
        assert CV.off >= EARLY + 3 * 2 * NP * 4 + 2 * NP * 4, CV.off
        S.late_keys([("cb", j, ci) for j in range(8) for ci in range(3)] + [("zz", 0), ("zz", 1)] +
                    [("m2", i) for i in range(3)] + [("mean", i) for i in range(3)] +
                    [("rstd", i) for i in range(3)] + [("sq", i) for i in range(2)] + ["ACC", "ACCs"])
        SG = [CV.f32(NT) for _ in range(2)]
        UU = [CV.f32(NT) for _ in range(2)]
        UBP = [CV.bf16(HALO + NP) for _ in range(2)]
        DG = [CV.bf16(NPT * 128).rearrange("p (k c) -> p k c", k=NPT) for _ in range(2)]
        assert CV.off <= WO_END, (CV.off, WO_END)

        def b1_u0(j):
            sl = j % 2
            def dg_fn(e):
                ins = None
                for k in range(KD, 31):
                    ins = e.activation(out=DG[sl][:, k - KD, :], in_=IDB[:, :], func=AF.Identity, scale=PB[:, j, k:k + 1])
                return ins
            S.op("act", dg_fn, ["IDB", "PB"], [("dg", sl)])

            def cons_g(ci, b, c0, c1):
                act(SG[sl][:, c0:c1], PS[b][:, 0:c1 - c0], AF.Sigmoid, [("ps", b)], [("sg", sl, ci)])
            inproj(24 + j, HCH, cons_g)

            def cons_v(ci, b, c0, c1):
                copy("act", UU[sl][:, c0:c1], PS[b][:, 0:c1 - c0], [("ps", b)], [("uu", sl, ci)])
            inproj(16 + j, HCH, cons_v)
            for ci, (c0, c1) in enumerate(HCH):
                tt("dve", UU[sl][:, c0:c1], UU[sl][:, c0:c1], SG[sl][:, c0:c1], ALU.mult,
                   [("uu", sl, ci), ("sg", sl, ci)], [("uu", sl, ci)])
            uk = [("uu", sl, 0), ("uu", sl, 1), ("uu", sl, 2)]
            ceng = "dve"
            copy(ceng, UBP[sl][:, :], UU[sl][:, 0:HALO + NP], uk, [("ubp", sl)])
            copy(ceng, UBS[:, j, :, 30:38], UU[sl][:, HALO + NP:NT].rearrange("p (s t) -> p s t", s=NSEQ), uk,
                 [("ubsn", j)])

        def b2_proj(j):
            def cons(ci, b, c0, c1):
                act(MIX[:, 8 + j, c0 - HALO:c1 - HALO], PS[b][:, 0:c1 - c0], AF.Silu, [("ps", b)], [("mixb", j, ci)])
            inproj(32 + j, NCHK, cons)

        def b1_taps(j):
            sl = j % 2
            uk = [("uu", sl, 0), ("uu", sl, 1), ("uu", sl, 2)]
            ubsk = [("ubsn", j)] + [("ubsh", rt, j // 4) for rt in range(4)]
            accs = ACC[:, NP:NTOK].rearrange("p (s t) -> p s t", s=NSEQ)
            if j == 0:
                ts("dve", ACC[:, 0:NP], UU[sl][:, 2:2 + NP], PB[:, j, 0:1], None, ALU.mult, None, uk + ["PB"], ["ACC"])
                ts("dve", accs, UBS[:, j, :, 0:8], PB[:, j, 0:1], None, ALU.mult, None, ubsk + ["PB"], ["ACCs"])
            else:
                act(ACC[:, 0:NP], UU[sl][:, 2:2 + NP], AF.Identity, uk + ["PB"], ["ACC"], scale=PB[:, j, 0:1])
                act(accs, UBS[:, j, :, 0:8], AF.Identity, ubsk + ["PB"], ["ACCs"], scale=PB[:, j, 0:1])
            for k in range(1, KD):
                stt(ACC[:, 0:NP], UU[sl][:, 2 + k:2 + k + NP], PB[:, j, k:k + 1], ACC[:, 0:NP], ALU.mult, ALU.add,
                    uk + ["PB", "ACC"], ["ACC"])
                stt(accs, UBS[:, j, :, k:k + 8], PB[:, j, k:k + 1], accs, ALU.mult, ALU.add,
                    ubsk + ["PB", "ACCs"], ["ACCs"])

        def b1_u1(j):
            sl = j % 2
            uk = [("uu", sl, 0), ("uu", sl, 1), ("uu", sl, 2)]
            b = next_bank()
            transposes([(PS[b][:30, 0:128], UU[sl][:, HALO + NP - 30:HALO + NP], IDF[:, :]),
                        (PS[b][:, 128:256], UU[sl][:, HALO + NP:NT], IDF[:, :])], uk + ["IDF"], [("ps", b)])
            copy("act", STG[:30, j * 128:(j + 1) * 128], PS[b][:30, 0:128], [("ps", b)], ["STG"])
            copy("act", STG2[:, j * 128:(j + 1) * 128], PS[b][:, 128:256], [("ps", b)], ["STG2"])
            for ci, (c0, c1) in enumerate([(0, 512), (512, 1024)]):
                b = next_bank()
                pairs = [(DG[sl][:, k - KD, :], UBP[sl][:, c0 + k + 2:c0 + k + 2 + 512]) for k in range(KD, 31)]
                mm_group(PS[b][:, :], pairs, [("dg", sl), ("ubp", sl)], [("ps", b)])
                stt(CB[:, j, c0:c1], PS[b][:, :], PB[:, j, 31:32], ACC[:, c0:c1], ALU.add, ALU.add,
                    [("ps", b), "PB", "ACC"], [("cb", j, ci)])
            b = next_bank()
            pairs = [(DG[sl][:, k - KD, :], UBS[:, j, :, k:k + 8]) for k in range(KD, 31)]
            mm_group(PS[b][:, 0:NS].rearrange("p (s t) -> p s t", s=NSEQ), pairs,
                     [("dg", sl), ("ubsn", j)] + [("ubsh", rt, j // 4) for rt in range(4)], [("ps", b)])
            stt(CB[:, j, NP:NTOK], PS[b][:, 0:NS], PB[:, j, 31:32], ACC[:, NP:NTOK], ALU.add, ALU.add,
                [("ps", b), "PB", "ACCs"], [("cb", j, 2)])

        b1_u0(0)
        for j in range(8):
            b1_taps(j)
            if j + 1 < 8:
                b1_u0(j + 1)
            if j == 0:
                b2_proj(0)
                b2_proj(1)
                a_outputs()
            if j == 2:
                dma("sp", ocbh_d, stcb_d.rearrange("(s r) c -> s r c", r=30)[:, 8:30, :], [], ["ocbh"], nofence=True)
            if 2 <= j <= 5:
                n_ = j - 2
                dma("pool", wo_bf[n_, :, :].rearrange("p (k j) -> p k j", k=16),
                    wout_d[n_].rearrange("p (k j) -> p k j", k=16), [], [("wobf", n_)], nofence=True)
            if j == 7:
                b2_proj(2)
                b2_proj(3)
            b1_u1(j)
        S.nofence_default = True
        dma("sp", ocbp_d, STG[:30, :], ["STG"], ["ocbp"])
        dma("sp", ocbn_d, STG2[:, :], ["STG2"], ["ocbn"])
        S.nofence_default = False

        SCH = [(0, 512), (512, 1024), (1024, NTOK)]
        sbanks = {}
        for ci, (c0, c1) in enumerate(SCH):
            n = c1 - c0
            b1_ = next_bank()
            b2_ = next_bank()
            sbanks[ci] = (b1_, b2_)
            for j in range(8):
                sq = j % 2
                if j % 2 == 0:
                    act(SQ[sq][:, 0:n], CB[:, j, c0:c1], AF.Square, [("cb", j, ci)], [("sq", sq)])
                else:
                    tt("dve", SQ[sq][:, 0:n], CB[:, j, c0:c1], CB[:, j, c0:c1], ALU.mult, [("cb", j, ci)], [("sq", sq)])
                S.op("pe", (lambda e, j=j, b=b1_, c0=c0, c1=c1, n=n:
                            e.matmul(PS[b][:, 0:n], ONES[:, :], CB[:, j, c0:c1], start=(j == 0), stop=(j == 7))),
                     ["ONES", ("cb", j, ci)], [("ps", b1_)] if j == 0 else [("psacc", b1_, j)])
                S.op("pe", (lambda e, j=j, b=b2_, sq=sq, n=n:
                            e.matmul(PS[b][:, 0:n], ONES[:, :], SQ[sq][:, 0:n], start=(j == 0), stop=(j == 7))),
                     ["ONES", ("sq", sq)], [("ps", b2_)] if j == 0 else [("psacc", b2_, j)])
        mk = [("mean", ci) for ci in range(3)]
        rk_ = [("rstd", ci) for ci in range(3)]
        m2k = [("m2", ci) for ci in range(3)]
        for ci, (c0, c1) in enumerate(SCH):
            b1_, _b2 = sbanks[ci]
            k1 = [("ps", b1_)] + [("psacc", b1_, j) for j in range(1, 8)]
            act(MEAN[:, c0:c1], PS[b1_][:, 0:c1 - c0], AF.Identity, k1, [("mean", ci)], scale=1.0 / WA)
        tt("dve", ZZ[0][:, :], MEAN[:, :], MEAN[:, :], ALU.mult, mk, m2k)
        for ci, (c0, c1) in enumerate(SCH):
            _b1, b2_ = sbanks[ci]
            k2 = [("ps", b2_)] + [("psacc", b2_, j) for j in range(1, 8)]
            stt(RSTD[:, c0:c1], PS[b2_][:, 0:c1 - c0], 1.0 / WA, ZZ[0][:, c0:c1], ALU.mult, ALU.subtract,
                k2 + m2k, [("rstd", ci)])
        act(RSTD[:, :], RSTD[:, :], AF.Ln, rk_ + ["EPSC"], rk_, bias=EPSC[:, :])
        act(RSTD[:, :], RSTD[:, :], AF.Exp, rk_, rk_, scale=-0.5)
        stt(MEAN[:, :], MEAN[:, :], -1.0, RSTD[:, :], ALU.mult, ALU.mult, mk + rk_, mk)
        stat_k = [("rstd", i) for i in range(3)] + [("mean", i) for i in range(3)]

        def b2_rest_a(j):
            sl = j % 2
            cbk = [("cb", j, i) for i in range(3)]
            zk = [("zz", sl)] + ([("m2", i) for i in range(3)] if sl == 0 else [])
            tt("dve", ZZ[sl][:, :], CB[:, j, :], RSTD[:, :], ALU.mult, cbk + stat_k, zk)
            tt("dve", ZZ[sl][:, :], ZZ[sl][:, :], MEAN[:, :], ALU.add, zk + stat_k, zk)
            act(ZZ[sl][:, :], ZZ[sl][:, :], AF.Silu, zk + ["PB"], zk, bias=PB[:, j, 33:34], scale=PB[:, j, 32:33])

        def b2_rest_b(j):
            sl = j % 2
            zk = [("zz", sl)] + ([("m2", i) for i in range(3)] if sl == 0 else [])
            tt("dve", MIX[:, 8 + j, :], ZZ[sl][:, :], MIX[:, 8 + j, :], ALU.mult,
               zk + [("mixb", j, i) for i in range(3)], [("mixB", j)])

        def wo_view(off):
            return MAINF[:, off // 4: off // 4 + 16 * 512 // 2].bitcast(BF16).rearrange("p (k n) -> p k n", k=16)

        WO = [wo_view(WO_END), None, None]
        WSL = {0: 0, 1: 1, 2: 2, 3: 0}

        def wo_load(n):
            src = wo_bf[n, :, :].rearrange("p (k j) -> p k j", k=16)
            for hf in range(2):
                dma("pool", WO[WSL[n]][:, hf * 8:(hf + 1) * 8, :], src[:, hf * 8:(hf + 1) * 8, :], [("wobf", n)],
                    [("wo", WSL[n], hf)])

        wo_load(0)
        b2_rest_a(0)
        for j in range(8):
            if j + 1 < 8:
                b2_rest_a(j + 1)
            b2_rest_b(j)
            if j + 4 < 8:
                b2_proj(j + 4)

        S.fence()
        CV.reset(0)
        HRES = CV.f32(9 * D).rearrange("p (i d) -> p i d", i=9)
        WO[1] = CV.bf16(16 * 512).rearrange("p (k n) -> p k n", k=16)
        WO[2] = CV.bf16(16 * 512).rearrange("p (k n) -> p k n", k=16)
        NXRE = 6
        XRE = [CV.f32(512) for _ in range(NXRE)]
        SQJ = CV.bf16(512)
        FGB = CV.f32(D)
        assert CV.off <= WO_END, (CV.off, WO_END)
        wo_load(1)
        allmix = []
        for h in range(8):
            allmix += [("mixp", h), ("mixs", h)]
        for j in range(8):
            allmix += [("mixB", j)]
        xi = 0
        amix = []
        for h in range(8):
            amix += [("mixp", h), ("mixs", h)]
        bmix = [("mixB", j) for j in range(8)]

        def mm_part(out, pairs, first, last, reads, writes):
            def fn(e):
                ins = None
                n_ = len(pairs)
                for q, (l, r) in enumerate(pairs):
                    ins = e.matmul(out, l, r, start=(first and q == 0), stop=(last and q == n_ - 1))
                return ins
            return S.op("pe", fn, reads, writes)

        pend = [None]

        def fin_a(i):
            S.op("dve", lambda e: e.tensor_reduce(out=SSQ1[:, i:i + 1], in_=SSQ[:, i, :],
                                                  axis=mybir.AxisListType.X, op=ALU.add),
                 [("ssq", i, q) for q in range(4)], [("ssq1", i)])
            act(LN2[:, i:i + 1], SSQ1[:, i:i + 1], AF.Ln, [("ssq1", i), "EPSC"], [("ln2", i)],
                bias=EPSC[:, :], scale=1.0 / D)
            act(RS2[:, i:i + 1], LN2[:, i:i + 1], AF.Exp, [("ln2", i)], [("rs2", i)], scale=-0.5)

        def fin_b(i):
            hk = [("hres", i, q) for q in range(4)]
            if i < 8:
                stt(HRES[:, i, :], HRES[:, i, :], RS2[:, i:i + 1], FGB[:, :], ALU.mult, ALU.mult,
                    hk + [("rs2", i), "FGB"], hk)
                dma("pool", y_d[i * 128:(i + 1) * 128, :], HRES[:, i, :], hk, [("y", i)])
            else:
                for hf in range(2):
                    cs = slice(hf * 1024, (hf + 1) * 1024)
                    hk2 = [("hres", i, 2 * hf), ("hres", i, 2 * hf + 1)]
                    stt(HRES[:, i, cs], HRES[:, i, cs], RS2[:, i:i + 1], FGB[:, cs], ALU.mult, ALU.mult,
                        hk2 + [("rs2", i), "FGB"], hk2)
                    dma("pool", y_d[i * 128:(i + 1) * 128, cs], HRES[:, i, cs], hk2, [("y", i, hf)])

        G = 4
        ph1 = [(n, i) for n in range(2) for i in range(9)]
        ph2 = [(n, i) for i in range(9) for n in (2, 3)]
        G0 = 7
        batches = [ph1[0:G0]] + [ph1[g0:g0 + G] for g0 in range(G0, len(ph1), G)] + \
                  [ph2[g0:g0 + G] for g0 in range(0, len(ph2), G)]
        for batch in batches:
            banks = []
            for (n, i) in batch:
                b = next_bank()
                banks.append(b)
                pairs = [(MIX[:, kc, i * 128:(i + 1) * 128], WO[WSL[n]][:, kc, :]) for kc in range(8)]
                mm_part(PS[b][:, :], pairs, True, False, amix + [("wo", WSL[n], 0)], [("ps", b)])
            for (n, i), b in zip(batch, banks):
                pairs = [(MIX[:, kc, i * 128:(i + 1) * 128], WO[WSL[n]][:, kc, :]) for kc in range(8, 16)]
                mm_part(PS[b][:, :], pairs, False, True, bmix + [("wo", WSL[n], 1)], [("ps2", b)])
                xs_ = xi % NXRE
                xi += 1
                r0 = HALO + i * 128
                dma("sp", XRE[xs_][:, :], x_d[r0:r0 + 128, n * 512:(n + 1) * 512], [], [("xre", xs_)])
                tt("dve", HRES[:, i, n * 512:(n + 1) * 512], PS[b][:, :], XRE[xs_][:, :], ALU.add,
                   [("ps", b), ("ps2", b), ("xre", xs_)], [("hres", i, n)])
                act(SQJ[:, :], HRES[:, i, n * 512:(n + 1) * 512], AF.Square, [("hres", i, n)], ["SQJ", ("ssq", i, n)],
                    accum_out=SSQ[:, i, n:n + 1])
                if n == 3:
                    if pend[0] is not None:
                        fin_b(pend[0])
                    fin_a(i)
                    pend[0] = i
                if i == 4 and n == 0:
                    wo_load(2)
                if i == 0 and n == 1:
                    dma("sp", FGB[:, :], fg_d.partition_broadcast(128), [], ["FGB"])
                if i == 8 and n == 0:
                    wo_load(3)

        fin_b(pend[0])

        S.finalize()
        with nc.Block(no_gpsimd_drain=True) as block:
            @block.tensor
            def _(e):
                S.emit("pe", e, sems, lane_sems, cc_sem)

            @block.scalar
            def _(e):
                S.emit("act", e, sems, lane_sems, cc_sem)

            @block.vector
            def _(e):
                S.emit("dve", e, sems, lane_sems, cc_sem)

            @block.gpsimd
            def _(e):
                S.emit("pool", e, sems, lane_sems, cc_sem)

            @block.sync
            def _(e):
                S.emit("sp", e, sems, lane_sems, cc_sem, final_wait=True)
    return nc


_NC_CACHE = {}


def kernel(x_prompt, x_sample, state_lru_h, state_lru_conv, state_glu_conv,
           norm_gain, w_in, conv_a_w, conv_a_b, gate_a_w, gate_a_b, gate_x_w, gate_x_b,
           lru_param, conv_b_w, conv_b_b, ln_b_gain, ln_b_bias, w_out, final_gain):
    f = np.float32
    x_prompt = np.asarray(x_prompt, f)
    x_sample = np.asarray(x_sample, f)
    sth = np.asarray(state_lru_h, f)[0]
    stca = np.asarray(state_lru_conv, f)[0]
    stcb = np.asarray(state_glu_conv, f)[0]
    w_in = np.asarray(w_in, f)[0]
    w_out = np.asarray(w_out, f)[0]

    win_r = np.ascontiguousarray(w_in.reshape(16, 128, 40, 128).transpose(2, 1, 0, 3).reshape(40, 128, 2048))
    wout_r = np.ascontiguousarray(w_out.reshape(16, 128, 4, 512).transpose(2, 1, 0, 3).reshape(4, 128, 16 * 512))
    gws = np.stack([np.asarray(gate_a_w, f)[0], np.asarray(gate_x_w, f)[0]])
    gw_r = np.ascontiguousarray(gws.transpose(2, 0, 1, 3).reshape(128, 16 * 128))

    def chan(v):
        return np.asarray(v, f).reshape(8, 128).T

    pa = np.zeros((128, 8, 8), f)
    caw = np.asarray(conv_a_w, f)[0]
    for k in range(4):
        pa[:, :, k] = chan(caw[k])
    pa[:, :, 4] = chan(np.asarray(conv_a_b, f)[0])
    pa[:, :, 5] = chan(np.asarray(gate_a_b, f)[0])
    pa[:, :, 6] = chan(np.asarray(gate_x_b, f)[0])
    pa[:, :, 7] = chan(np.asarray(lru_param, f)[0])
    pb = np.zeros((128, 8, 35), f)
    cbw = np.asarray(conv_b_w, f)[0]
    for k in range(31):
        pb[:, :, k] = chan(cbw[k])
    pb[:, :, 31] = chan(np.asarray(conv_b_b, f)[0])
    pb[:, :, 32] = chan(np.asarray(ln_b_gain, f)[0])
    pb[:, :, 33] = chan(np.asarray(ln_b_bias, f)[0])
    ng = np.ascontiguousarray(np.asarray(norm_gain, f)[0])
    fg = np.ascontiguousarray(np.asarray(final_gain, f))
    ident = np.eye(128, dtype=f)

    in_maps = []
    for c in range(NCORES):
        q, half = c // 2, c % 2
        xl = np.zeros((NT, D), f)
        if half == 1:
            xl[0:HALO] = x_prompt[q, NP - HALO:NP]
        xl[HALO:HALO + NP] = x_prompt[q, half * NP:(half + 1) * NP]
        xl[HALO + NP:] = x_sample[c * NSEQ:(c + 1) * NSEQ].reshape(NS, D)
        in_maps.append({
            "x": xl,
            "sth": np.ascontiguousarray(sth[c * NSEQ:(c + 1) * NSEQ]),
            "stca": np.ascontiguousarray(stca[c * NSEQ:(c + 1) * NSEQ].reshape(NSEQ * 3, WA)),
            "stcb": np.ascontiguousarray(stcb[c * NSEQ:(c + 1) * NSEQ].reshape(NSEQ * 30, WA)),
            "win": win_r, "wout": wout_r, "gw": gw_r,
            "pa": pa.reshape(128, 64), "pb": pb.reshape(128, 280),
            "ng": ng, "fg": fg, "ident": ident,
            "mask": np.full((128, 1), float(half), f),
        })

    if "nc" not in _NC_CACHE:
        _NC_CACHE["nc"] = build_nc()
    nc = _NC_CACHE["nc"]
    res = run_bass_kernel_spmd(nc, in_maps, core_ids=list(range(NCORES)))
    R = res.results

    y_prompt = np.zeros((4, 2048, D), f)
    y_sample = np.zeros((128, TS, D), f)
    o_hp = np.zeros((1, 4, WA), f)
    o_cap = np.zeros((1, 4, 3, WA), f)
    o_cbp = np.zeros((1, 4, 30, WA), f)
    o_hs = np.zeros((1, 128, WA), f)
    o_cas = np.zeros((1, 128, 3, WA), f)
    o_cbs = np.zeros((1, 128, 30, WA), f)
    for c in range(NCORES):
        q, half = c // 2, c % 2
        r = R[c]
        y_prompt[q, half * NP:(half + 1) * NP] = r["y"][0:NP]
        y_sample[c * NSEQ:(c + 1) * NSEQ] = r["y"][NP:].reshape(NSEQ, TS, D)
        if half == 1:
            o_hp[0, q] = r["ohp"].reshape(WA)
            o_cap[0, q] = r["ocap"]
            o_cbp[0, q] = r["ocbp"]
        o_hs[0, c * NSEQ:(c + 1) * NSEQ] = r["ohs"]
        o_cas[0, c * NSEQ:(c + 1) * NSEQ] = r["ocas"].reshape(NSEQ, 3, WA)
        o_cbs[0, c * NSEQ:(c + 1) * NSEQ, 0:22] = r["ocbh"]
        o_cbs[0, c * NSEQ:(c + 1) * NSEQ, 22:30] = r["ocbn"].reshape(NSEQ, TS, WA)
    return (y_prompt, y_sample, o_hp, o_cap, o_cbp, o_hs, o_cas, o_cbs)
```

```python
import contextlib
import numpy as np
import concourse.bass as bass
import concourse.mybir as mybir
from concourse.bass_utils import run_bass_kernel_spmd

F32 = mybir.dt.float32
BF16 = mybir.dt.bfloat16
AF = mybir.ActivationFunctionType
ALU = mybir.AluOpType

NCORES = 8
D = 2048
WA = 1024
NH = 8
CONV_A = 4
CONV_B = 31
HALO = 32
NP = 1024
NS = 128
NSEQ = 16
TS = 8
NT = HALO + NP + NS
NTOK = NP + NS
EPS = 1e-6
NWS = 4
NLANES = 30
NL_SP = 20

HCH = [(0, 512), (512, 1024), (1024, NT)]
NCHK = [(HALO, HALO + 512), (HALO + 512, HALO + 1024), (HALO + 1024, NT)]


class Sched:
    def __init__(self):
        self.ops = []
        self.res = {}
        self.lane_last = [None] * NLANES
        self.lane_count = [0] * NLANES
        self.next_lane = {"sp": 0, "pool": NL_SP}
        self.cur_fence = []
        self.phase_keys = set()
        self.nofence_default = False

    def _deps(self, reads, writes, nofence=False):
        deps = set()
        for k in list(reads) + list(writes):
            if not nofence:
                self.phase_keys.add(k)
            if k not in self.res and not nofence:
                deps.update(self.cur_fence)
        for k in reads:
            st = self.res.get(k)
            if st is not None and st[0] is not None:
                deps.add(st[0])
        for k in writes:
            st = self.res.get(k)
            if st is not None:
                if st[0] is not None:
                    deps.add(st[0])
                deps.update(st[1])
        return deps

    def op(self, eng, fn, reads=(), writes=(), ndma=0, cc=False, after=(), nofence=None):
        reads = list(reads)
        writes = list(writes)
        if nofence is None:
            nofence = self.nofence_default
        deps = self._deps(reads, writes, nofence)
        deps.update(after)
        idx = len(self.ops)
        o = dict(eng=eng, fn=fn, deps=deps, ndma=ndma, cc=cc, lane=None, sig=False, val=None)
        if ndma:
            lane = self.next_lane[eng]
            lo, hi = (0, NL_SP) if eng == "sp" else (NL_SP, NLANES)
            self.next_lane[eng] = lo + (lane + 1 - lo) % (hi - lo)
            if self.lane_last[lane] is not None:
                deps.add(self.lane_last[lane])
            self.lane_last[lane] = idx
            self.lane_count[lane] += ndma
            o["lane"] = lane
            o["val"] = 16 * self.lane_count[lane]
        self.ops.append(o)
        for k in reads:
            st = self.res.setdefault(k, [None, []])
            st[1].append(idx)
        for k in writes:
            self.res[k] = [idx, []]
        return idx

    def fence(self):
        deps = set(self.cur_fence)
        for k in self.phase_keys:
            st = self.res.get(k)
            if st is not None:
                if st[0] is not None:
                    deps.add(st[0])
                deps.update(st[1])
        self.cur_fence = sorted(deps)
        self.phase_keys = set()

    def late_keys(self, keys):
        deps = set(self.cur_fence)
        for k in self.phase_keys:
            st = self.res.get(k)
            if st is not None:
                if st[0] is not None:
                    deps.add(st[0])
                deps.update(st[1])
        deps = sorted(deps)
        for k in keys:
            assert k not in self.res, k
            self.res[k] = [None, list(deps)]

    def finalize(self):
        for o in self.ops:
            for d in o["deps"]:
                self.ops[d]["sig"] = True
        cnt = {}
        for o in self.ops:
            if o["ndma"] or o["cc"]:
                continue
            if o["sig"]:
                cnt[o["eng"]] = cnt.get(o["eng"], 0) + 1
                o["val"] = cnt[o["eng"]]

    def emit(self, eng_name, eng, sems, lane_sems, cc_sem, final_wait=False):
        seen = {}
        for o in self.ops:
            if o["eng"] != eng_name:
                continue
            need = {}
            for d in o["deps"]:
                p = self.ops[d]
                if p["cc"]:
                    key, sem, val = "cc", cc_sem, 1
                elif p["ndma"]:
                    key, sem, val = ("l", p["lane"]), lane_sems[p["lane"]], p["val"]
                else:
                    if p["eng"] == "pe" and eng_name == "pe":
                        continue
                    key, sem, val = p["eng"], sems[p["eng"]], p["val"]
                if val > need.get(key, (None, 0))[1]:
                    need[key] = (sem, val)
            for key, (sem, val) in need.items():
                if val > seen.get(key, 0):
                    eng.wait_ge(sem, val)
                    seen[key] = val
            r = o["fn"](eng)
            if o["cc"]:
                r.then_inc(cc_sem, 1)
            elif o["ndma"]:
                rl = r if isinstance(r, (list, tuple)) else [r]
                assert len(rl) == o["ndma"], (len(rl), o["ndma"])
                for ins in rl:
                    ins.then_inc(lane_sems[o["lane"]], 16)
            elif o["sig"]:
                r.then_inc(sems[o["eng"]], 1)
        if final_wait:
            for lane in range(NLANES):
                if self.lane_count[lane]:
                    eng.wait_ge(lane_sems[lane], 16 * self.lane_count[lane])


def build_nc():
    nc = bass.Bass("TRN2", target_bir_lowering=False)
    S = Sched()

    def din(name, shape):
        return nc.dram_tensor(name, list(shape), F32, kind="ExternalInput").ap()

    def dout(name, shape):
        return nc.dram_tensor(name, list(shape), F32, kind="ExternalOutput").ap()

    x_d = din("x", [NT, D])
    sth_d = din("sth", [NSEQ, WA])
    stca_d = din("stca", [NSEQ * 3, WA])
    stcb_d = din("stcb", [NSEQ * 30, WA])
    win_d = din("win", [40, 128, D])
    wout_d = din("wout", [4, 128, 16 * 512])
    gw_d = din("gw", [128, 16 * 128])
    pa_d = din("pa", [128, 64])
    pb_d = din("pb", [128, 8 * 35])
    ng_d = din("ng", [D])
    fg_d = din("fg", [D])
    ident_d = din("ident", [128, 128])
    mask_d = din("mask", [128, 1])

    y_d = dout("y", [NTOK, D])
    ohp_d = dout("ohp", [8, 128])
    ocap_d = dout("ocap", [3, WA])
    ocbp_d = dout("ocbp", [30, WA])
    ohs_d = dout("ohs", [NSEQ, WA])
    ocas_d = dout("ocas", [NSEQ * 3, WA])
    ocbh_d = dout("ocbh", [NSEQ, 22, WA])
    ocbn_d = dout("ocbn", [NS, WA])

    a_sc = nc.dram_tensor("a_scr", [8, 128, NP], F32)
    b_sc = nc.dram_tensor("b_scr", [8, 128, NP], F32)
    wo_bf = nc.dram_tensor("wo_bf", [4, 128, 16 * 512], BF16)
    cc_in = nc.dram_tensor("cc_in", [128, 8], F32)
    cc_out = nc.dram_tensor("cc_out", [256, 8], F32)

    es = contextlib.ExitStack()
    with es:
        def sb(name, shape, dt):
            return es.enter_context(nc.sbuf_tensor(name, list(shape), dt))

        PA = sb("PA", [128, 8, 8], F32)
        PB = sb("PB", [128, 8, 35], F32)
        GW = sb("GW", [128, 16, 128], BF16)
        IDF = sb("IDF", [128, 128], F32)
        IDB = sb("IDB", [128, 128], BF16)
        ONES = sb("ONES", [128, 128], BF16)
        MASK = sb("MASK", [128, 1], F32)
        EPSC = sb("EPSC", [128, 1], F32)
        ONE1 = sb("ONE1", [1, 128], F32)
        CL = sb("CL", [128, 8], F32)
        CLT = sb("CLT", [128, 8], F32)
        SS = sb("SS", [128, 16], F32)
        LNT = sb("LNT", [128, 16], F32)
        RS = sb("RS", [128, 16], F32)
        H0S = sb("H0S", [128, 8, NSEQ], F32)
        HSL = sb("HSL", [128, 8, NSEQ], F32)
        X3 = sb("X3", [128, 8, NSEQ * 3], F32)
        HFIN = sb("HFIN", [128, 8], F32)
        HINR = sb("HINR", [128, 8], F32)
        HIN = sb("HIN", [128, 8], F32)
        HFO = sb("HFO", [128, 8], F32)
        XAP3 = sb("XAP3", [128, 8, 3], F32)
        TMP16 = sb("TMP16", [128, NSEQ], F32)
        MIX = sb("MIX", [128, 16, NTOK], BF16)
        STG = sb("STG", [128, 1024], F32)
        STG2 = sb("STG2", [128, 1024], F32)
        SSQ = sb("SSQ", [128, 9, 4], F32)
        SSQ1 = sb("SSQ1", [128, 9], F32)
        LN2 = sb("LN2", [128, 9], F32)
        RS2 = sb("RS2", [128, 9], F32)

        main_bytes = (nc.sbuf_bytes_remaining - 1024) // 64 * 64
        MAINF = sb("MAIN", [128, main_bytes // 4], F32)

        class Carver:
            def __init__(self):
                self.off = 0

            def reset(self, base=0):
                self.off = base

            def f32(self, n):
                a = MAINF[:, self.off // 4: self.off // 4 + n]
                self.off += 4 * n
                assert self.off <= main_bytes, (self.off, main_bytes)
                return a

            def bf16(self, n):
                n2 = (n + 1) // 2
                a = MAINF[:, self.off // 4: self.off // 4 + n2].bitcast(BF16)
                self.off += 4 * n2
                assert self.off <= main_bytes, (self.off, main_bytes)
                return a[:, 0:n]

        CV = Carver()
        XNT = CV.bf16(16 * NT).rearrange("p (k t) -> p k t", k=16)
        WS = CV.bf16(NWS * 16 * 128).rearrange("p (s k j) -> p s k j", s=NWS, k=16)
        UBS = CV.bf16(8 * NSEQ * 38).rearrange("p (j s r) -> p j s r", j=8, s=NSEQ)
        EARLY_B = CV.off
        XS = CV.f32(8 * NSEQ * 11).rearrange("p (h s r) -> p h s r", h=8, s=NSEQ)
        HS = CV.f32(8 * NS).rearrange("p (h t) -> p h t", h=8)
        EARLY = CV.off
        WO_END = main_bytes - 16 * 512 * 2

        PS = [es.enter_context(nc.psum_tensor(f"ps{b}", [128, 512], F32)) for b in range(8)]
        sems = {e: es.enter_context(nc.semaphore(f"s_{e}")) for e in ("pe", "act", "dve", "pool")}
        lane_sems = [es.enter_context(nc.semaphore(f"lane{i}")) for i in range(NLANES)]
        cc_sem = es.enter_context(nc.semaphore("cc_sem"))

        bank_ctr = [0]

        def next_bank():
            b = bank_ctr[0] % 8
            bank_ctr[0] += 1
            return b

        def xkeys(c0, c1):
            ks = []
            for t in range(c0 // 128, (c1 - 1) // 128 + 1):
                ks += [("xnt", t, 0), ("xnt", t, 1)]
            return ks

        def dma(eng, out, in_, reads, writes, after=(), nofence=None, **kw):
            return S.op(eng, lambda e: e.dma_start(out=out, in_=in_, **kw), reads, writes, ndma=1, after=after,
                        nofence=nofence)

        def act(out, in_, func, reads, writes, bias=None, scale=None, accum_out=None):
            kw = {}
            if bias is not None:
                kw["bias"] = bias
            if scale is not None:
                kw["scale"] = scale
            if accum_out is not None:
                kw["accum_out"] = accum_out
            return S.op("act", lambda e: e.activation(out=out, in_=in_, func=func, **kw), reads, writes)

        def tt(eng, out, in0, in1, op, reads, writes):
            return S.op(eng, lambda e: e.tensor_tensor(out=out, in0=in0, in1=in1, op=op), reads, writes)

        def stt(out, in0, scalar, in1, op0, op1, reads, writes):
            return S.op("dve", lambda e: e.scalar_tensor_tensor(out=out, in0=in0, scalar=scalar, in1=in1,
                                                               op0=op0, op1=op1), reads, writes)

        def ts(eng, out, in0, s1, s2, op0, op1, reads, writes):
            if op1 is None:
                return S.op(eng, lambda e: e.tensor_scalar(out=out, in0=in0, scalar1=s1, scalar2=None, op0=op0),
                            reads, writes)
            return S.op(eng, lambda e: e.tensor_scalar(out=out, in0=in0, scalar1=s1, scalar2=s2, op0=op0, op1=op1),
                        reads, writes)

        def copy(eng, out, in_, reads, writes):
            if eng == "act":
                return S.op("act", lambda e: e.copy(out=out, in_=in_), reads, writes)
            return S.op(eng, lambda e: e.tensor_copy(out=out, in_=in_), reads, writes)

        def scan(out, a, b, initial, reads, writes):
            return S.op("dve", lambda e: e.tensor_tensor_scan(out=out, data0=a, data1=b, initial=initial,
                                                              op0=ALU.mult, op1=ALU.add), reads, writes)

        def mm_group(out, pairs, reads, writes):
            def fn(e):
                n = len(pairs)
                ins = None
                for i, (l, r) in enumerate(pairs):
                    ins = e.matmul(out, l, r, start=(i == 0), stop=(i == n - 1))
                return ins
            return S.op("pe", fn, reads, writes)

        def transposes(items, reads, writes):
            def fn(e):
                ins = None
                for (o, i, idn) in items:
                    ins = e.transpose(out=o, in_=i, identity=idn)
                return ins
            return S.op("pe", fn, reads, writes)

        wseq = list(range(0, 16)) + [24, 16, 25, 17, 32, 33]
        for j in range(2, 8):
            wseq += [24 + j, 16 + j]
        wseq += list(range(34, 40))
        wpos = {m: i for i, m in enumerate(wseq)}
        wloaded = [0]

        w_after = {}

        def w_prefetch(upto):
            while wloaded[0] <= min(upto, len(wseq) - 1):
                i = wloaded[0]
                m = wseq[i]
                s = i % NWS
                dma("pool", WS[:, s, :, :], win_d[m].rearrange("p (k j) -> p k j", k=16), [], [("ws", s)],
                    after=w_after.get(i, ()))
                wloaded[0] += 1

        inproj_done = {}

        def inproj(m, chunks, consume, cis=None):
            i = wpos[m]
            s = i % NWS
            w_prefetch(i)
            ndone = inproj_done.get(m, 0) + (len(chunks) if cis is None else len(cis))
            inproj_done[m] = ndone
            for ci, (c0, c1) in enumerate(chunks):
                if cis is not None and ci not in cis:
                    continue
                b = next_bank()
                pairs = [(WS[:, s, kc, :], XNT[:, kc, c0:c1]) for kc in range(16)]
                mm_group(PS[b][:, 0:c1 - c0], pairs, [("ws", s)] + xkeys(c0, c1), [("ps", b)])
                consume(ci, b, c0, c1)
            if ndone >= len(chunks):
                w_prefetch(i + NWS)

        CV.reset(EARLY)
        NXT = 5
        XT = [CV.f32(D) for _ in range(NXT)]
        XB = [CV.bf16(D) for _ in range(2)]
        STA = XT[2][:, 0:1024]
        STH = XT[3][:, 0:1024]
        GBC = CV.f32(D)

        JUNK = CV.bf16(D)
        NG1 = CV.f32(D)
        XA_OFF = CV.off
        XA = [CV.f32(HALO + NP) for _ in range(2)]
        ntile = (NT + 127) // 128
        xload = {}

        def p1_load(t):
            r0 = t * 128
            nr = min(128, NT - r0)
            xload[t] = dma("sp", XT[t % NXT][:nr, :], x_d[r0:r0 + nr, :], [], [("xt", t % NXT)])

        S.op("dve", lambda e: e.memset(EPSC[:, :], EPS), [], ["EPSC"])
        S.op("dve", lambda e: e.memset(ONES[:, :], 1.0), [], ["ONES"])
        dma("sp", NG1[0:1, :], ng_d.unsqueeze(0), [], ["NG1"])
        p1_load(0)
        dma("sp", IDF[:, :], ident_d, [], ["IDF"])
        S.op("dve", lambda e: e.memset(ONE1[:, :], 1.0), [], ["ONE1"])
        for q in range(4):
            b = next_bank()
            S.op("pe", (lambda e, q=q, b=b: e.matmul(PS[b][:, :], ONE1[0:1, :], NG1[0:1, q * 512:(q + 1) * 512],
                                                    start=True, stop=True)), ["ONE1", "NG1"], [("ps", b)])
            copy("dve", GBC[:, q * 512:(q + 1) * 512], PS[b][:, :], [("ps", b)], ["GBC"])
        copy("dve", IDB[:, :], IDF[:, :], ["IDF"], ["IDB"])
        p1_load(1)
        p1_load(2)
        p1_load(3)
        p1_load(4)
        dma("sp", PA[:, :, :], pa_d.rearrange("p (h k) -> p h k", h=8), [], ["PA"])
        dma("sp", PB[:, :, :], pb_d.rearrange("p (h k) -> p h k", h=8), [], ["PB"])
        dma("sp", MASK[:, :], mask_d, [], ["MASK"])

        def p1_sq(t):
            r0 = t * 128
            nr = min(128, NT - r0)
            xs_ = t % NXT
            act(JUNK[:nr, :], XT[xs_][:nr, :], AF.Square, [("xt", xs_)], ["JUNK", ("ss", t)],
                accum_out=SS[:nr, t:t + 1])
            act(LNT[:nr, t:t + 1], SS[:nr, t:t + 1], AF.Ln, [("ss", t), "EPSC"], [("lnt", t)],
                bias=EPSC[:nr, :], scale=1.0 / D)
            act(RS[:nr, t:t + 1], LNT[:nr, t:t + 1], AF.Exp, [("lnt", t)], [("rs", t)], scale=-0.5)

        def p1_stt(t):
            r0 = t * 128
            nr = min(128, NT - r0)
            xs_, bs_ = t % NXT, t % 2
            stt(XB[bs_][:nr, :], XT[xs_][:nr, :], RS[:nr, t:t + 1], GBC[:nr, :], ALU.mult, ALU.mult,
                [("xt", xs_), ("rs", t), "GBC"], [("xb", bs_)])
            if t + NXT < ntile:
                p1_load(t + NXT)
            if t == 4:
                dma("pool", GW[:, :, :], gw_d.rearrange("p (g j) -> p g j", g=16), [], ["GW"], after=[xload[ntile - 1]])
            if t == 1:
                w_after[0] = [xload[5]]
                w_prefetch(0)
            if t in (4, 5, 6):
                w_after[t - 3] = [xload[8 if t == 4 else ntile - 1]]
                w_prefetch(t - 3)

        def p1_tr(t):
            r0 = t * 128
            nr = min(128, NT - r0)
            bs_ = t % 2
            for half in range(2):
                b = next_bank()
                psb = PS[b][:, :].bitcast(BF16)
                items = [(psb[:, s * 128: s * 128 + nr], XB[bs_][:nr, (half * 8 + s) * 128:(half * 8 + s + 1) * 128],
                          IDB[:nr, :nr]) for s in range(8)]
                transposes(items, [("xb", bs_), "IDB"], [("ps", b)])
                src = psb.rearrange("p (s c) -> p s c", s=8)[:, :, 0:nr]
                dst = XNT[:, half * 8:(half + 1) * 8, r0:r0 + nr]
                copy("act" if half == 0 else "dve", dst, src, [("ps", b)], [("xnt", t, half)])

        def a1_s0(h, cis=None):
            sl = h % 2

            def consume(ci, b, c0, c1):
                if ci < 2:
                    copy("act", XA[sl][:, c0:c1], PS[b][:, 0:c1 - c0], [("ps", b)], [("xa", sl, ci)])
                else:
                    copy("act", XA[sl][:, 1024:HALO + NP], PS[b][:, 0:HALO], [("ps", b)], [("xa", sl, 2)])
                    copy("act", XS[:, h, :, 3:11], PS[b][:, HALO:HALO + NS].rearrange("p (s t) -> p s t", s=NSEQ),
                         [("ps", b), "XShist"], [("xs", h)])
            inproj(h, HCH, consume, cis)

        early_xa = {7: (0, 0), 8: (0, 1)}
        xa_done = {h: set() for h in range(8)}
        p1_sq(0)
        p1_sq(1)
        p1_stt(0)
        for t in range(ntile):
            if t + 2 < ntile:
                p1_sq(t + 2)
            if t + 1 < ntile:
                p1_stt(t + 1)
            p1_tr(t)
            if t in early_xa:
                h_, c_ = early_xa[t]
                a1_s0(h_, cis=[c_])
                xa_done[h_].add(c_)
        act(CLT[:, :], PA[:, :, 7], AF.Exp, ["PA"], ["CLT"], scale=-1.0)
        act(CL[:, :], CLT[:, :], AF.Ln, ["CLT"], ["CL0"], bias=1.0)
        ts("dve", CL[:, :], CL[:, :], -8.0, None, ALU.mult, None, ["CL0"], ["CL"])

        dma("sp", STH[:NSEQ, :], sth_d, [], [("xt", 3)])
        dma("sp", STA[:NSEQ * 3, :], stca_d, [], [("xt", 2)])
        b = next_bank()
        transposes([(PS[b][:, h * NSEQ:(h + 1) * NSEQ], STH[:NSEQ, h * 128:(h + 1) * 128], IDF[:NSEQ, :NSEQ])
                    for h in range(8)], [("xt", 3), "IDF"], [("ps", b)])
        copy("dve", H0S[:, :, :], PS[b][:, 0:8 * NSEQ].rearrange("p (h s) -> p h s", h=8), [("ps", b)], ["H0S"])
        b = next_bank()
        transposes([(PS[b][:, h * 48:(h + 1) * 48], STA[:48, h * 128:(h + 1) * 128], IDF[:48, :48])
                    for h in range(8)], [("xt", 2), "IDF"], [("ps", b)])
        copy("dve", XS[:, :, :, 0:3], PS[b][:, 0:8 * 48].rearrange("p (h s r) -> p h s r", h=8, s=NSEQ),
             [("ps", b)], ["XShist"])
        STCB = [NG1[:, 0:1024], NG1[:, 1024:2048]]

        def stcb_load(rt):
            sl = rt % 2
            dma("sp", STCB[sl][:120, :], stcb_d[rt * 120:(rt + 1) * 120, :], [], [("stcb", sl), "NG1"])

        def stcb_tile(rt):
            sl = rt % 2
            for half in range(2):
                b = next_bank()
                transposes([(PS[b][:, jj * 120:(jj + 1) * 120],
                             STCB[sl][:120, (half * 4 + jj) * 128:(half * 4 + jj + 1) * 128],
                             IDF[:120, :120]) for jj in range(4)], [("stcb", sl), "IDF"], [("ps", b)])
                copy("act", UBS[:, half * 4:(half + 1) * 4, rt * 4:(rt + 1) * 4, 0:30],
                     PS[b][:, 0:480].rearrange("p (j s r) -> p j s r", j=4, s=4), [("ps", b)], [("ubsh", rt, half)])

        S.fence()
        CV.reset(EARLY)
        XC = [CV.f32(NTOK) for _ in range(2)]
        XCB = [CV.bf16(NTOK) for _ in range(2)]
        RR = CV.f32(NTOK)
        II = CV.f32(NTOK)
        T1 = CV.f32(NTOK)
        AT = [CV.f32(NTOK) for _ in range(2)]
        BT = [CV.f32(NTOK) for _ in range(2)]
        HSCR = CV.f32(NP)
        assert CV.off <= XA_OFF, (CV.off, XA_OFF)

        def a1_s1(h):
            sl = h % 2
            xa_r = [("xa", sl, 0), ("xa", sl, 1), ("xa", sl, 2), "PA"]
            ts("dve", XC[sl][:, 0:NP], XA[sl][:, HALO:HALO + NP], PA[:, h, 3:4], PA[:, h, 4:5], ALU.mult, ALU.add,
               xa_r, [("xc", sl)])
            for k in range(3):
                stt(XC[sl][:, 0:NP], XA[sl][:, HALO - 3 + k:HALO - 3 + k + NP], PA[:, h, k:k + 1], XC[sl][:, 0:NP],
                    ALU.mult, ALU.add, xa_r + [("xc", sl)], [("xc", sl)])
            xcs = XC[sl][:, NP:NTOK].rearrange("p (s t) -> p s t", s=NSEQ)
            ts("dve", xcs, XS[:, h, :, 3:11], PA[:, h, 3:4], PA[:, h, 4:5], ALU.mult, ALU.add,
               [("xs", h), "XShist", "PA"], [("xcs", sl)])
            for k in range(3):
                stt(xcs, XS[:, h, :, k:k + 8], PA[:, h, k:k + 1], xcs, ALU.mult, ALU.add,
                    [("xs", h), "XShist", "PA", ("xcs", sl)], [("xcs", sl)])
            copy("dve", XCB[sl][:, :], XC[sl][:, :], [("xc", sl), ("xcs", sl)], [("xcb", sl)])
            copy("act", XAP3[:, h, :], XA[sl][:, HALO + NP - 3:HALO + NP], xa_r, [("xap3", h)])

        GCH = [(0, 512), (512, 1024), (1024, NTOK)]

        def a1_s2(h):
            sl = h % 2
            for g, dst, key, bcol in ((0, RR, "RR", 5), (1, II, "II", 6)):
                for ci, (c0, c1) in enumerate(GCH):
                    b = next_bank()
                    mm_group(PS[b][:, 0:c1 - c0], [(GW[:, g * 8 + h, :], XCB[sl][:, c0:c1])], ["GW", ("xcb", sl)],
                             [("ps", b)])
                    act(dst[:, c0:c1], PS[b][:, 0:c1 - c0], AF.Sigmoid, [("ps", b), "PA"], [(key, ci)],
                        bias=PA[:, h, bcol:bcol + 1])

        def a1_s3(h):
            sl = h % 2
            rk = [("RR", i) for i in range(3)]
            ik = [("II", i) for i in range(3)]
            atk = [("at", sl), ("ats", sl)]
            btk = [("bt", sl), ("bts", sl)]
            act(AT[sl][:, :], RR[:, :], AF.Exp, rk + ["CL"], atk, scale=CL[:, h:h + 1])
            act(T1[:, :], AT[sl][:, :], AF.Square, atk, ["T1"])
            act(T1[:, :], T1[:, :], AF.Ln, ["T1"], ["T1"], scale=-1.0, bias=1.0)
            act(T1[:, :], T1[:, :], AF.Exp, ["T1"], ["T1"], scale=0.5)
            tt("dve", BT[sl][:, :], II[:, :], XC[sl][:, :], ALU.mult, ik + [("xc", sl), ("xcs", sl)], btk)

        def a1_s4(h):
            sl = h % 2
            btk = [("bt", sl), ("bts", sl)]
            tt("dve", BT[sl][:, :], BT[sl][:, :], T1[:, :], ALU.mult, btk + ["T1"], btk)
            dma("sp", a_sc[h, :, :], AT[sl][:, 0:NP], [("at", sl)], [("asc", h)])
            dma("sp", b_sc[h, :, :], BT[sl][:, 0:NP], [("bt", sl)], [("bsc", h)])
            scan(HSCR[:, :], AT[sl][:, 0:NP], BT[sl][:, 0:NP], 0.0, [("at", sl), ("bt", sl)], ["HSCR"])
            copy("dve", HFIN[:, h:h + 1], HSCR[:, NP - 1:NP], ["HSCR"], [("hfin", h)])
            a3 = AT[sl][:, NP:NTOK].rearrange("p (s t) -> p s t", s=NSEQ)
            b3 = BT[sl][:, NP:NTOK].rearrange("p (s t) -> p s t", s=NSEQ)
            tt("dve", TMP16[:, :], a3[:, :, 0], H0S[:, h, :], ALU.mult, [("ats", sl), "H0S"], ["TMP16"])
            tt("dve", b3[:, :, 0], b3[:, :, 0], TMP16[:, :], ALU.add, [("bts", sl), "TMP16"], [("bts", sl)])
            S.op("dve", lambda e: e.memset(a3[:, :, 0], 0.0), [("ats", sl)], [("ats", sl)])
            scan(HS[:, h, :], AT[sl][:, NP:NTOK], BT[sl][:, NP:NTOK], 0.0, [("ats", sl), ("bts", sl)], [("hs", h)])

        def a2_t0(h):
            def consume(ci, b, c0, c1):
                act(MIX[:, h, c0 - HALO:c1 - HALO], PS[b][:, 0:c1 - c0], AF.Silu, [("ps", b)], [("mix", h, ci)])
            inproj(8 + h, NCHK, consume)

        for i in range(-3, 8):
            if 0 <= i < 8:
                a1_s3(i)
            if 0 <= i + 2 < 8:
                a1_s1(i + 2)
            if 0 <= i < 8:
                a1_s4(i)
            if 0 <= i + 1 < 8:
                a1_s2(i + 1)
            if 0 <= i + 3 < 8:
                a1_s0(i + 3, cis=[c for c in range(3) if c not in xa_done[i + 3]])
            elif i + 3 >= 8:
                a2_t0(i + 3 - 8)
            if i == 0:
                stcb_load(0)
                stcb_load(1)
            if 1 <= i <= 4:
                stcb_tile(i - 1)
                if i + 1 <= 3:
                    stcb_load(i + 1)

        S.fence()
        CV.reset(EARLY)
        NAB = 3
        ABL = [CV.f32(2 * NP) for _ in range(NAB)]
        HP = [CV.f32(NP) for _ in range(2)]

        def a_out_gather():
            hsk = [("hs", h) for h in range(8)]
            copy("act", HSL[:, :, :], HS[:, :, :].rearrange("p h (s t) -> p h s t", s=NSEQ)[:, :, :, TS - 1], hsk, ["HSL"])
            copy("act", X3[:, :, :].rearrange("p h (s r) -> p h s r", s=NSEQ), XS[:, :, :, 8:11],
                 [("xs", h) for h in range(8)], ["X3"])

        def a_outputs():
            S.nofence_default = True
            for half in range(2):
                b = next_bank()
                transposes([(PS[b][:NSEQ, q * 128:(q + 1) * 128], HSL[:, half * 4 + q, :], IDF[:, :]) for q in range(4)],
                           ["HSL", "IDF"], [("ps", b)])
                copy("act", STG[:NSEQ, half * 512:(half + 1) * 512], PS[b][:NSEQ, :], [("ps", b)], ["STG"])
            dma("sp", ohs_d, STG[:NSEQ, :], ["STG"], ["ohs"])
            for half in range(2):
                b = next_bank()
                transposes([(PS[b][:48, q * 128:(q + 1) * 128], X3[:, half * 4 + q, :], IDF[:, :]) for q in range(4)],
                           ["X3", "IDF"], [("ps", b)])
                copy("act", STG2[:48, half * 512:(half + 1) * 512], PS[b][:48, :], [("ps", b)], ["STG2"])
            dma("sp", ocas_d, STG2[:48, :], ["STG2"], ["ocas"])
            for half in range(2):
                b = next_bank()
                transposes([(PS[b][:3, q * 128:(q + 1) * 128], XAP3[:, half * 4 + q, :], IDF[:, :]) for q in range(4)],
                           [("xap3", h) for h in range(8)] + ["IDF"], [("ps", b)])
                copy("act", STG[:3, half * 512:(half + 1) * 512], PS[b][:3, :], [("ps", b)], ["STG"])
            dma("sp", ocap_d, STG[:3, :], ["STG"], ["ocap"])
            b = next_bank()
            transposes([(PS[b][:8, 0:128], HFO[:, :], IDF[:, :])], [("hfo", h) for h in range(8)] + ["IDF"], [("ps", b)])
            copy("act", STG2[:8, 0:128], PS[b][:8, 0:128], [("ps", b)], ["STG2"])
            dma("sp", ohp_d, STG2[:8, 0:128], ["STG2"], ["ohp"])
            S.nofence_default = False

        def a2_load(h):
            sl = h % NAB
            dma("sp", ABL[sl][:, 0:NP], a_sc[h, :, :], [("asc", h)], [("abla", sl)])
            dma("sp", ABL[sl][:, NP:2 * NP], b_sc[h, :, :], [("bsc", h)], [("ablb", sl)])

        def a2_t1(h):
            sl = h % 2
            al = h % NAB
            scan(HP[sl][:, :], ABL[al][:, 0:NP], ABL[al][:, NP:2 * NP], HIN[:, h:h + 1],
                 [("abla", al), ("ablb", al), "HIN"], [("hp", sl)])
            mk = [("mix", h, i) for i in range(3)]
            tt("dve", MIX[:, h, 0:NP], HP[sl][:, :], MIX[:, h, 0:NP], ALU.mult, [("hp", sl)] + mk, [("mixp", h)])
            tt("dve", MIX[:, h, NP:NTOK], HS[:, h, :], MIX[:, h, NP:NTOK], ALU.mult, [("hs", h)] + mk, [("mixs", h)])
            copy("dve", HFO[:, h:h + 1], HP[sl][:, NP - 1:NP], [("hp", sl)], [("hfo", h)])

        a2_load(0)
        a2_load(1)
        a2_load(2)
        a_out_gather()
        hk = [("hfin", h) for h in range(8)]
        dma("pool", cc_in[:, :], HFIN[:, :], hk, ["cc_in"])
        S.op("pool", lambda e: e.collective_compute("AllGather", ALU.bypass,
                                                    replica_groups=[[0, 1], [2, 3], [4, 5], [6, 7]],
                                                    ins=[cc_in.ap().opt()], outs=[cc_out.ap().opt()]),
             ["cc_in"], ["cc_out"], cc=True)
        dma("sp", HINR[:, :], cc_out[0:128, :], ["cc_out"], ["HINR"])
        ts("dve", HIN[:, :], HINR[:, :], MASK[:, 0:1], None, ALU.mult, None, ["HINR", "MASK"], ["HIN"])

        for h in range(8):
            if h + 3 < 8:
                a2_t0(h + 3)
            a2_t1(h)
            if h + 3 < 8:
                a2_load(h + 3)

        KD = 14
        NPT = 31 - KD
        CV.reset(EARLY_B)
        CB = CV.bf16(8 * NTOK).rearrange("p (j t) -> p j t", j=8)
        ZZ = [CV.f32(NTOK) for _ in range(2)]
        MEAN = CV.f32(NTOK)
        RSTD = CV.f32(NTOK)
        ACC = CV.f32(NTOK)
        SQ = [CV.bf16(512) for _ in range(2)]
        assert CV.off >= EARLY + 3 * 2 * NP * 4 + 2 * NP * 4, CV.off
        S.late_keys([("cb", j, ci) for j in range(8) for ci in range(3)] + [("zz", 0), ("zz", 1)] +
                    [("m2", i) for i in range(3)] + [("mean", i) for i in range(3)] +
                    [("rstd", i) for i in range(3)] + [("sq", i) for i in range(2)] + ["ACC", "ACCs"])
        SG = [CV.f32(NT) for _ in range(2)]
        UU = [CV.f32(NT) for _ in range(2)]
        UBP = [CV.bf16(HALO + NP) for _ in range(2)]
        DG = [CV.bf16(NPT * 128).rearrange("p (k c) -> p k c", k=NPT) for _ in range(2)]
        assert CV.off <= WO_END, (CV.off, WO_END)

        def b1_u0(j):
            sl = j % 2
            def dg_fn(e):
                ins = None
                for k in range(KD, 31):
                    ins = e.activation(out=DG[sl][:, k - KD, :], in_=IDB[:, :], func=AF.Identity, scale=PB[:, j, k:k + 1])
                return ins
            S.op("act", dg_fn, ["IDB", "PB"], [("dg", sl)])

            def cons_g(ci, b, c0, c1):
                act(SG[sl][:, c0:c1], PS[b][:, 0:c1 - c0], AF.Sigmoid, [("ps", b)], [("sg", sl, ci)])
            inproj(24 + j, HCH, cons_g)

            def cons_v(ci, b, c0, c1):
                copy("act", UU[sl][:, c0:c1], PS[b][:, 0:c1 - c0], [("ps", b)], [("uu", sl, ci)])
            inproj(16 + j, HCH, cons_v)
            for ci, (c0, c1) in enumerate(HCH):
                tt("dve", UU[sl][:, c0:c1], UU[sl][:, c0:c1], SG[sl][:, c0:c1], ALU.mult,
                   [("uu", sl, ci), ("sg", sl, ci)], [("uu", sl, ci)])
            uk = [("uu", sl, 0), ("uu", sl, 1), ("uu", sl, 2)]
            ceng = "dve"
            copy(ceng, UBP[sl][:, :], UU[sl][:, 0:HALO + NP], uk, [("ubp", sl)])
            copy(ceng, UBS[:, j, :, 30:38], UU[sl][:, HALO + NP:NT].rearrange("p (s t) -> p s t", s=NSEQ), uk,
                 [("ubsn", j)])

        def b2_proj(j):
            def cons(ci, b, c0, c1):
                act(MIX[:, 8 + j, c0 - HALO:c1 - HALO], PS[b][:, 0:c1 - c0], AF.Silu, [("ps", b)], [("mixb", j, ci)])
            inproj(32 + j, NCHK, cons)

        def b1_taps(j):
            sl = j % 2
            uk = [("uu", sl, 0), ("uu", sl, 1), ("uu", sl, 2)]
            ubsk = [("ubsn", j)] + [("ubsh", rt, j // 4) for rt in range(4)]
            accs = ACC[:, NP:NTOK].rearrange("p (s t) -> p s t", s=NSEQ)
            if j == 0:
                ts("dve", ACC[:, 0:NP], UU[sl][:, 2:2 + NP], PB[:, j, 0:1], None, ALU.mult, None, uk + ["PB"], ["ACC"])
                ts("dve", accs, UBS[:, j, :, 0:8], PB[:, j, 0:1], None, ALU.mult, None, ubsk + ["PB"], ["ACCs"])
            else:
                act(ACC[:, 0:NP], UU[sl][:, 2:2 + NP], AF.Identity, uk + ["PB"], ["ACC"], scale=PB[:, j, 0:1])
                act(accs, UBS[:, j, :, 0:8], AF.Identity, ubsk + ["PB"], ["ACCs"], scale=PB[:, j, 0:1])
            for k in range(1, KD):
                stt(ACC[:, 0:NP], UU[sl][:, 2 + k:2 + k + NP], PB[:, j, k:k + 1], ACC[:, 0:NP], ALU.mult, ALU.add,
                    uk + ["PB", "ACC"], ["ACC"])
                stt(accs, UBS[:, j, :, k:k + 8], PB[:, j, k:k + 1], accs, ALU.mult, ALU.add,
                    ubsk + ["PB", "ACCs"], ["ACCs"])

        def b1_u1(j):
            sl = j % 2
            uk = [("uu", sl, 0), ("uu", sl, 1), ("uu", sl, 2)]
            b = next_bank()
            transposes([(PS[b][:30, 0:128], UU[sl][:, HALO + NP - 30:HALO + NP], IDF[:, :]),
                        (PS[b][:, 128:256], UU[sl][:, HALO + NP:NT], IDF[:, :])], uk + ["IDF"], [("ps", b)])
            copy("act", STG[:30, j * 128:(j + 1) * 128], PS[b][:30, 0:128], [("ps", b)], ["STG"])
            copy("act", STG2[:, j * 128:(j + 1) * 128], PS[b][:, 128:256], [("ps", b)], ["STG2"])
            for ci, (c0, c1) in enumerate([(0, 512), (512, 1024)]):
                b = next_bank()
                pairs = [(DG[sl][:, k - KD, :], UBP[sl][:, c0 + k + 2:c0 + k + 2 + 512]) for k in range(KD, 31)]
                mm_group(PS[b][:, :], pairs, [("dg", sl), ("ubp", sl)], [("ps", b)])
                stt(CB[:, j, c0:c1], PS[b][:, :], PB[:, j, 31:32], ACC[:, c0:c1], ALU.add, ALU.add,
                    [("ps", b), "PB", "ACC"], [("cb", j, ci)])
            b = next_bank()
            pairs = [(DG[sl][:, k - KD, :], UBS[:, j, :, k:k + 8]) for k in range(KD, 31)]
            mm_group(PS[b][:, 0:NS].rearrange("p (s t) -> p s t", s=NSEQ), pairs,
                     [("dg", sl), ("ubsn", j)] + [("ubsh", rt, j // 4) for rt in range(4)], [("ps", b)])
            stt(CB[:, j, NP:NTOK], PS[b][:, 0:NS], PB[:, j, 31:32], ACC[:, NP:NTOK], ALU.add, ALU.add,
                [("ps", b), "PB", "ACCs"], [("cb", j, 2)])

        b1_u0(0)
        for j in range(8):
            b1_taps(j)
            if j + 1 < 8:
                b1_u0(j + 1)
            if j == 0:
                b2_proj(0)
                b2_proj(1)
                a_outputs()
            if j == 2:
                dma("sp", ocbh_d, stcb_d.rearrange("(s r) c -> s r c", r=30)[:, 8:30, :], [], ["ocbh"], nofence=True)
            if 2 <= j <= 5:
                n_ = j - 2
                dma("pool", wo_bf[n_, :, :].rearrange("p (k j) -> p k j", k=16),
                    wout_d[n_].rearrange("p (k j) -> p k j", k=16), [], [("wobf", n_)], nofence=True)
            if j == 7:
                b2_proj(2)
                b2_proj(3)
            b1_u1(j)
        S.nofence_default = True
        dma("sp", ocbp_d, STG[:30, :], ["STG"], ["ocbp"])
        dma("sp", ocbn_d, STG2[:, :], ["STG2"], ["ocbn"])
        S.nofence_default = False

        SCH = [(0, 512), (512, 1024), (1024, NTOK)]
        sbanks = {}
        for ci, (c0, c1) in enumerate(SCH):
            n = c1 - c0
            b1_ = next_bank()
            b2_ = next_bank()
            sbanks[ci] = (b1_, b2_)
            for j in range(8):
                sq = j % 2
                if j % 2 == 0:
                    act(SQ[sq][:, 0:n], CB[:, j, c0:c1], AF.Square, [("cb", j, ci)], [("sq", sq)])
                else:
                    tt("dve", SQ[sq][:, 0:n], CB[:, j, c0:c1], CB[:, j, c0:c1], ALU.mult, [("cb", j, ci)], [("sq", sq)])
                S.op("pe", (lambda e, j=j, b=b1_, c0=c0, c1=c1, n=n:
                            e.matmul(PS[b][:, 0:n], ONES[:, :], CB[:, j, c0:c1], start=(j == 0), stop=(j == 7))),
                     ["ONES", ("cb", j, ci)], [("ps", b1_)] if j == 0 else [("psacc", b1_, j)])
                S.op("pe", (lambda e, j=j, b=b2_, sq=sq, n=n:
                            e.matmul(PS[b][:, 0:n], ONES[:, :], SQ[sq][:, 0:n], start=(j == 0), stop=(j == 7))),
                     ["ONES", ("sq", sq)], [("ps", b2_)] if j == 0 else [("psacc", b2_, j)])
        mk = [("mean", ci) for ci in range(3)]
        rk_ = [("rstd", ci) for ci in range(3)]
        m2k = [("m2", ci) for ci in range(3)]
        for ci, (c0, c1) in enumerate(SCH):
            b1_, _b2 = sbanks[ci]
            k1 = [("ps", b1_)] + [("psacc", b1_, j) for j in range(1, 8)]
            act(MEAN[:, c0:c1], PS[b1_][:, 0:c1 - c0], AF.Identity, k1, [("mean", ci)], scale=1.0 / WA)
        tt("dve", ZZ[0][:, :], MEAN[:, :], MEAN[:, :], ALU.mult, mk, m2k)
        for ci, (c0, c1) in enumerate(SCH):
            _b1, b2_ = sbanks[ci]
            k2 = [("ps", b2_)] + [("psacc", b2_, j) for j in range(1, 8)]
            stt(RSTD[:, c0:c1], PS[b2_][:, 0:c1 - c0], 1.0 / WA, ZZ[0][:, c0:c1], ALU.mult, ALU.subtract,
                k2 + m2k, [("rstd", ci)])
        act(RSTD[:, :], RSTD[:, :], AF.Ln, rk_ + ["EPSC"], rk_, bias=EPSC[:, :])
        act(RSTD[:, :], RSTD[:, :], AF.Exp, rk_, rk_, scale=-0.5)
        stt(MEAN[:, :], MEAN[:, :], -1.0, RSTD[:, :], ALU.mult, ALU.mult, mk + rk_, mk)
        stat_k = [("rstd", i) for i in range(3)] + [("mean", i) for i in range(3)]

        def b2_rest_a(j):
            sl = j % 2
            cbk = [("cb", j, i) for i in range(3)]
            zk = [("zz", sl)] + ([("m2", i) for i in range(3)] if sl == 0 else [])
            tt("dve", ZZ[sl][:, :], CB[:, j, :], RSTD[:, :], ALU.mult, cbk + stat_k, zk)
            tt("dve", ZZ[sl][:, :], ZZ[sl][:, :], MEAN[:, :], ALU.add, zk + stat_k, zk)
            act(ZZ[sl][:, :], ZZ[sl][:, :], AF.Silu, zk + ["PB"], zk, bias=PB[:, j, 33:34], scale=PB[:, j, 32:33])

        def b2_rest_b(j):
            sl = j % 2
            zk = [("zz", sl)] + ([("m2", i) for i in range(3)] if sl == 0 else [])
            tt("dve", MIX[:, 8 + j, :], ZZ[sl][:, :], MIX[:, 8 + j, :], ALU.mult,
               zk + [("mixb", j, i) for i in range(3)], [("mixB", j)])

        def wo_view(off):
            return MAINF[:, off // 4: off // 4 + 16 * 512 // 2].bitcast(BF16).rearrange("p (k n) -> p k n", k=16)

        WO = [wo_view(WO_END), None, None]
        WSL = {0: 0, 1: 1, 2: 2, 3: 0}

        def wo_load(n):
            src = wo_bf[n, :, :].rearrange("p (k j) -> p k j", k=16)
            for hf in range(2):
                dma("pool", WO[WSL[n]][:, hf * 8:(hf + 1) * 8, :], src[:, hf * 8:(hf + 1) * 8, :], [("wobf", n)],
                    [("wo", WSL[n], hf)])

        wo_load(0)
        b2_rest_a(0)
        for j in range(8):
            if j + 1 < 8:
                b2_rest_a(j + 1)
            b2_rest_b(j)
            if j + 4 < 8:
                b2_proj(j + 4)

        S.fence()
        CV.reset(0)
        HRES = CV.f32(9 * D).rearrange("p (i d) -> p i d", i=9)
        WO[1] = CV.bf16(16 * 512).rearrange("p (k n) -> p k n", k=16)
        WO[2] = CV.bf16(16 * 512).rearrange("p (k n) -> p k n", k=16)
        NXRE = 6
        XRE = [CV.f32(512) for _ in range(NXRE)]
        SQJ = CV.bf16(512)
        FGB = CV.f32(D)
        assert CV.off <= WO_END, (CV.off, WO_END)
        wo_load(1)
        allmix = []
        for h in range(8):
            allmix += [("mixp", h), ("mixs", h)]
        for j in range(8):
            allmix += [("mixB", j)]
        xi = 0
        amix = []
        for h in range(8):
            amix += [("mixp", h), ("mixs", h)]
        bmix = [("mixB", j) for j in range(8)]

        def mm_part(out, pairs, first, last, reads, writes):
            def fn(e):
                ins = None
                n_ = len(pairs)
                for q, (l, r) in enumerate(pairs):
                    ins = e.matmul(out, l, r, start=(first and q == 0), stop=(last and q == n_ - 1))
                return ins
            return S.op("pe", fn, reads, writes)

        pend = [None]

        def fin_a(i):
            S.op("dve", lambda e: e.tensor_reduce(out=SSQ1[:, i:i + 1], in_=SSQ[:, i, :],
                                                  axis=mybir.AxisListType.X, op=ALU.add),
                 [("ssq", i, q) for q in range(4)], [("ssq1", i)])
            act(LN2[:, i:i + 1], SSQ1[:, i:i + 1], AF.Ln, [("ssq1", i), "EPSC"], [("ln2", i)],
                bias=EPSC[:, :], scale=1.0 / D)
            act(RS2[:, i:i + 1], LN2[:, i:i + 1], AF.Exp, [("ln2", i)], [("rs2", i)], scale=-0.5)

        def fin_b(i):
            hk = [("hres", i, q) for q in range(4)]
            if i < 8:
                stt(HRES[:, i, :], HRES[:, i, :], RS2[:, i:i + 1], FGB[:, :], ALU.mult, ALU.mult,
                    hk + [("rs2", i), "FGB"], hk)
                dma("pool", y_d[i * 128:(i + 1) * 128, :], HRES[:, i, :], hk, [("y", i)])
            else:
                for hf in range(2):
                    cs = slice(hf * 1024, (hf + 1) * 1024)
                    hk2 = [("hres", i, 2 * hf), ("hres", i, 2 * hf + 1)]
                    stt(HRES[:, i, cs], HRES[:, i, cs], RS2[:, i:i + 1], FGB[:, cs], ALU.mult, ALU.mult,
                        hk2 + [("rs2", i), "FGB"], hk2)
                    dma("pool", y_d[i * 128:(i + 1) * 128, cs], HRES[:, i, cs], hk2, [("y", i, hf)])

        G = 4
        ph1 = [(n, i) for n in range(2) for i in range(9)]
        ph2 = [(n, i) for i in range(9) for n in (2, 3)]
        G0 = 7
        batches = [ph1[0:G0]] + [ph1[g0:g0 + G] for g0 in range(G0, len(ph1), G)] + \
                  [ph2[g0:g0 + G] for g0 in range(0, len(ph2), G)]
        for batch in batches:
            banks = []
            for (n, i) in batch:
                b = next_bank()
                banks.append(b)
                pairs = [(MIX[:, kc, i * 128:(i + 1) * 128], WO[WSL[n]][:, kc, :]) for kc in range(8)]
                mm_part(PS[b][:, :], pairs, True, False, amix + [("wo", WSL[n], 0)], [("ps", b)])
            for (n, i), b in zip(batch, banks):
                pairs = [(MIX[:, kc, i * 128:(i + 1) * 128], WO[WSL[n]][:, kc, :]) for kc in range(8, 16)]
                mm_part(PS[b][:, :], pairs, False, True, bmix + [("wo", WSL[n], 1)], [("ps2", b)])
                xs_ = xi % NXRE
                xi += 1
                r0 = HALO + i * 128
                dma("sp", XRE[xs_][:, :], x_d[r0:r0 + 128, n * 512:(n + 1) * 512], [], [("xre", xs_)])
                tt("dve", HRES[:, i, n * 512:(n + 1) * 512], PS[b][:, :], XRE[xs_][:, :], ALU.add,
                   [("ps", b), ("ps2", b), ("xre", xs_)], [("hres", i, n)])
                act(SQJ[:, :], HRES[:, i, n * 512:(n + 1) * 512], AF.Square, [("hres", i, n)], ["SQJ", ("ssq", i, n)],
                    accum_out=SSQ[:, i, n:n + 1])
                if n == 3:
                    if pend[0] is not None:
                        fin_b(pend[0])
                    fin_a(i)
                    pend[0] = i
                if i == 4 and n == 0:
                    wo_load(2)
                if i == 0 and n == 1:
                    dma("sp", FGB[:, :], fg_d.partition_broadcast(128), [], ["FGB"])
                if i == 8 and n == 0:
                    wo_load(3)

        fin_b(pend[0])

        S.finalize()
        with nc.Block(no_gpsimd_drain=True) as block:
            @block.tensor
            def _(e):
                S.emit("pe", e, sems, lane_sems, cc_sem)

            @block.scalar
            def _(e):
                S.emit("act", e, sems, lane_sems, cc_sem)

            @block.vector
            def _(e):
                S.emit("dve", e, sems, lane_sems, cc_sem)

            @block.gpsimd
            def _(e):
                S.emit("pool", e, sems, lane_sems, cc_sem)

            @block.sync
            def _(e):
                S.emit("sp", e, sems, lane_sems, cc_sem, final_wait=True)
    return nc


_NC_CACHE = {}


def kernel(x_prompt, x_sample, state_lru_h, state_lru_conv, state_glu_conv,
           norm_gain, w_in, conv_a_w, conv_a_b, gate_a_w, gate_a_b, gate_x_w, gate_x_b,
           lru_param, conv_b_w, conv_b_b, ln_b_gain, ln_b_bias, w_out, final_gain):
    f = np.float32
    x_prompt = np.asarray(x_prompt, f)
    x_sample = np.asarray(x_sample, f)
    sth = np.asarray(state_lru_h, f)[0]
    stca = np.asarray(state_lru_conv, f)[0]
    stcb = np.asarray(state_glu_conv, f)[0]
    w_in = np.asarray(w_in, f)[0]
    w_out = np.asarray(w_out, f)[0]

    win_r = np.ascontiguousarray(w_in.reshape(16, 128, 40, 128).transpose(2, 1, 0, 3).reshape(40, 128, 2048))
    wout_r = np.ascontiguousarray(w_out.reshape(16, 128, 4, 512).transpose(2, 1, 0, 3).reshape(4, 128, 16 * 512))
    gws = np.stack([np.asarray(gate_a_w, f)[0], np.asarray(gate_x_w, f)[0]])
    gw_r = np.ascontiguousarray(gws.transpose(2, 0, 1, 3).reshape(128, 16 * 128))

    def chan(v):
        return np.asarray(v, f).reshape(8, 128).T

    pa = np.zeros((128, 8, 8), f)
    caw = np.asarray(conv_a_w, f)[0]
    for k in range(4):
        pa[:, :, k] = chan(caw[k])
    pa[:, :, 4] = chan(np.asarray(conv_a_b, f)[0])
    pa[:, :, 5] = chan(np.asarray(gate_a_b, f)[0])
    pa[:, :, 6] = chan(np.asarray(gate_x_b, f)[0])
    pa[:, :, 7] = chan(np.asarray(lru_param, f)[0])
    pb = np.zeros((128, 8, 35), f)
    cbw = np.asarray(conv_b_w, f)[0]
    for k in range(31):
        pb[:, :, k] = chan(cbw[k])
    pb[:, :, 31] = chan(np.asarray(conv_b_b, f)[0])
    pb[:, :, 32] = chan(np.asarray(ln_b_gain, f)[0])
    pb[:, :, 33] = chan(np.asarray(ln_b_bias, f)[0])
    ng = np.ascontiguousarray(np.asarray(norm_gain, f)[0])
    fg = np.ascontiguousarray(np.asarray(final_gain, f))
    ident = np.eye(128, dtype=f)

    in_maps = []
    for c in range(NCORES):
        q, half = c // 2, c % 2
        xl = np.zeros((NT, D), f)
        if half == 1:
            xl[0:HALO] = x_prompt[q, NP - HALO:NP]
        xl[HALO:HALO + NP] = x_prompt[q, half * NP:(half + 1) * NP]
        xl[HALO + NP:] = x_sample[c * NSEQ:(c + 1) * NSEQ].reshape(NS, D)
        in_maps.append({
            "x": xl,
            "sth": np.ascontiguousarray(sth[c * NSEQ:(c + 1) * NSEQ]),
            "stca": np.ascontiguousarray(stca[c * NSEQ:(c + 1) * NSEQ].reshape(NSEQ * 3, WA)),
            "stcb": np.ascontiguousarray(stcb[c * NSEQ:(c + 1) * NSEQ].reshape(NSEQ * 30, WA)),
            "win": win_r, "wout": wout_r, "gw": gw_r,
            "pa": pa.reshape(128, 64), "pb": pb.reshape(128, 280),
            "ng": ng, "fg": fg, "ident": ident,
            "mask": np.full((128, 1), float(half), f),
        })

    if "nc" not in _NC_CACHE:
        _NC_CACHE["nc"] = build_nc()
    nc = _NC_CACHE["nc"]
    res = run_bass_kernel_spmd(nc, in_maps, core_ids=list(range(NCORES)))
    R = res.results

    y_prompt = np.zeros((4, 2048, D), f)
    y_sample = np.zeros((128, TS, D), f)
    o_hp = np.zeros((1, 4, WA), f)
    o_cap = np.zeros((1, 4, 3, WA), f)
    o_cbp = np.zeros((1, 4, 30, WA), f)
    o_hs = np.zeros((1, 128, WA), f)
    o_cas = np.zeros((1, 128, 3, WA), f)
    o_cbs = np.zeros((1, 128, 30, WA), f)
    for c in range(NCORES):
        q, half = c // 2, c % 2
        r = R[c]
        y_prompt[q, half * NP:(half + 1) * NP] = r["y"][0:NP]
        y_sample[c * NSEQ:(c + 1) * NSEQ] = r["y"][NP:].reshape(NSEQ, TS, D)
        if half == 1:
            o_hp[0, q] = r["ohp"].reshape(WA)
            o_cap[0, q] = r["ocap"]
            o_cbp[0, q] = r["ocbp"]
        o_hs[0, c * NSEQ:(c + 1) * NSEQ] = r["ohs"]
        o_cas[0, c * NSEQ:(c + 1) * NSEQ] = r["ocas"].reshape(NSEQ, 3, WA)
        o_cbs[0, c * NSEQ:(c + 1) * NSEQ, 0:22] = r["ocbh"]
        o_cbs[0, c * NSEQ:(c + 1) * NSEQ, 22:30] = r["ocbn"].reshape(NSEQ, TS, WA)
    return (y_prompt, y_sample, o_hp, o_cap, o_cbp, o_hs, o_cas, o_cbs)
```

```python
import contextlib
import numpy as np
import concourse.bass as bass
import concourse.mybir as mybir
from concourse.bass_utils import run_bass_kernel_spmd

F32 = mybir.dt.float32
BF16 = mybir.dt.bfloat16
AF = mybir.ActivationFunctionType
ALU = mybir.AluOpType

NCORES = 8
D = 2048
WA = 1024
NH = 8
CONV_A = 4
CONV_B = 31
HALO = 32
NP = 1024
NS = 128
NSEQ = 16
TS = 8
NT = HALO + NP + NS
NTOK = NP + NS
EPS = 1e-6
NWS = 4
NLANES = 30
NL_SP = 20

HCH = [(0, 512), (512, 1024), (1024, NT)]
NCHK = [(HALO, HALO + 512), (HALO + 512, HALO + 1024), (HALO + 1024, NT)]


class Sched:
    def __init__(self):
        self.ops = []
        self.res = {}
        self.lane_last = [None] * NLANES
        self.lane_count = [0] * NLANES
        self.next_lane = {"sp": 0, "pool": NL_SP}
        self.cur_fence = []
        self.phase_keys = set()
        self.nofence_default = False

    def _deps(self, reads, writes, nofence=False):
        deps = set()
        for k in list(reads) + list(writes):
            if not nofence:
                self.phase_keys.add(k)
            if k not in self.res and not nofence:
                deps.update(self.cur_fence)
        for k in reads:
            st = self.res.get(k)
            if st is not None and st[0] is not None:
                deps.add(st[0])
        for k in writes:
            st = self.res.get(k)
            if st is not None:
                if st[0] is not None:
                    deps.add(st[0])
                deps.update(st[1])
        return deps

    def op(self, eng, fn, reads=(), writes=(), ndma=0, cc=False, after=(), nofence=None):
        reads = list(reads)
        writes = list(writes)
        if nofence is None:
            nofence = self.nofence_default
        deps = self._deps(reads, writes, nofence)
        deps.update(after)
        idx = len(self.ops)
        o = dict(eng=eng, fn=fn, deps=deps, ndma=ndma, cc=cc, lane=None, sig=False, val=None)
        if ndma:
            lane = self.next_lane[eng]
            lo, hi = (0, NL_SP) if eng == "sp" else (NL_SP, NLANES)
            self.next_lane[eng] = lo + (lane + 1 - lo) % (hi - lo)
            if self.lane_last[lane] is not None:
                deps.add(self.lane_last[lane])
            self.lane_last[lane] = idx
            self.lane_count[lane] += ndma
            o["lane"] = lane
            o["val"] = 16 * self.lane_count[lane]
        self.ops.append(o)
        for k in reads:
            st = self.res.setdefault(k, [None, []])
            st[1].append(idx)
        for k in writes:
            self.res[k] = [idx, []]
        return idx

    def fence(self):
        deps = set(self.cur_fence)
        for k in self.phase_keys:
            st = self.res.get(k)
            if st is not None:
                if st[0] is not None:
                    deps.add(st[0])
                deps.update(st[1])
        self.cur_fence = sorted(deps)
        self.phase_keys = set()

    def late_keys(self, keys):
        deps = set(self.cur_fence)
        for k in self.phase_keys:
            st = self.res.get(k)
            if st is not None:
                if st[0] is not None:
                    deps.add(st[0])
                deps.update(st[1])
        deps = sorted(deps)
        for k in keys:
            assert k not in self.res, k
            self.res[k] = [None, list(deps)]

    def finalize(self):
        for o in self.ops:
            for d in o["deps"]:
                self.ops[d]["sig"] = True
        cnt = {}
        for o in self.ops:
            if o["ndma"] or o["cc"]:
                continue
            if o["sig"]:
                cnt[o["eng"]] = cnt.get(o["eng"], 0) + 1
                o["val"] = cnt[o["eng"]]

    def emit(self, eng_name, eng, sems, lane_sems, cc_sem, final_wait=False):
        seen = {}
        for o in self.ops:
            if o["eng"] != eng_name:
                continue
            need = {}
            for d in o["deps"]:
                p = self.ops[d]
                if p["cc"]:
                    key, sem, val = "cc", cc_sem, 1
                elif p["ndma"]:
                    key, sem, val = ("l", p["lane"]), lane_sems[p["lane"]], p["val"]
                else:
                    if p["eng"] == "pe" and eng_name == "pe":
                        continue
                    key, sem, val = p["eng"], sems[p["eng"]], p["val"]
                if val > need.get(key, (None, 0))[1]:
                    need[key] = (sem, val)
            for key, (sem, val) in need.items():
                if val > seen.get(key, 0):
                    eng.wait_ge(sem, val)
                    seen[key] = val
            r = o["fn"](eng)
            if o["cc"]:
                r.then_inc(cc_sem, 1)
            elif o["ndma"]:
                rl = r if isinstance(r, (list, tuple)) else [r]
                assert len(rl) == o["ndma"], (len(rl), o["ndma"])
                for ins in rl:
                    ins.then_inc(lane_sems[o["lane"]], 16)
            elif o["sig"]:
                r.then_inc(sems[o["eng"]], 1)
        if final_wait:
            for lane in range(NLANES):
                if self.lane_count[lane]:
                    eng.wait_ge(lane_sems[lane], 16 * self.lane_count[lane])


def build_nc():
    nc = bass.Bass("TRN2", target_bir_lowering=False)
    S = Sched()

    def din(name, shape):
        return nc.dram_tensor(name, list(shape), F32, kind="ExternalInput").ap()

    def dout(name, shape):
        return nc.dram_tensor(name, list(shape), F32, kind="ExternalOutput").ap()

    x_d = din("x", [NT, D])
    sth_d = din("sth", [NSEQ, WA])
    stca_d = din("stca", [NSEQ * 3, WA])
    stcb_d = din("stcb", [NSEQ * 30, WA])
    win_d = din("win", [40, 128, D])
    wout_d = din("wout", [4, 128, 16 * 512])
    gw_d = din("gw", [128, 16 * 128])
    pa_d = din("pa", [128, 64])
    pb_d = din("pb", [128, 8 * 35])
    ng_d = din("ng", [D])
    fg_d = din("fg", [D])
    ident_d = din("ident", [128, 128])
    mask_d = din("mask", [128, 1])

    y_d = dout("y", [NTOK, D])
    ohp_d = dout("ohp", [8, 128])
    ocap_d = dout("ocap", [3, WA])
    ocbp_d = dout("ocbp", [30, WA])
    ohs_d = dout("ohs", [NSEQ, WA])
    ocas_d = dout("ocas", [NSEQ * 3, WA])
    ocbh_d = dout("ocbh", [NSEQ, 22, WA])
    ocbn_d = dout("ocbn", [NS, WA])

    a_sc = nc.dram_tensor("a_scr", [8, 128, NP], F32)
    b_sc = nc.dram_tensor("b_scr", [8, 128, NP], F32)
    wo_bf = nc.dram_tensor("wo_bf", [4, 128, 16 * 512], BF16)
    cc_in = nc.dram_tensor("cc_in", [128, 8], F32)
    cc_out = nc.dram_tensor("cc_out", [256, 8], F32)

    es = contextlib.ExitStack()
    with es:
        def sb(name, shape, dt):
            return es.enter_context(nc.sbuf_tensor(name, list(shape), dt))

        PA = sb("PA", [128, 8, 8], F32)
        PB = sb("PB", [128, 8, 35], F32)
        GW = sb("GW", [128, 16, 128], BF16)
        IDF = sb("IDF", [128, 128], F32)
        IDB = sb("IDB", [128, 128], BF16)
        ONES = sb("ONES", [128, 128], BF16)
        MASK = sb("MASK", [128, 1], F32)
        EPSC = sb("EPSC", [128, 1], F32)
        ONE1 = sb("ONE1", [1, 128], F32)
        CL = sb("CL", [128, 8], F32)
        CLT = sb("CLT", [128, 8], F32)
        SS = sb("SS", [128, 16], F32)
        LNT = sb("LNT", [128, 16], F32)
        RS = sb("RS", [128, 16], F32)
        H0S = sb("H0S", [128, 8, NSEQ], F32)
        HSL = sb("HSL", [128, 8, NSEQ], F32)
        X3 = sb("X3", [128, 8, NSEQ * 3], F32)
        HFIN = sb("HFIN", [128, 8], F32)
        HINR = sb("HINR", [128, 8], F32)
        HIN = sb("HIN", [128, 8], F32)
        HFO = sb("HFO", [128, 8], F32)
        XAP3 = sb("XAP3", [128, 8, 3], F32)
        TMP16 = sb("TMP16", [128, NSEQ], F32)
        MIX = sb("MIX", [128, 16, NTOK], BF16)
        STG = sb("STG", [128, 1024], F32)
        STG2 = sb("STG2", [128, 1024], F32)
        SSQ = sb("SSQ", [128, 9, 4], F32)
        SSQ1 = sb("SSQ1", [128, 9], F32)
        LN2 = sb("LN2", [128, 9], F32)
        RS2 = sb("RS2", [128, 9], F32)

        main_bytes = (nc.sbuf_bytes_remaining - 1024) // 64 * 64
        MAINF = sb("MAIN", [128, main_bytes // 4], F32)

        class Carver:
            def __init__(self):
                self.off = 0

            def reset(self, base=0):
                self.off = base

            def f32(self, n):
                a = MAINF[:, self.off // 4: self.off // 4 + n]
                self.off += 4 * n
                assert self.off <= main_bytes, (self.off, main_bytes)
                return a

            def bf16(self, n):
                n2 = (n + 1) // 2
                a = MAINF[:, self.off // 4: self.off // 4 + n2].bitcast(BF16)
                self.off += 4 * n2
                assert self.off <= main_bytes, (self.off, main_bytes)
                return a[:, 0:n]

        CV = Carver()
        XNT = CV.bf16(16 * NT).rearrange("p (k t) -> p k t", k=16)
        WS = CV.bf16(NWS * 16 * 128).rearrange("p (s k j) -> p s k j", s=NWS, k=16)
        UBS = CV.bf16(8 * NSEQ * 38).rearrange("p (j s r) -> p j s r", j=8, s=NSEQ)
        EARLY_B = CV.off
        XS = CV.f32(8 * NSEQ * 11).rearrange("p (h s r) -> p h s r", h=8, s=NSEQ)
        HS = CV.f32(8 * NS).rearrange("p (h t) -> p h t", h=8)
        EARLY = CV.off
        WO_END = main_bytes - 16 * 512 * 2

        PS = [es.enter_context(nc.psum_tensor(f"ps{b}", [128, 512], F32)) for b in range(8)]
        sems = {e: es.enter_context(nc.semaphore(f"s_{e}")) for e in ("pe", "act", "dve", "pool")}
        lane_sems = [es.enter_context(nc.semaphore(f"lane{i}")) for i in range(NLANES)]
        cc_sem = es.enter_context(nc.semaphore("cc_sem"))

        bank_ctr = [0]

        def next_bank():
            b = bank_ctr[0] % 8
            bank_ctr[0] += 1
            return b

        def xkeys(c0, c1):
            ks = []
            for t in range(c0 // 128, (c1 - 1) // 128 + 1):
                ks += [("xnt", t, 0), ("xnt", t, 1)]
            return ks

        def dma(eng, out, in_, reads, writes, after=(), nofence=None, **kw):
            return S.op(eng, lambda e: e.dma_start(out=out, in_=in_, **kw), reads, writes, ndma=1, after=after,
                        nofence=nofence)

        def act(out, in_, func, reads, writes, bias=None, scale=None, accum_out=None):
            kw = {}
            if bias is not None:
                kw["bias"] = bias
            if scale is not None:
                kw["scale"] = scale
            if accum_out is not None:
                kw["accum_out"] = accum_out
            return S.op("act", lambda e: e.activation(out=out, in_=in_, func=func, **kw), reads, writes)

        def tt(eng, out, in0, in1, op, reads, writes):
            return S.op(eng, lambda e: e.tensor_tensor(out=out, in0=in0, in1=in1, op=op), reads, writes)

        def stt(out, in0, scalar, in1, op0, op1, reads, writes):
            return S.op("dve", lambda e: e.scalar_tensor_tensor(out=out, in0=in0, scalar=scalar, in1=in1,
                                                               op0=op0, op1=op1), reads, writes)

        def ts(eng, out, in0, s1, s2, op0, op1, reads, writes):
            if op1 is None:
                return S.op(eng, lambda e: e.tensor_scalar(out=out, in0=in0, scalar1=s1, scalar2=None, op0=op0),
                            reads, writes)
            return S.op(eng, lambda e: e.tensor_scalar(out=out, in0=in0, scalar1=s1, scalar2=s2, op0=op0, op1=op1),
                        reads, writes)

        def copy(eng, out, in_, reads, writes):
            if eng == "act":
                return S.op("act", lambda e: e.copy(out=out, in_=in_), reads, writes)
            return S.op(eng, lambda e: e.tensor_copy(out=out, in_=in_), reads, writes)

        def scan(out, a, b, initial, reads, writes):
            return S.op("dve", lambda e: e.tensor_tensor_scan(out=out, data0=a, data1=b, initial=initial,
                                                              op0=ALU.mult, op1=ALU.add), reads, writes)

        def mm_group(out, pairs, reads, writes):
            def fn(e):
                n = len(pairs)
                ins = None
                for i, (l, r) in enumerate(pairs):
                    ins = e.matmul(out, l, r, start=(i == 0), stop=(i == n - 1))
                return ins
            return S.op("pe", fn, reads, writes)

        def transposes(items, reads, writes):
            def fn(e):
                ins = None
                for (o, i, idn) in items:
                    ins = e.transpose(out=o, in_=i, identity=idn)
                return ins
            return S.op("pe", fn, reads, writes)

        wseq = list(range(0, 16)) + [24, 16, 25, 17, 32, 33]
        for j in range(2, 8):
            wseq += [24 + j, 16 + j]
        wseq += list(range(34, 40))
        wpos = {m: i for i, m in enumerate(wseq)}
        wloaded = [0]

        w_after = {}

        def w_prefetch(upto):
            while wloaded[0] <= min(upto, len(wseq) - 1):
                i = wloaded[0]
                m = wseq[i]
                s = i % NWS
                dma("pool", WS[:, s, :, :], win_d[m].rearrange("p (k j) -> p k j", k=16), [], [("ws", s)],
                    after=w_after.get(i, ()))
                wloaded[0] += 1

        inproj_done = {}

        def inproj(m, chunks, consume, cis=None):
            i = wpos[m]
            s = i % NWS
            w_prefetch(i)
            ndone = inproj_done.get(m, 0) + (len(chunks) if cis is None else len(cis))
            inproj_done[m] = ndone
            for ci, (c0, c1) in enumerate(chunks):
                if cis is not None and ci not in cis:
                    continue
                b = next_bank()
                pairs = [(WS[:, s, kc, :], XNT[:, kc, c0:c1]) for kc in range(16)]
                mm_group(PS[b][:, 0:c1 - c0], pairs, [("ws", s)] + xkeys(c0, c1), [("ps", b)])
                consume(ci, b, c0, c1)
            if ndone >= len(chunks):
                w_prefetch(i + NWS)

        CV.reset(EARLY)
        NXT = 5
        XT = [CV.f32(D) for _ in range(NXT)]
        XB = [CV.bf16(D) for _ in range(2)]
        STA = XT[2][:, 0:1024]
        STH = XT[3][:, 0:1024]
        GBC = CV.f32(D)

        JUNK = CV.bf16(D)
        NG1 = CV.f32(D)
        XA_OFF = CV.off
        XA = [CV.f32(HALO + NP) for _ in range(2)]
        ntile = (NT + 127) // 128
        xload = {}

        def p1_load(t):
            r0 = t * 128
            nr = min(128, NT - r0)
            xload[t] = dma("sp", XT[t % NXT][:nr, :], x_d[r0:r0 + nr, :], [], [("xt", t % NXT)])

        S.op("dve", lambda e: e.memset(EPSC[:, :], EPS), [], ["EPSC"])
        S.op("dve", lambda e: e.memset(ONES[:, :], 1.0), [], ["ONES"])
        dma("sp", NG1[0:1, :], ng_d.unsqueeze(0), [], ["NG1"])
        p1_load(0)
        dma("sp", IDF[:, :], ident_d, [], ["IDF"])
        S.op("dve", lambda e: e.memset(ONE1[:, :], 1.0), [], ["ONE1"])
        for q in range(4):
            b = next_bank()
            S.op("pe", (lambda e, q=q, b=b: e.matmul(PS[b][:, :], ONE1[0:1, :], NG1[0:1, q * 512:(q + 1) * 512],
                                                    start=True, stop=True)), ["ONE1", "NG1"], [("ps", b)])
            copy("dve", GBC[:, q * 512:(q + 1) * 512], PS[b][:, :], [("ps", b)], ["GBC"])
        copy("dve", IDB[:, :], IDF[:, :], ["IDF"], ["IDB"])
        p1_load(1)
        p1_load(2)
        p1_load(3)
        p1_load(4)
        dma("sp", PA[:, :, :], pa_d.rearrange("p (h k) -> p h k", h=8), [], ["PA"])
        dma("sp", PB[:, :, :], pb_d.rearrange("p (h k) -> p h k", h=8), [], ["PB"])
        dma("sp", MASK[:, :], mask_d, [], ["MASK"])

        def p1_sq(t):
            r0 = t * 128
            nr = min(128, NT - r0)
            xs_ = t % NXT
            act(JUNK[:nr, :], XT[xs_][:nr, :], AF.Square, [("xt", xs_)], ["JUNK", ("ss", t)],
                accum_out=SS[:nr, t:t + 1])
            act(LNT[:nr, t:t + 1], SS[:nr, t:t + 1], AF.Ln, [("ss", t), "EPSC"], [("lnt", t)],
                bias=EPSC[:nr, :], scale=1.0 / D)
            act(RS[:nr, t:t + 1], LNT[:nr, t:t + 1], AF.Exp, [("lnt", t)], [("rs", t)], scale=-0.5)

        def p1_stt(t):
            r0 = t * 128
            nr = min(128, NT - r0)
            xs_, bs_ = t % NXT, t % 2
            stt(XB[bs_][:nr, :], XT[xs_][:nr, :], RS[:nr, t:t + 1], GBC[:nr, :], ALU.mult, ALU.mult,
                [("xt", xs_), ("rs", t), "GBC"], [("xb", bs_)])
            if t + NXT < ntile:
                p1_load(t + NXT)
            if t == 4:
                dma("pool", GW[:, :, :], gw_d.rearrange("p (g j) -> p g j", g=16), [], ["GW"], after=[xload[ntile - 1]])
            if t == 1:
                w_after[0] = [xload[5]]
                w_prefetch(0)
            if t in (4, 5, 6):
                w_after[t - 3] = [xload[8 if t == 4 else ntile - 1]]
                w_prefetch(t - 3)

        def p1_tr(t):
            r0 = t * 128
            nr = min(128, NT - r0)
            bs_ = t % 2
            for half in range(2):
                b = next_bank()
                psb = PS[b][:, :].bitcast(BF16)
                items = [(psb[:, s * 128: s * 128 + nr], XB[bs_][:nr, (half * 8 + s) * 128:(half * 8 + s + 1) * 128],
                          IDB[:nr, :nr]) for s in range(8)]
                transposes(items, [("xb", bs_), "IDB"], [("ps", b)])
                src = psb.rearrange("p (s c) -> p s c", s=8)[:, :, 0:nr]
                dst = XNT[:, half * 8:(half + 1) * 8, r0:r0 + nr]
                copy("act" if half == 0 else "dve", dst, src, [("ps", b)], [("xnt", t, half)])

        def a1_s0(h, cis=None):
            sl = h % 2

            def consume(ci, b, c0, c1):
                if ci < 2:
                    copy("act", XA[sl][:, c0:c1], PS[b][:, 0:c1 - c0], [("ps", b)], [("xa", sl, ci)])
                else:
                    copy("act", XA[sl][:, 1024:HALO + NP], PS[b][:, 0:HALO], [("ps", b)], [("xa", sl, 2)])
                    copy("act", XS[:, h, :, 3:11], PS[b][:, HALO:HALO + NS].rearrange("p (s t) -> p s t", s=NSEQ),
                         [("ps", b), "XShist"], [("xs", h)])
            inproj(h, HCH, consume, cis)

        early_xa = {7: (0, 0), 8: (0, 1)}
        xa_done = {h: set() for h in range(8)}
        p1_sq(0)
        p1_sq(1)
        p1_stt(0)
        for t in range(ntile):
            if t + 2 < ntile:
                p1_sq(t + 2)
            if t + 1 < ntile:
                p1_stt(t + 1)
            p1_tr(t)
            if t in early_xa:
                h_, c_ = early_xa[t]
                a1_s0(h_, cis=[c_])
                xa_done[h_].add(c_)
        act(CLT[:, :], PA[:, :, 7], AF.Exp, ["PA"], ["CLT"], scale=-1.0)
        act(CL[:, :], CLT[:, :], AF.Ln, ["CLT"], ["CL0"], bias=1.0)
        ts("dve", CL[:, :], CL[:, :], -8.0, None, ALU.mult, None, ["CL0"], ["CL"])

        a1_s0(0, cis=[2])
        xa_done[0].add(2)
        dma("sp", STH[:NSEQ, :], sth_d, [], [("xt", 3)])
        dma("sp", STA[:NSEQ * 3, :], stca_d, [], [("xt", 2)])
        b = next_bank()
        transposes([(PS[b][:, h * NSEQ:(h + 1) * NSEQ], STH[:NSEQ, h * 128:(h + 1) * 128], IDF[:NSEQ, :NSEQ])
                    for h in range(8)], [("xt", 3), "IDF"], [("ps", b)])
        copy("dve", H0S[:, :, :], PS[b][:, 0:8 * NSEQ].rearrange("p (h s) -> p h s", h=8), [("ps", b)], ["H0S"])
        b = next_bank()
        transposes([(PS[b][:, h * 48:(h + 1) * 48], STA[:48, h * 128:(h + 1) * 128], IDF[:48, :48])
                    for h in range(8)], [("xt", 2), "IDF"], [("ps", b)])
        copy("dve", XS[:, :, :, 0:3], PS[b][:, 0:8 * 48].rearrange("p (h s r) -> p h s r", h=8, s=NSEQ),
             [("ps", b)], ["XShist"])
        STCB = [NG1[:, 0:1024], NG1[:, 1024:2048]]

        def stcb_load(rt):
            sl = rt % 2
            dma("sp", STCB[sl][:120, :], stcb_d[rt * 120:(rt + 1) * 120, :], [], [("stcb", sl), "NG1"])

        def stcb_tile(rt):
            sl = rt % 2
            for half in range(2):
                b = next_bank()
                transposes([(PS[b][:, jj * 120:(jj + 1) * 120],
                             STCB[sl][:120, (half * 4 + jj) * 128:(half * 4 + jj + 1) * 128],
                             IDF[:120, :120]) for jj in range(4)], [("stcb", sl), "IDF"], [("ps", b)])
                copy("act", UBS[:, half * 4:(half + 1) * 4, rt * 4:(rt + 1) * 4, 0:30],
                     PS[b][:, 0:480].rearrange("p (j s r) -> p j s r", j=4, s=4), [("ps", b)], [("ubsh", rt, half)])

        S.fence()
        CV.reset(EARLY)
        XC = [CV.f32(NTOK) for _ in range(2)]
        XCB = [CV.bf16(NTOK) for _ in range(2)]
        RR = CV.f32(NTOK)
        II = CV.f32(NTOK)
        T1 = CV.f32(NTOK)
        AT = [CV.f32(NTOK) for _ in range(2)]
        BT = [CV.f32(NTOK) for _ in range(2)]
        HSCR = CV.f32(NP)
        assert CV.off <= XA_OFF, (CV.off, XA_OFF)

        def a1_s1(h):
            sl = h % 2
            xa_r = [("xa", sl, 0), ("xa", sl, 1), ("xa", sl, 2), "PA"]
            ts("dve", XC[sl][:, 0:NP], XA[sl][:, HALO:HALO + NP], PA[:, h, 3:4], PA[:, h, 4:5], ALU.mult, ALU.add,
               xa_r, [("xc", sl)])
            for k in range(3):
                stt(XC[sl][:, 0:NP], XA[sl][:, HALO - 3 + k:HALO - 3 + k + NP], PA[:, h, k:k + 1], XC[sl][:, 0:NP],
                    ALU.mult, ALU.add, xa_r + [("xc", sl)], [("xc", sl)])
            xcs = XC[sl][:, NP:NTOK].rearrange("p (s t) -> p s t", s=NSEQ)
            ts("dve", xcs, XS[:, h, :, 3:11], PA[:, h, 3:4], PA[:, h, 4:5], ALU.mult, ALU.add,
               [("xs", h), "XShist", "PA"], [("xcs", sl)])
            for k in range(3):
                stt(xcs, XS[:, h, :, k:k + 8], PA[:, h, k:k + 1], xcs, ALU.mult, ALU.add,
                    [("xs", h), "XShist", "PA", ("xcs", sl)], [("xcs", sl)])
            copy("dve", XCB[sl][:, :], XC[sl][:, :], [("xc", sl), ("xcs", sl)], [("xcb", sl)])
            copy("act", XAP3[:, h, :], XA[sl][:, HALO + NP - 3:HALO + NP], xa_r, [("xap3", h)])

        GCH = [(0, 512), (512, 1024), (1024, NTOK)]

        def a1_s2(h):
            sl = h % 2
            for g, dst, key, bcol in ((0, RR, "RR", 5), (1, II, "II", 6)):
                for ci, (c0, c1) in enumerate(GCH):
                    b = next_bank()
                    mm_group(PS[b][:, 0:c1 - c0], [(GW[:, g * 8 + h, :], XCB[sl][:, c0:c1])], ["GW", ("xcb", sl)],
                             [("ps", b)])
                    act(dst[:, c0:c1], PS[b][:, 0:c1 - c0], AF.Sigmoid, [("ps", b), "PA"], [(key, ci)],
                        bias=PA[:, h, bcol:bcol + 1])

        def a1_s3(h):
            sl = h % 2
            rk = [("RR", i) for i in range(3)]
            ik = [("II", i) for i in range(3)]
            atk = [("at", sl), ("ats", sl)]
            btk = [("bt", sl), ("bts", sl)]
            act(AT[sl][:, :], RR[:, :], AF.Exp, rk + ["CL"], atk, scale=CL[:, h:h + 1])
            act(T1[:, :], AT[sl][:, :], AF.Square, atk, ["T1"])
            act(T1[:, :], T1[:, :], AF.Ln, ["T1"], ["T1"], scale=-1.0, bias=1.0)
            act(T1[:, :], T1[:, :], AF.Exp, ["T1"], ["T1"], scale=0.5)
            tt("dve", BT[sl][:, :], II[:, :], XC[sl][:, :], ALU.mult, ik + [("xc", sl), ("xcs", sl)], btk)

        def a1_s4(h):
            sl = h % 2
            btk = [("bt", sl), ("bts", sl)]
            tt("dve", BT[sl][:, :], BT[sl][:, :], T1[:, :], ALU.mult, btk + ["T1"], btk)
            dma("sp", a_sc[h, :, :], AT[sl][:, 0:NP], [("at", sl)], [("asc", h)])
            dma("sp", b_sc[h, :, :], BT[sl][:, 0:NP], [("bt", sl)], [("bsc", h)])
            scan(HSCR[:, :], AT[sl][:, 0:NP], BT[sl][:, 0:NP], 0.0, [("at", sl), ("bt", sl)], ["HSCR"])
            copy("dve", HFIN[:, h:h + 1], HSCR[:, NP - 1:NP], ["HSCR"], [("hfin", h)])
            a3 = AT[sl][:, NP:NTOK].rearrange("p (s t) -> p s t", s=NSEQ)
            b3 = BT[sl][:, NP:NTOK].rearrange("p (s t) -> p s t", s=NSEQ)
            tt("dve", TMP16[:, :], a3[:, :, 0], H0S[:, h, :], ALU.mult, [("ats", sl), "H0S"], ["TMP16"])
            tt("dve", b3[:, :, 0], b3[:, :, 0], TMP16[:, :], ALU.add, [("bts", sl), "TMP16"], [("bts", sl)])
            S.op("dve", lambda e: e.memset(a3[:, :, 0], 0.0), [("ats", sl)], [("ats", sl)])
            scan(HS[:, h, :], AT[sl][:, NP:NTOK], BT[sl][:, NP:NTOK], 0.0, [("ats", sl), ("bts", sl)], [("hs", h)])

        def a2_t0(h):
            def consume(ci, b, c0, c1):
                act(MIX[:, h, c0 - HALO:c1 - HALO], PS[b][:, 0:c1 - c0], AF.Silu, [("ps", b)], [("mix", h, ci)])
            inproj(8 + h, NCHK, consume)

        for i in range(-3, 8):
            if 0 <= i < 8:
                a1_s3(i)
            if 0 <= i + 2 < 8:
                a1_s1(i + 2)
            if 0 <= i < 8:
                a1_s4(i)
            if 0 <= i + 1 < 8:
                a1_s2(i + 1)
            if 0 <= i + 3 < 8:
                a1_s0(i + 3, cis=[c for c in range(3) if c not in xa_done[i + 3]])
            elif i + 3 >= 8:
                a2_t0(i + 3 - 8)
            if i == 0:
                stcb_load(0)
                stcb_load(1)
            if 1 <= i <= 4:
                stcb_tile(i - 1)
                if i + 1 <= 3:
                    stcb_load(i + 1)

        S.fence()
        CV.reset(EARLY)
        NAB = 3
        ABL = [CV.f32(2 * NP) for _ in range(NAB)]
        HP = [CV.f32(NP) for _ in range(2)]

        def a_out_gather():
            hsk = [("hs", h) for h in range(8)]
            copy("act", HSL[:, :, :], HS[:, :, :].rearrange("p h (s t) -> p h s t", s=NSEQ)[:, :, :, TS - 1], hsk, ["HSL"])
            copy("act", X3[:, :, :].rearrange("p h (s r) -> p h s r", s=NSEQ), XS[:, :, :, 8:11],
                 [("xs", h) for h in range(8)], ["X3"])

        def a_outputs():
            S.nofence_default = True
            for half in range(2):
                b = next_bank()
                transposes([(PS[b][:NSEQ, q * 128:(q + 1) * 128], HSL[:, half * 4 + q, :], IDF[:, :]) for q in range(4)],
                           ["HSL", "IDF"], [("ps", b)])
                copy("act", STG[:NSEQ, half * 512:(half + 1) * 512], PS[b][:NSEQ, :], [("ps", b)], ["STG"])
            dma("sp", ohs_d, STG[:NSEQ, :], ["STG"], ["ohs"])
            for half in range(2):
                b = next_bank()
                transposes([(PS[b][:48, q * 128:(q + 1) * 128], X3[:, half * 4 + q, :], IDF[:, :]) for q in range(4)],
                           ["X3", "IDF"], [("ps", b)])
                copy("act", STG2[:48, half * 512:(half + 1) * 512], PS[b][:48, :], [("ps", b)], ["STG2"])
            dma("sp", ocas_d, STG2[:48, :], ["STG2"], ["ocas"])
            for half in range(2):
                b = next_bank()
                transposes([(PS[b][:3, q * 128:(q + 1) * 128], XAP3[:, half * 4 + q, :], IDF[:, :]) for q in range(4)],
                           [("xap3", h) for h in range(8)] + ["IDF"], [("ps", b)])
                copy("act", STG[:3, half * 512:(half + 1) * 512], PS[b][:3, :], [("ps", b)], ["STG"])
            dma("sp", ocap_d, STG[:3, :], ["STG"], ["ocap"])
            b = next_bank()
            transposes([(PS[b][:8, 0:128], HFO[:, :], IDF[:, :])], [("hfo", h) for h in range(8)] + ["IDF"], [("ps", b)])
            copy("act", STG2[:8, 0:128], PS[b][:8, 0:128], [("ps", b)], ["STG2"])
            dma("sp", ohp_d, STG2[:8, 0:128], ["STG2"], ["ohp"])
            S.nofence_default = False

        def a2_load(h):
            sl = h % NAB
            dma("sp", ABL[sl][:, 0:NP], a_sc[h, :, :], [("asc", h)], [("abla", sl)])
            dma("sp", ABL[sl][:, NP:2 * NP], b_sc[h, :, :], [("bsc", h)], [("ablb", sl)])

        def a2_t1(h):
            sl = h % 2
            al = h % NAB
            scan(HP[sl][:, :], ABL[al][:, 0:NP], ABL[al][:, NP:2 * NP], HIN[:, h:h + 1],
                 [("abla", al), ("ablb", al), "HIN"], [("hp", sl)])
            mk = [("mix", h, i) for i in range(3)]
            tt("dve", MIX[:, h, 0:NP], HP[sl][:, :], MIX[:, h, 0:NP], ALU.mult, [("hp", sl)] + mk, [("mixp", h)])
            tt("dve", MIX[:, h, NP:NTOK], HS[:, h, :], MIX[:, h, NP:NTOK], ALU.mult, [("hs", h)] + mk, [("mixs", h)])
            copy("dve", HFO[:, h:h + 1], HP[sl][:, NP - 1:NP], [("hp", sl)], [("hfo", h)])

        a2_load(0)
        a2_load(1)
        a2_load(2)
        a_out_gather()
        hk = [("hfin", h) for h in range(8)]
        dma("pool", cc_in[:, :], HFIN[:, :], hk, ["cc_in"])
        S.op("pool", lambda e: e.collective_compute("AllGather", ALU.bypass,
                                                    replica_groups=[[0, 1], [2, 3], [4, 5], [6, 7]],
                                                    ins=[cc_in.ap().opt()], outs=[cc_out.ap().opt()]),
             ["cc_in"], ["cc_out"], cc=True)
        dma("sp", HINR[:, :], cc_out[0:128, :], ["cc_out"], ["HINR"])
        ts("dve", HIN[:, :], HINR[:, :], MASK[:, 0:1], None, ALU.mult, None, ["HINR", "MASK"], ["HIN"])

        for h in range(8):
            if h + 3 < 8:
                a2_t0(h + 3)
            a2_t1(h)
            if h + 3 < 8:
                a2_load(h + 3)

        KD = 14
        NPT = 31 - KD
        CV.reset(EARLY_B)
        CB = CV.bf16(8 * NTOK).rearrange("p (j t) -> p j t", j=8)
        ZZ = [CV.f32(NTOK) for _ in range(2)]
        MEAN = CV.f32(NTOK)
        RSTD = CV.f32(NTOK)
        ACC = CV.f32(NTOK)
        SQ = [CV.bf16(512) for _ in range(2)]
        assert CV.off >= EARLY + 3 * 2 * NP * 4 + 2 * NP * 4, CV.off
        S.late_keys([("cb", j, ci) for j in range(8) for ci in range(3)] + [("zz", 0), ("zz", 1)] +
                    [("m2", i) for i in range(3)] + [("mean", i) for i in range(3)] +
                    [("rstd", i) for i in range(3)] + [("sq", i) for i in range(2)] + ["ACC", "ACCs"])
        SG = [CV.f32(NT) for _ in range(2)]
        UU = [CV.f32(NT) for _ in range(2)]
        UBP = [CV.bf16(HALO + NP) for _ in range(2)]
        DG = [CV.bf16(NPT * 128).rearrange("p (k c) -> p k c", k=NPT) for _ in range(2)]
        assert CV.off <= WO_END, (CV.off, WO_END)

        def b1_u0(j):
            sl = j % 2
            def dg_fn(e):
                ins = None
                for k in range(KD, 31):
                    ins = e.activation(out=DG[sl][:, k - KD, :], in_=IDB[:, :], func=AF.Identity, scale=PB[:, j, k:k + 1])
                return ins
            S.op("act", dg_fn, ["IDB", "PB"], [("dg", sl)])

            def cons_g(ci, b, c0, c1):
                act(SG[sl][:, c0:c1], PS[b][:, 0:c1 - c0], AF.Sigmoid, [("ps", b)], [("sg", sl, ci)])
            inproj(24 + j, HCH, cons_g)

            def cons_v(ci, b, c0, c1):
                copy("act", UU[sl][:, c0:c1], PS[b][:, 0:c1 - c0], [("ps", b)], [("uu", sl, ci)])
            inproj(16 + j, HCH, cons_v)
            for ci, (c0, c1) in enumerate(HCH):
                tt("dve", UU[sl][:, c0:c1], UU[sl][:, c0:c1], SG[sl][:, c0:c1], ALU.mult,
                   [("uu", sl, ci), ("sg", sl, ci)], [("uu", sl, ci)])
            uk = [("uu", sl, 0), ("uu", sl, 1), ("uu", sl, 2)]
            ceng = "dve"
            copy(ceng, UBP[sl][:, :], UU[sl][:, 0:HALO + NP], uk, [("ubp", sl)])
            copy(ceng, UBS[:, j, :, 30:38], UU[sl][:, HALO + NP:NT].rearrange("p (s t) -> p s t", s=NSEQ), uk,
                 [("ubsn", j)])

        def b2_proj(j):
            def cons(ci, b, c0, c1):
                act(MIX[:, 8 + j, c0 - HALO:c1 - HALO], PS[b][:, 0:c1 - c0], AF.Silu, [("ps", b)], [("mixb", j, ci)])
            inproj(32 + j, NCHK, cons)

        def b1_taps(j):
            sl = j % 2
            uk = [("uu", sl, 0), ("uu", sl, 1), ("uu", sl, 2)]
            ubsk = [("ubsn", j)] + [("ubsh", rt, j // 4) for rt in range(4)]
            accs = ACC[:, NP:NTOK].rearrange("p (s t) -> p s t", s=NSEQ)
            if j == 0:
                ts("dve", ACC[:, 0:NP], UU[sl][:, 2:2 + NP], PB[:, j, 0:1], None, ALU.mult, None, uk + ["PB"], ["ACC"])
                ts("dve", accs, UBS[:, j, :, 0:8], PB[:, j, 0:1], None, ALU.mult, None, ubsk + ["PB"], ["ACCs"])
            else:
                act(ACC[:, 0:NP], UU[sl][:, 2:2 + NP], AF.Identity, uk + ["PB"], ["ACC"], scale=PB[:, j, 0:1])
                act(accs, UBS[:, j, :, 0:8], AF.Identity, ubsk + ["PB"], ["ACCs"], scale=PB[:, j, 0:1])
            for k in range(1, KD):
                stt(ACC[:, 0:NP], UU[sl][:, 2 + k:2 + k + NP], PB[:, j, k:k + 1], ACC[:, 0:NP], ALU.mult, ALU.add,
                    uk + ["PB", "ACC"], ["ACC"])
                stt(accs, UBS[:, j, :, k:k + 8], PB[:, j, k:k + 1], accs, ALU.mult, ALU.add,
                    ubsk + ["PB", "ACCs"], ["ACCs"])

        def b1_u1(j):
            sl = j % 2
            uk = [("uu", sl, 0), ("uu", sl, 1), ("uu", sl, 2)]
            b = next_bank()
            transposes([(PS[b][:30, 0:128], UU[sl][:, HALO + NP - 30:HALO + NP], IDF[:, :]),
                        (PS[b][:, 128:256], UU[sl][:, HALO + NP:NT], IDF[:, :])], uk + ["IDF"], [("ps", b)])
            copy("act", STG[:30, j * 128:(j + 1) * 128], PS[b][:30, 0:128], [("ps", b)], ["STG"])
            copy("act", STG2[:, j * 128:(j + 1) * 128], PS[b][:, 128:256], [("ps", b)], ["STG2"])
            for ci, (c0, c1) in enumerate([(0, 512), (512, 1024)]):
                b = next_bank()
                pairs = [(DG[sl][:, k - KD, :], UBP[sl][:, c0 + k + 2:c0 + k + 2 + 512]) for k in range(KD, 31)]
                mm_group(PS[b][:, :], pairs, [("dg", sl), ("ubp", sl)], [("ps", b)])
                stt(CB[:, j, c0:c1], PS[b][:, :], PB[:, j, 31:32], ACC[:, c0:c1], ALU.add, ALU.add,
                    [("ps", b), "PB", "ACC"], [("cb", j, ci)])
            b = next_bank()
            pairs = [(DG[sl][:, k - KD, :], UBS[:, j, :, k:k + 8]) for k in range(KD, 31)]
            mm_group(PS[b][:, 0:NS].rearrange("p (s t) -> p s t", s=NSEQ), pairs,
                     [("dg", sl), ("ubsn", j)] + [("ubsh", rt, j // 4) for rt in range(4)], [("ps", b)])
            stt(CB[:, j, NP:NTOK], PS[b][:, 0:NS], PB[:, j, 31:32], ACC[:, NP:NTOK], ALU.add, ALU.add,
                [("ps", b), "PB", "ACCs"], [("cb", j, 2)])

        b1_u0(0)
        for j in range(8):
            b1_taps(j)
            if j + 1 < 8:
                b1_u0(j + 1)
            if j == 0:
                b2_proj(0)
                b2_proj(1)
                a_outputs()
            if j == 2:
                dma("sp", ocbh_d, stcb_d.rearrange("(s r) c -> s r c", r=30)[:, 8:30, :], [], ["ocbh"], nofence=True)
            if 2 <= j <= 5:
                n_ = j - 2
                dma("pool", wo_bf[n_, :, :].rearrange("p (k j) -> p k j", k=16),
                    wout_d[n_].rearrange("p (k j) -> p k j", k=16), [], [("wobf", n_)], nofence=True)
            if j == 7:
                b2_proj(2)
                b2_proj(3)
            b1_u1(j)
        S.nofence_default = True
        dma("sp", ocbp_d, STG[:30, :], ["STG"], ["ocbp"])
        dma("sp", ocbn_d, STG2[:, :], ["STG2"], ["ocbn"])
        S.nofence_default = False

        SCH = [(0, 512), (512, 1024), (1024, NTOK)]
        sbanks = {}
        for ci, (c0, c1) in enumerate(SCH):
            n = c1 - c0
            b1_ = next_bank()
            b2_ = next_bank()
            sbanks[ci] = (b1_, b2_)
            for j in range(8):
                sq = j % 2
                if j % 2 == 0:
                    act(SQ[sq][:, 0:n], CB[:, j, c0:c1], AF.Square, [("cb", j, ci)], [("sq", sq)])
                else:
                    tt("dve", SQ[sq][:, 0:n], CB[:, j, c0:c1], CB[:, j, c0:c1], ALU.mult, [("cb", j, ci)], [("sq", sq)])
                S.op("pe", (lambda e, j=j, b=b1_, c0=c0, c1=c1, n=n:
                            e.matmul(PS[b][:, 0:n], ONES[:, :], CB[:, j, c0:c1], start=(j == 0), stop=(j == 7))),
                     ["ONES", ("cb", j, ci)], [("ps", b1_)] if j == 0 else [("psacc", b1_, j)])
                S.op("pe", (lambda e, j=j, b=b2_, sq=sq, n=n:
                            e.matmul(PS[b][:, 0:n], ONES[:, :], SQ[sq][:, 0:n], start=(j == 0), stop=(j == 7))),
                     ["ONES", ("sq", sq)], [("ps", b2_)] if j == 0 else [("psacc", b2_, j)])
        mk = [("mean", ci) for ci in range(3)]
        rk_ = [("rstd", ci) for ci in range(3)]
        m2k = [("m2", ci) for ci in range(3)]
        for ci, (c0, c1) in enumerate(SCH):
            b1_, _b2 = sbanks[ci]
            k1 = [("ps", b1_)] + [("psacc", b1_, j) for j in range(1, 8)]
            act(MEAN[:, c0:c1], PS[b1_][:, 0:c1 - c0], AF.Identity, k1, [("mean", ci)], scale=1.0 / WA)
        tt("dve", ZZ[0][:, :], MEAN[:, :], MEAN[:, :], ALU.mult, mk, m2k)
        for ci, (c0, c1) in enumerate(SCH):
            _b1, b2_ = sbanks[ci]
            k2 = [("ps", b2_)] + [("psacc", b2_, j) for j in range(1, 8)]
            stt(RSTD[:, c0:c1], PS[b2_][:, 0:c1 - c0], 1.0 / WA, ZZ[0][:, c0:c1], ALU.mult, ALU.subtract,
                k2 + m2k, [("rstd", ci)])
        act(RSTD[:, :], RSTD[:, :], AF.Ln, rk_ + ["EPSC"], rk_, bias=EPSC[:, :])
        act(RSTD[:, :], RSTD[:, :], AF.Exp, rk_, rk_, scale=-0.5)
        stt(MEAN[:, :], MEAN[:, :], -1.0, RSTD[:, :], ALU.mult, ALU.mult, mk + rk_, mk)
        stat_k = [("rstd", i) for i in range(3)] + [("mean", i) for i in range(3)]

        def b2_rest_a(j):
            sl = j % 2
            cbk = [("cb", j, i) for i in range(3)]
            zk = [("zz", sl)] + ([("m2", i) for i in range(3)] if sl == 0 else [])
            tt("dve", ZZ[sl][:, :], CB[:, j, :], RSTD[:, :], ALU.mult, cbk + stat_k, zk)
            tt("dve", ZZ[sl][:, :], ZZ[sl][:, :], MEAN[:, :], ALU.add, zk + stat_k, zk)
            act(ZZ[sl][:, :], ZZ[sl][:, :], AF.Silu, zk + ["PB"], zk, bias=PB[:, j, 33:34], scale=PB[:, j, 32:33])

        def b2_rest_b(j):
            sl = j % 2
            zk = [("zz", sl)] + ([("m2", i) for i in range(3)] if sl == 0 else [])
            tt("dve", MIX[:, 8 + j, :], ZZ[sl][:, :], MIX[:, 8 + j, :], ALU.mult,
               zk + [("mixb", j, i) for i in range(3)], [("mixB", j)])

        def wo_view(off):
            return MAINF[:, off // 4: off // 4 + 16 * 512 // 2].bitcast(BF16).rearrange("p (k n) -> p k n", k=16)

        WO = [wo_view(WO_END), None, None]
        WSL = {0: 0, 1: 1, 2: 2, 3: 0}

        def wo_load(n):
            src = wo_bf[n, :, :].rearrange("p (k j) -> p k j", k=16)
            for hf in range(2):
                dma("pool", WO[WSL[n]][:, hf * 8:(hf + 1) * 8, :], src[:, hf * 8:(hf + 1) * 8, :], [("wobf", n)],
                    [("wo", WSL[n], hf)])

        wo_load(0)
        b2_rest_a(0)
        for j in range(8):
            if j + 1 < 8:
                b2_rest_a(j + 1)
            b2_rest_b(j)
            if j + 4 < 8:
                b2_proj(j + 4)

        S.fence()
        CV.reset(0)
        HRES = CV.f32(9 * D).rearrange("p (i d) -> p i d", i=9)
        WO[1] = CV.bf16(16 * 512).rearrange("p (k n) -> p k n", k=16)
        WO[2] = CV.bf16(16 * 512).rearrange("p (k n) -> p k n", k=16)
        NXRE = 6
        XRE = [CV.f32(512) for _ in range(NXRE)]
        SQJ = CV.bf16(512)
        FGB = CV.f32(D)
        assert CV.off <= WO_END, (CV.off, WO_END)
        wo_load(1)
        allmix = []
        for h in range(8):
            allmix += [("mixp", h), ("mixs", h)]
        for j in range(8):
            allmix += [("mixB", j)]
        xi = 0
        amix = []
        for h in range(8):
            amix += [("mixp", h), ("mixs", h)]
        bmix = [("mixB", j) for j in range(8)]

        def mm_part(out, pairs, first, last, reads, writes):
            def fn(e):
                ins = None
                n_ = len(pairs)
                for q, (l, r) in enumerate(pairs):
                    ins = e.matmul(out, l, r, start=(first and q == 0), stop=(last and q == n_ - 1))
                return ins
            return S.op("pe", fn, reads, writes)

        pend = [None]

        def fin_a(i):
            S.op("dve", lambda e: e.tensor_reduce(out=SSQ1[:, i:i + 1], in_=SSQ[:, i, :],
                                                  axis=mybir.AxisListType.X, op=ALU.add),
                 [("ssq", i, q) for q in range(4)], [("ssq1", i)])
            act(LN2[:, i:i + 1], SSQ1[:, i:i + 1], AF.Ln, [("ssq1", i), "EPSC"], [("ln2", i)],
                bias=EPSC[:, :], scale=1.0 / D)
            act(RS2[:, i:i + 1], LN2[:, i:i + 1], AF.Exp, [("ln2", i)], [("rs2", i)], scale=-0.5)

        def fin_b(i):
            hk = [("hres", i, q) for q in range(4)]
            if i < 8:
                stt(HRES[:, i, :], HRES[:, i, :], RS2[:, i:i + 1], FGB[:, :], ALU.mult, ALU.mult,
                    hk + [("rs2", i), "FGB"], hk)
                dma("pool", y_d[i * 128:(i + 1) * 128, :], HRES[:, i, :], hk, [("y", i)])
            else:
                for hf in range(2):
                    cs = slice(hf * 1024, (hf + 1) * 1024)
                    hk2 = [("hres", i, 2 * hf), ("hres", i, 2 * hf + 1)]
                    stt(HRES[:, i, cs], HRES[:, i, cs], RS2[:, i:i + 1], FGB[:, cs], ALU.mult, ALU.mult,
                        hk2 + [("rs2", i), "FGB"], hk2)
                    dma("pool", y_d[i * 128:(i + 1) * 128, cs], HRES[:, i, cs], hk2, [("y", i, hf)])

        G = 4
        ph1 = [(n, i) for n in range(2) for i in range(9)]
        ph2 = [(n, i) for i in range(9) for n in (2, 3)]
        G0 = 7
        batches = [ph1[0:G0]] + [ph1[g0:g0 + G] for g0 in range(G0, len(ph1), G)] + \
                  [ph2[g0:g0 + G] for g0 in range(0, len(ph2), G)]
        for batch in batches:
            banks = []
            for (n, i) in batch:
                b = next_bank()
                banks.append(b)
                pairs = [(MIX[:, kc, i * 128:(i + 1) * 128], WO[WSL[n]][:, kc, :]) for kc in range(8)]
                mm_part(PS[b][:, :], pairs, True, False, amix + [("wo", WSL[n], 0)], [("ps", b)])
            for (n, i), b in zip(batch, banks):
                pairs = [(MIX[:, kc, i * 128:(i + 1) * 128], WO[WSL[n]][:, kc, :]) for kc in range(8, 16)]
                mm_part(PS[b][:, :], pairs, False, True, bmix + [("wo", WSL[n], 1)], [("ps2", b)])
                xs_ = xi % NXRE
                xi += 1
                r0 = HALO + i * 128
                dma("sp", XRE[xs_][:, :], x_d[r0:r0 + 128, n * 512:(n + 1) * 512], [], [("xre", xs_)])
                tt("dve", HRES[:, i, n * 512:(n + 1) * 512], PS[b][:, :], XRE[xs_][:, :], ALU.add,
                   [("ps", b), ("ps2", b), ("xre", xs_)], [("hres", i, n)])
                act(SQJ[:, :], HRES[:, i, n * 512:(n + 1) * 512], AF.Square, [("hres", i, n)], ["SQJ", ("ssq", i, n)],
                    accum_out=SSQ[:, i, n:n + 1])
                if n == 3:
                    if pend[0] is not None:
                        fin_b(pend[0])
                    fin_a(i)
                    pend[0] = i
                if i == 4 and n == 0:
                    wo_load(2)
                if i == 0 and n == 1:
                    dma("sp", FGB[:, :], fg_d.partition_broadcast(128), [], ["FGB"])
                if i == 8 and n == 0:
                    wo_load(3)

        fin_b(pend[0])

        S.finalize()
        with nc.Block(no_gpsimd_drain=True) as block:
            @block.tensor
            def _(e):
                S.emit("pe", e, sems, lane_sems, cc_sem)

            @block.scalar
            def _(e):
                S.emit("act", e, sems, lane_sems, cc_sem)

            @block.vector
            def _(e):
                S.emit("dve", e, sems, lane_sems, cc_sem)

            @block.gpsimd
            def _(e):
                S.emit("pool", e, sems, lane_sems, cc_sem)

            @block.sync
            def _(e):
                S.emit("sp", e, sems, lane_sems, cc_sem, final_wait=True)
    return nc


_NC_CACHE = {}


def kernel(x_prompt, x_sample, state_lru_h, state_lru_conv, state_glu_conv,
           norm_gain, w_in, conv_a_w, conv_a_b, gate_a_w, gate_a_b, gate_x_w, gate_x_b,
           lru_param, conv_b_w, conv_b_b, ln_b_gain, ln_b_bias, w_out, final_gain):
    f = np.float32
    x_prompt = np.asarray(x_prompt, f)
    x_sample = np.asarray(x_sample, f)
    sth = np.asarray(state_lru_h, f)[0]
    stca = np.asarray(state_lru_conv, f)[0]
    stcb = np.asarray(state_glu_conv, f)[0]
    w_in = np.asarray(w_in, f)[0]
    w_out = np.asarray(w_out, f)[0]

    win_r = np.ascontiguousarray(w_in.reshape(16, 128, 40, 128).transpose(2, 1, 0, 3).reshape(40, 128, 2048))
    wout_r = np.ascontiguousarray(w_out.reshape(16, 128, 4, 512).transpose(2, 1, 0, 3).reshape(4, 128, 16 * 512))
    gws = np.stack([np.asarray(gate_a_w, f)[0], np.asarray(gate_x_w, f)[0]])
    gw_r = np.ascontiguousarray(gws.transpose(2, 0, 1, 3).reshape(128, 16 * 128))

    def chan(v):
        return np.asarray(v, f).reshape(8, 128).T

    pa = np.zeros((128, 8, 8), f)
    caw = np.asarray(conv_a_w, f)[0]
    for k in range(4):
        pa[:, :, k] = chan(caw[k])
    pa[:, :, 4] = chan(np.asarray(conv_a_b, f)[0])
    pa[:, :, 5] = chan(np.asarray(gate_a_b, f)[0])
    pa[:, :, 6] = chan(np.asarray(gate_x_b, f)[0])
    pa[:, :, 7] = chan(np.asarray(lru_param, f)[0])
    pb = np.zeros((128, 8, 35), f)
    cbw = np.asarray(conv_b_w, f)[0]
    for k in range(31):
        pb[:, :, k] = chan(cbw[k])
    pb[:, :, 31] = chan(np.asarray(conv_b_b, f)[0])
    pb[:, :, 32] = chan(np.asarray(ln_b_gain, f)[0])
    pb[:, :, 33] = chan(np.asarray(ln_b_bias, f)[0])
    ng = np.ascontiguousarray(np.asarray(norm_gain, f)[0])
    fg = np.ascontiguousarray(np.asarray(final_gain, f))
    ident = np.eye(128, dtype=f)

    in_maps = []
    for c in range(NCORES):
        q, half = c // 2, c % 2
        xl = np.zeros((NT, D), f)
        if half == 1:
            xl[0:HALO] = x_prompt[q, NP - HALO:NP]
        xl[HALO:HALO + NP] = x_prompt[q, half * NP:(half + 1) * NP]
        xl[HALO + NP:] = x_sample[c * NSEQ:(c + 1) * NSEQ].reshape(NS, D)
        in_maps.append({
            "x": xl,
            "sth": np.ascontiguousarray(sth[c * NSEQ:(c + 1) * NSEQ]),
            "stca": np.ascontiguousarray(stca[c * NSEQ:(c + 1) * NSEQ].reshape(NSEQ * 3, WA)),
            "stcb": np.ascontiguousarray(stcb[c * NSEQ:(c + 1) * NSEQ].reshape(NSEQ * 30, WA)),
            "win": win_r, "wout": wout_r, "gw": gw_r,
            "pa": pa.reshape(128, 64), "pb": pb.reshape(128, 280),
            "ng": ng, "fg": fg, "ident": ident,
            "mask": np.full((128, 1), float(half), f),
        })

    if "nc" not in _NC_CACHE:
        _NC_CACHE["nc"] = build_nc()
    nc = _NC_CACHE["nc"]
    res = run_bass_kernel_spmd(nc, in_maps, core_ids=list(range(NCORES)))
    R = res.results

    y_prompt = np.zeros((4, 2048, D), f)
    y_sample = np.zeros((128, TS, D), f)
    o_hp = np.zeros((1, 4, WA), f)
    o_cap = np.zeros((1, 4, 3, WA), f)
    o_cbp = np.zeros((1, 4, 30, WA), f)
    o_hs = np.zeros((1, 128, WA), f)
    o_cas = np.zeros((1, 128, 3, WA), f)
    o_cbs = np.zeros((1, 128, 30, WA), f)
    for c in range(NCORES):
        q, half = c // 2, c % 2
        r = R[c]
        y_prompt[q, half * NP:(half + 1) * NP] = r["y"][0:NP]
        y_sample[c * NSEQ:(c + 1) * NSEQ] = r["y"][NP:].reshape(NSEQ, TS, D)
        if half == 1:
            o_hp[0, q] = r["ohp"].reshape(WA)
            o_cap[0, q] = r["ocap"]
            o_cbp[0, q] = r["ocbp"]
        o_hs[0, c * NSEQ:(c + 1) * NSEQ] = r["ohs"]
        o_cas[0, c * NSEQ:(c + 1) * NSEQ] = r["ocas"].reshape(NSEQ, 3, WA)
        o_cbs[0, c * NSEQ:(c + 1) * NSEQ, 0:22] = r["ocbh"]
        o_cbs[0, c * NSEQ:(c + 1) * NSEQ, 22:30] = r["ocbn"].reshape(NSEQ, TS, WA)
    return (y_prompt, y_sample, o_hp, o_cap, o_cbp, o_hs, o_cas, o_cbs)
```
